# Optimizing a Trainium2 kernel written in Bass

```python
import math
import jax, jax.numpy as jnp
from jax import lax
import numpy as np


D_MODEL = 1024
BATCH = 4
SEQ = 8192
DEPTH = 2
DEC_BATCH = 128
DEC_SEQ = 4
PAST_LEN = 16384
PAGE_SIZE = 128

N_A_LAYERS = DEPTH // 2
N_B_LAYERS = DEPTH - N_A_LAYERS
EXPAND = 2
BRANCH_WIDTH = EXPAND * D_MODEL
A_HEADS = 4
A_KEY_WIDTH = D_MODEL // 2
A_HEAD_DK = A_KEY_WIDTH // A_HEADS
A_HEAD_DV = BRANCH_WIDTH // A_HEADS
A_GATE_RANK = 16
A_GATE_NORMALIZER = 16.0
A_CHUNK = 64
A_IN_WIDTH = 2 * A_KEY_WIDTH + 2 * BRANCH_WIDTH + A_GATE_RANK
B_HEAD_DIM = 64
B_HEADS = BRANCH_WIDTH // B_HEAD_DIM
B_KV_HEADS = 4
B_GROUP = B_HEADS // B_KV_HEADS
B_KV_WIDTH = B_KV_HEADS * B_HEAD_DIM
B_IN_WIDTH = 2 * BRANCH_WIDTH
WINDOW = 128
SWA_BLOCK = 128
ROPE_THETA = 10000.0
RMS_EPS = 1e-6

kernel_name = 'yoco_gla_swa_sink_decoder_step'


def rmsnorm(x, g):
    xf = x.astype(jnp.float32)
    xf = xf * lax.rsqrt(jnp.mean(xf * xf, axis=-1, keepdims=True) + RMS_EPS)
    return (xf * g.astype(jnp.float32)).astype(x.dtype)


def rope(x, pos):
    half = x.shape[-1] // 2
    inv_freq = ROPE_THETA ** (-jnp.arange(half, dtype=jnp.float32) / half)
    ang = pos.astype(jnp.float32)[:, None] * inv_freq[None, :]
    cos = jnp.cos(ang)[None, :, None, :]
    sin = jnp.sin(ang)[None, :, None, :]
    xf = x.astype(jnp.float32)
    x1, x2 = xf[..., :half], xf[..., half:]
    return jnp.concatenate([x1 * cos - x2 * sin, x2 * cos + x1 * sin], axis=-1).astype(x.dtype)


def gla_recurrence(q, k, v, log_a, state0):
    bsz, seq_len, n_heads, _ = q.shape
    dv = v.shape[-1]
    chunk = math.gcd(seq_len, A_CHUNK)
    n_chunks = seq_len // chunk

    def to_chunks(t):
        return t.reshape(bsz, n_chunks, chunk, n_heads, t.shape[-1]).transpose(1, 0, 3, 2, 4)

    causal = jnp.tril(jnp.ones((chunk, chunk), dtype=bool))[:, :, None]

    def step(state, blk):
        qc, kc, vc, ac = blk
        b = jnp.cumsum(ac, axis=2)
        rel = jnp.where(causal, b[:, :, :, None, :] - b[:, :, None, :, :], -jnp.inf)
        scores = jnp.einsum('bhtd,bhsd,bhtsd->bhts', qc, kc, jnp.exp(rel))
        out = (jnp.einsum('bhtd,bhdv->bhtv', qc * jnp.exp(b), state)
               + jnp.einsum('bhts,bhsv->bhtv', scores, vc))
        b_last = b[:, :, -1:, :]
        new_state = (jnp.exp(b_last[:, :, 0, :, None]) * state
                     + jnp.einsum('bhsd,bhsv->bhdv', kc * jnp.exp(b_last - b), vc))
        return new_state, out

    state, out = lax.scan(step, state0, (to_chunks(q), to_chunks(k), to_chunks(v), to_chunks(log_a)))
    return out.transpose(1, 0, 3, 2, 4).reshape(bsz, seq_len, n_heads, dv), state


def gla_mixer(h, state0, norm_g, w_in, w_gate2, b_gate, out_norm_g, w_out):
    bsz, seq_len, _ = h.shape
    u = rmsnorm(h, norm_g)
    proj = u @ w_in
    q, k, v, gate, g_low = jnp.split(
        proj, [A_KEY_WIDTH, 2 * A_KEY_WIDTH, 2 * A_KEY_WIDTH + BRANCH_WIDTH,
               2 * A_KEY_WIDTH + 2 * BRANCH_WIDTH], axis=-1)
    log_a = jax.nn.log_sigmoid((g_low @ w_gate2 + b_gate).astype(jnp.float32)) / A_GATE_NORMALIZER

    def heads(t, d):
        return t.reshape(bsz, seq_len, A_HEADS, d).astype(jnp.float32)

    o, state = gla_recurrence(heads(q, A_HEAD_DK) * (A_HEAD_DK ** -0.5), heads(k, A_HEAD_DK),
                              heads(v, A_HEAD_DV), heads(log_a, A_HEAD_DK),
                              state0.astype(jnp.float32))
    o = rmsnorm(o, out_norm_g).reshape(bsz, seq_len, BRANCH_WIDTH).astype(h.dtype)
    return (o * jax.nn.silu(gate)) @ w_out, state


def shared_kv(h, kv_norm, w_k, w_v, k_norm, pos):
    bsz, seq_len, _ = h.shape
    u = rmsnorm(h, kv_norm)
    k = rope(rmsnorm((u @ w_k).reshape(bsz, seq_len, B_KV_HEADS, B_HEAD_DIM), k_norm), pos)
    v = (u @ w_v).reshape(bsz, seq_len, B_KV_HEADS, B_HEAD_DIM)
    return k, v


def swa_query(h, norm_g, w_in, q_norm_g, pos):
    bsz, seq_len, _ = h.shape
    q, gate = jnp.split(rmsnorm(h, norm_g) @ w_in, [BRANCH_WIDTH], axis=-1)
    q = rope(rmsnorm(q.reshape(bsz, seq_len, B_HEADS, B_HEAD_DIM), q_norm_g), pos)
    return q, gate


def attend(q, k, v, qpos, kpos, sinks):
    bsz, lq = q.shape[:2]
    qg = q.reshape(bsz, lq, B_KV_HEADS, B_GROUP, B_HEAD_DIM)
    s = jnp.einsum('bqgrd,bkgd->bgrqk', qg, k).astype(jnp.float32) * (B_HEAD_DIM ** -0.5)
    dpos = qpos[:, None] - kpos[None, :]
    mask = (dpos >= 0) & (dpos <= WINDOW) & (kpos[None, :] >= 0)
    s = jnp.where(mask, s, -jnp.inf)
    sink = sinks.astype(jnp.float32).reshape(B_KV_HEADS, B_GROUP)[None, :, :, None, None]
    m = jnp.maximum(jnp.max(s, axis=-1, keepdims=True), sink)
    p = jnp.exp(s - m)
    p = p / (jnp.sum(p, axis=-1, keepdims=True) + jnp.exp(sink - m))
    o = jnp.einsum('bgrqk,bkgd->bqgrd', p.astype(v.dtype), v)
    return o.reshape(bsz, lq, B_HEADS, B_HEAD_DIM)


def swa_banded(q, k, v, sinks):
    bsz, seq_len = q.shape[:2]
    n_blocks = seq_len // SWA_BLOCK
    span = SWA_BLOCK + WINDOW
    pad = ((0, 0), (WINDOW, 0), (0, 0), (0, 0))
    kpad, vpad = jnp.pad(k, pad), jnp.pad(v, pad)

    def block(i):
        start = i * SWA_BLOCK
        qb = lax.dynamic_slice_in_dim(q, start, SWA_BLOCK, axis=1)
        kb = lax.dynamic_slice_in_dim(kpad, start, span, axis=1)
        vb = lax.dynamic_slice_in_dim(vpad, start, span, axis=1)
        qpos = start + jnp.arange(SWA_BLOCK, dtype=jnp.int32)
        kpos = start - WINDOW + jnp.arange(span, dtype=jnp.int32)
        return attend(qb, kb, vb, qpos, kpos, sinks)

    o = lax.map(block, jnp.arange(n_blocks, dtype=jnp.int32))
    return o.transpose(1, 0, 2, 3, 4).reshape(bsz, seq_len, B_HEADS, B_HEAD_DIM)


def swa_out(o, gate, w_out):
    bsz, seq_len = o.shape[:2]
    return (o.reshape(bsz, seq_len, BRANCH_WIDTH) * jax.nn.silu(gate)) @ w_out


def setup_inputs(seed: int = 0) -> dict:
    key = jax.random.key(seed)
    ks = jax.random.split(key, 24)

    def nrm(k, shape, scale):
        return jax.random.normal(k, shape, jnp.float32) * scale

    win_buf = min(WINDOW, PAST_LEN)
    return {
        'x_prompt': nrm(ks[0], (BATCH, SEQ, D_MODEL), 1.0),
        'x_sample': nrm(ks[1], (DEC_BATCH, DEC_SEQ, D_MODEL), 1.0),
        'state_gla': nrm(ks[2], (N_A_LAYERS, DEC_BATCH, A_HEADS, A_HEAD_DK, A_HEAD_DV), 0.5),
        'cache_swa_k': nrm(ks[3], (DEC_BATCH, win_buf, B_KV_HEADS, B_HEAD_DIM), 1.0),
        'cache_swa_v': nrm(ks[4], (DEC_BATCH, win_buf, B_KV_HEADS, B_HEAD_DIM), 1.0),
        'a_norm': 1.0 + nrm(ks[5], (N_A_LAYERS, D_MODEL), 0.02),
        'a_w_in': nrm(ks[6], (N_A_LAYERS, D_MODEL, A_IN_WIDTH), D_MODEL ** -0.5),
        'a_w_gate2': nrm(ks[7], (N_A_LAYERS, A_GATE_RANK, A_KEY_WIDTH), A_GATE_RANK ** -0.5),
        'a_b_gate': nrm(ks[8], (N_A_LAYERS, A_KEY_WIDTH), 0.1),
        'a_out_norm': 1.0 + nrm(ks[9], (N_A_LAYERS, A_HEAD_DV), 0.02),
        'a_w_out': nrm(ks[10], (N_A_LAYERS, BRANCH_WIDTH, D_MODEL), BRANCH_WIDTH ** -0.5),
        'kv_norm': 1.0 + nrm(ks[11], (D_MODEL,), 0.02),
        'w_k': nrm(ks[12], (D_MODEL, B_KV_WIDTH), D_MODEL ** -0.5),
        'w_v': nrm(ks[13], (D_MODEL, B_KV_WIDTH), D_MODEL ** -0.5),
        'k_norm': 1.0 + nrm(ks[14], (B_HEAD_DIM,), 0.02),
        'b_norm': 1.0 + nrm(ks[15], (N_B_LAYERS, D_MODEL), 0.02),
        'b_w_in': nrm(ks[16], (N_B_LAYERS, D_MODEL, B_IN_WIDTH), D_MODEL ** -0.5),
        'b_q_norm': 1.0 + nrm(ks[17], (N_B_LAYERS, B_HEAD_DIM), 0.02),
        'b_sinks': nrm(ks[18], (N_B_LAYERS, B_HEADS), 0.5),
        'b_w_out': nrm(ks[19], (N_B_LAYERS, BRANCH_WIDTH, D_MODEL), BRANCH_WIDTH ** -0.5),
    }


def reference(x_prompt, x_sample, state_gla, cache_swa_k, cache_swa_v,
              a_norm, a_w_in, a_w_gate2, a_b_gate, a_out_norm, a_w_out,
              kv_norm, w_k, w_v, k_norm,
              b_norm, b_w_in, b_q_norm, b_sinks, b_w_out):
    bsz_p, seq_p = x_prompt.shape[:2]
    seq_s = x_sample.shape[1]
    win_buf = cache_swa_k.shape[1]
    pos_p = jnp.arange(seq_p, dtype=jnp.int32)
    pos_s = PAST_LEN + jnp.arange(seq_s, dtype=jnp.int32)
    kpos_s = PAST_LEN - win_buf + jnp.arange(win_buf + seq_s, dtype=jnp.int32)

    hp, hs = x_prompt, x_sample
    gla_states_p, gla_states_s = [], []
    for layer in range(DEPTH):
        if layer < N_A_LAYERS:
            i = layer
            w = (a_norm[i], a_w_in[i], a_w_gate2[i], a_b_gate[i], a_out_norm[i], a_w_out[i])
            zero_state = jnp.zeros((bsz_p, A_HEADS, A_HEAD_DK, A_HEAD_DV), jnp.float32)
            dp, st_p = gla_mixer(hp, zero_state, *w)
            ds, st_s = gla_mixer(hs, state_gla[i], *w)
            hp, hs = hp + dp, hs + ds
            gla_states_p.append(st_p.astype(x_prompt.dtype))
            gla_states_s.append(st_s.astype(state_gla.dtype))
            if layer == N_A_LAYERS - 1:
                k_p, v_p = shared_kv(hp, kv_norm, w_k, w_v, k_norm, pos_p)
                k_new, v_new = shared_kv(hs, kv_norm, w_k, w_v, k_norm, pos_s)
                k_s = jnp.concatenate([cache_swa_k.astype(k_new.dtype), k_new], axis=1)
                v_s = jnp.concatenate([cache_swa_v.astype(v_new.dtype), v_new], axis=1)
        else:
            j = layer - N_A_LAYERS
            q_p, g_p = swa_query(hp, b_norm[j], b_w_in[j], b_q_norm[j], pos_p)
            hp = hp + swa_out(swa_banded(q_p, k_p, v_p, b_sinks[j]), g_p, b_w_out[j])
            q_s, g_s = swa_query(hs, b_norm[j], b_w_in[j], b_q_norm[j], pos_s)
            hs = hs + swa_out(attend(q_s, k_s, v_s, pos_s, kpos_s, b_sinks[j]), g_s, b_w_out[j])

    prompt_keep = min(WINDOW, seq_p)
    new_state_gla_prompt = jnp.stack(gla_states_p, axis=0)
    new_state_gla_sample = jnp.stack(gla_states_s, axis=0)
    new_cache_swa_k_prompt = k_p[:, seq_p - prompt_keep:]
    new_cache_swa_v_prompt = v_p[:, seq_p - prompt_keep:]
    new_cache_swa_k_sample = k_s[:, seq_s:]
    new_cache_swa_v_sample = v_s[:, seq_s:]
    return (hp, hs, new_state_gla_prompt, new_state_gla_sample,
            new_cache_swa_k_prompt, new_cache_swa_v_prompt,
            new_cache_swa_k_sample, new_cache_swa_v_sample)
```

```python
import contextlib
import numpy as np
import ml_dtypes
import concourse.bass as bass
import concourse.mybir as mybir
from concourse.bass_utils import run_bass_kernel_spmd

F32 = mybir.dt.float32
BF16 = mybir.dt.bfloat16
AF = mybir.ActivationFunctionType
ALU = mybir.AluOpType
AX = mybir.AxisListType

D = 1024
NCORES = 8
ENGS = ("pe", "act", "dve", "pool", "sp")
PIPELINE = True
SAME_ENGINE_INORDER = ()


class Buf:
    __slots__ = ("name", "last_w", "readers")

    def __init__(self, name):
        self.name = name
        self.last_w = None
        self.readers = []


class Ins:
    __slots__ = ("eng", "fn", "seq", "deps", "flag", "ms", "dma_slot", "dma_cnt", "epoch")

    def __init__(self, eng, fn, seq, epoch):
        self.eng = eng
        self.fn = fn
        self.seq = seq
        self.deps = []
        self.flag = False
        self.ms = None
        self.dma_slot = None
        self.dma_cnt = 0
        self.epoch = epoch


class DmaSlot:
    def __init__(self, name):
        self.name = name
        self.count = 0
        self.sem = None
        self.last = None


class Prog:
    def __init__(self, nc):
        self.nc = nc
        self.streams = {e: [] for e in ENGS}
        self.slots = []
        self.epoch = 0
        self.waited = {}

    def new_epoch(self):
        self.epoch += 1

    def slot(self, name):
        s = DmaSlot(name)
        self.slots.append(s)
        return s

    def _add_dep(self, ins, dep):
        if dep is None or dep is ins:
            return
        if dep.dma_slot is not None:
            key = (ins.eng, "dma", id(dep.dma_slot))
            val = dep.dma_cnt
        else:
            if dep.eng == ins.eng and (dep.eng == "pe" or dep.eng in SAME_ENGINE_INORDER):
                return
            key = (ins.eng, "eng", dep.eng)
            val = dep.seq
        if self.waited.get(key, -1) >= val:
            return
        self.waited[key] = val
        ins.deps.append(dep)
        if dep.dma_slot is None:
            dep.flag = True

    def emit(self, eng, fn, reads=(), writes=(), dma=None):
        st = self.streams[eng]
        ins = Ins(eng, fn, len(st), self.epoch)
        if dma is not None:
            dma.count += 1
            ins.dma_slot = dma
            ins.dma_cnt = dma.count
            dma.last = ins
        for b in reads:
            self._add_dep(ins, b.last_w)
        for b in writes:
            self._add_dep(ins, b.last_w)
            for r in b.readers:
                self._add_dep(ins, r)
        for b in reads:
            b.readers.append(ins)
        for b in writes:
            b.last_w = ins
            b.readers = []
        st.append(ins)
        return ins

    def barrier(self):
        lasts = []
        for e in ENGS:
            for ins in reversed(self.streams[e]):
                if ins.fn is not None:
                    if ins.dma_slot is None:
                        lasts.append(ins)
                    break
        dmas = [s.last for s in self.slots if s.last is not None]
        for e in ENGS:
            ins = self.emit(e, None)
            for d in lasts + dmas:
                if d.eng != e or d.dma_slot is not None:
                    self._add_dep(ins, d)

    def build(self, final_waits=()):
        nc = self.nc
        fin = self.emit("sp", None)
        for d in final_waits:
            if d.dma_slot is None or d.dma_slot.last is d:
                self._add_dep(fin, d)
        n_epochs = self.epoch + 1
        for e in ENGS:
            cnt = {}
            for ins in self.streams[e]:
                if ins.flag:
                    cnt[ins.epoch] = cnt.get(ins.epoch, 0) + 1
                    ins.ms = cnt[ins.epoch]
        with contextlib.ExitStack() as es:
            esem = {}
            for e in ENGS:
                for ep in range(n_epochs):
                    if any(i.flag and i.epoch == ep for i in self.streams[e]):
                        esem[(e, ep)] = es.enter_context(nc.semaphore(f"p_{e}_{ep}"))
            for s in self.slots:
                if s.count > 0:
                    s.sem = es.enter_context(nc.semaphore(f"d_{s.name}"))
            block = es.enter_context(nc.Block())

            def replay(e, eng):
                for ins in self.streams[e]:
                    for d in ins.deps:
                        if d.dma_slot is not None:
                            eng.wait_ge(d.dma_slot.sem, 16 * d.dma_cnt)
                        else:
                            eng.wait_ge(esem[(d.eng, d.epoch)], d.ms)
                    if ins.fn is None:
                        continue
                    bi = ins.fn(eng)
                    if ins.dma_slot is not None:
                        bi.then_inc(ins.dma_slot.sem, 16)
                    elif ins.flag:
                        bi.then_inc(esem[(e, ins.epoch)], 1)

            @block.tensor
            def _(eng):
                replay("pe", eng)

            @block.scalar
            def _(eng):
                replay("act", eng)

            @block.vector
            def _(eng):
                replay("dve", eng)

            @block.gpsimd
            def _(eng):
                replay("pool", eng)

            @block.sync
            def _(eng):
                replay("sp", eng)
        return nc


class T:
    __slots__ = ("ap", "b")

    def __init__(self, ap, b):
        self.ap = ap
        self.b = b

    def __getitem__(self, k):
        return self.ap[k]


ARENA_WORDS = 53200


class Ctx:
    def __init__(self, nc):
        self.nc = nc
        self.P = Prog(nc)
        self.arena = nc.alloc_sbuf_tensor("arena", [128, ARENA_WORDS], F32)
        self.off = 0
        self.psum = nc.alloc_psum_tensor("ps", [128, 8, 512], F32)
        self.pb = [Buf(f"bank{i}") for i in range(8)]
        self.rr = 0
        self.hold = set()

    def sb(self, name, free, dtype):
        n = int(np.prod(free))
        words = n if dtype == F32 else (n + 1) // 2
        words = (words + 7) // 8 * 8
        assert self.off + words <= ARENA_WORDS, f"arena overflow at {name}: {self.off + words}"
        ap = self.arena[:, self.off:self.off + words]
        self.off += words
        if dtype != F32:
            ap = ap.bitcast(dtype)
        ap = ap[:, 0:n]
        if len(free) == 2:
            ap = ap.rearrange("p (a b) -> p a b", a=free[0])
        elif len(free) == 3:
            ap = ap.rearrange("p (a b c) -> p a b c", a=free[0], b=free[1])
        elif len(free) == 4:
            ap = ap.rearrange("p (a b c d) -> p a b c d", a=free[0], b=free[1], c=free[2])
        return T(ap, Buf(name))

    def bank(self):
        for _ in range(6):
            i = 2 + self.rr % 6
            self.rr += 1
            if i not in self.hold:
                self.hold.add(i)
                return i
        raise RuntimeError("all PSUM banks held")

    def free(self, i):
        self.hold.discard(i)

    def bk(self, i):
        return self.psum[:, i, :]

    def bkbf(self, i):
        return self.psum[:, i, :].bitcast(BF16)

    def mm(self, out, lhsT, rhs, start, stop, R, W):
        self.P.emit("pe", lambda e: e.matmul(out, lhsT=lhsT, rhs=rhs, start=start, stop=stop), R, W)

    def tr(self, out, in_, ident, R, W):
        self.P.emit("pe", lambda e: e.transpose(out=out, in_=in_, identity=ident), R, W)

    def act(self, out, in_, func, R, W, scale=1.0, bias=0.0, accum=None):
        if accum is None:
            self.P.emit("act", lambda e: e.activation(out=out, in_=in_, func=func, scale=scale, bias=bias), R, W)
        else:
            self.P.emit("act", lambda e: e.activation(out=out, in_=in_, func=func, scale=scale, bias=bias, accum_out=accum), R, W)

    def ts(self, eng, out, in0, s1, op0, R, W, s2=None, op1=None):
        if op1 is None:
            self.P.emit(eng, lambda e: e.tensor_scalar(out=out, in0=in0, scalar1=s1, scalar2=None, op0=op0), R, W)
        else:
            self.P.emit(eng, lambda e: e.tensor_scalar(out=out, in0=in0, scalar1=s1, scalar2=s2, op0=op0, op1=op1), R, W)

    def tt(self, eng, out, in0, in1, op, R, W):
        self.P.emit(eng, lambda e: e.tensor_tensor(out=out, in0=in0, in1=in1, op=op), R, W)

    def stt(self, out, in0, scalar, in1, op0, op1, R, W):
        self.P.emit("dve", lambda e: e.scalar_tensor_tensor(out=out, in0=in0, scalar=scalar, in1=in1, op0=op0, op1=op1), R, W)

    def cp(self, eng, out, in_, R, W):
        if eng == "act":
            self.P.emit("act", lambda e: e.copy(out=out, in_=in_), R, W)
        else:
            self.P.emit(eng, lambda e: e.tensor_copy(out=out, in_=in_), R, W)

    def dma(self, out, in_, R, W, slot, eng="sp"):
        return self.P.emit(eng, lambda e: e.dma_start(out=out, in_=in_), R, W, dma=slot)

    def memset(self, eng, ap, val, W):
        self.P.emit(eng, lambda e: e.memset(ap, val), (), W)


def bc(ap, shape):
    return ap.to_broadcast(shape)


def load_weight_gen(c, dram, dst, nk, c_lo, c_hi, gain, gain_mod, stg, stg_slots, cnt):
    CH = 1024
    ns = len(stg)
    for k in range(nk):
        for c0 in range(c_lo, c_hi, CH):
            n = min(CH, c_hi - c0)
            i = cnt[0] % ns
            cnt[0] += 1
            c.dma(stg[i][:, 0:n], dram[k * 128:(k + 1) * 128, c0:c0 + n], [], [stg[i].b], stg_slots[i])
            eng = "act" if (cnt[0] % 2 == 0) else "dve"
            o = dst[:, k, c0:c0 + n]
            if gain is None:
                c.cp(eng, o, stg[i][:, 0:n], [stg[i].b], [dst.b])
            else:
                g = gain[:, (k % gain_mod):(k % gain_mod) + 1]
                if eng == "act":
                    c.act(o, stg[i][:, 0:n], AF.Copy, [stg[i].b, gain.b], [dst.b], scale=g)
                else:
                    c.ts("dve", o, stg[i][:, 0:n], g, ALU.mult, [stg[i].b, gain.b], [dst.b])
            yield


def load_weight(c, dram, dst, nk, ncols, gain, gain_mod, stg, stg_slots, cnt):
    for _ in load_weight_gen(c, dram, dst, nk, 0, ncols, gain, gain_mod, stg, stg_slots, cnt):
        pass


def rms_stats(c, xt, junk, ss, rr, xn, dcols=1024):
    c.act(junk[:, 0:dcols], xt[:, :], AF.Square, [xt.b], [junk.b, ss.b], accum=ss[:, 0:1])
    c.act(rr[:, 0:1], ss[:, 0:1], AF.Ln, [ss.b], [rr.b], scale=1.0 / dcols, bias=1e-6)
    c.act(rr[:, 0:1], rr[:, 0:1], AF.Exp, [rr.b], [rr.b], scale=-0.5)
    c.ts("dve", xn[:, :], xt[:, :], rr[:, 0:1], ALU.mult, [xt.b, rr.b], [xn.b])


def rms_tr(c, xn, uT, ident, tbank):
    pt = c.bkbf(tbank).rearrange("p (k t) -> p k t", k=8)
    for k in range(8):
        c.tr(pt[:, k, :], xn[:, k * 128:(k + 1) * 128], ident[:, :], [xn.b, ident.b], [c.pb[tbank]])
    c.cp("dve", uT[:, :, :], pt, [c.pb[tbank]], [uT.b])


def rms_prep(c, xt, junk, ss, rr, xn, uT, ident, tbank, dcols=1024):
    rms_stats(c, xt, junk, ss, rr, xn, dcols)
    rms_tr(c, xn, uT, ident, tbank)


def proj(c, uT, W, c0, n, bank):
    for k in range(8):
        c.mm(c.bk(bank)[:, 0:n], uT[:, k, :], W[:, k, c0:c0 + n], k == 0, k == 7, [uT.b, W.b], [c.pb[bank]])


def sigmoid_act(c, sig, gbank):
    c.act(sig[:, :], c.bk(gbank), AF.Exp, [c.pb[gbank]], [sig.b], scale=-1.0)
    c.act(sig[:, :], sig[:, :], AF.Ln, [sig.b], [sig.b], bias=1.0)
    c.act(sig[:, :], sig[:, :], AF.Exp, [sig.b], [sig.b], scale=-1.0)


def out_proj(c, og, ogT, Wout, xt, ho, ident):
    pt = c.psum[:, 0:2, :].bitcast(BF16).rearrange("p b (k t) -> p (b k) t", k=8)
    for k in range(16):
        c.tr(pt[:, k, :], og[:, k * 128:(k + 1) * 128], ident[:, :], [og.b, ident.b], [c.pb[0], c.pb[1]])
    c.cp("dve", ogT[:, 0:8, :], pt[:, 0:8, :], [c.pb[0]], [ogT.b])
    c.cp("act", ogT[:, 8:16, :], pt[:, 8:16, :], [c.pb[1]], [ogT.b])
    yield
    for half in range(2):
        bk = c.bank()
        for k in range(16):
            c.mm(c.bk(bk), ogT[:, k, :], Wout[:, k, half * 512:(half + 1) * 512], k == 0, k == 15,
                 [ogT.b, Wout.b], [c.pb[bk]])
            if k == 7:
                yield
        c.tt("dve", ho[:, half * 512:(half + 1) * 512], c.bk(bk), xt[:, half * 512:(half + 1) * 512], ALU.add,
             [c.pb[bk], xt.b], [ho.b])
        c.free(bk)
        yield


class NS:
    pass


NSEQ = 16
TS = 4


def build_program(NB):
    NPRE = NB - 1
    NMAIN = NB + 1
    nc = bass.Bass("TRN2", target_bir_lowering=False)

    def din(name, shape, dt=F32):
        return nc.dram_tensor(name, list(shape), dt, kind="ExternalInput").ap()

    def dout(name, shape, dt=F32):
        return nc.dram_tensor(name, list(shape), dt, kind="ExternalOutput").ap()

    x_pre = din("x_pre", [max(NPRE, 1) * 128, D])
    x_main = din("x_main", [NMAIN * 128, D])
    a_w_in = din("a_w_in", [D, 5136])
    a_w_out = din("a_w_out", [2048, D])
    wkv = din("wkv", [D, 512])
    b_w_in = din("b_w_in", [D, 4096])
    b_w_out = din("b_w_out", [2048, D])
    wg2a_d = din("wg2a", [17, 512])
    g_a = din("g_a", [128, 8])
    g_ao = din("g_ao", [128, 4])
    g_kv = din("g_kv", [128, 8])
    g_b = din("g_b", [128, 8])
    gk_d = din("gk", [128, 64])
    gq_d = din("gq", [128, 64])
    sinks_d = din("sinks", [128, 32])
    ident_d = din("ident", [128, 128], BF16)
    U_d = din("U", [128, 128], BF16)
    L_d = din("L", [128, 128], BF16)
    LU_d = din("LU", [128, 256], BF16)
    U32_d = din("U32", [128, 128])
    U432_d = din("U432", [128, 128])
    ones32_d = din("ones32", [128, 8])
    U4_d = din("U4", [128, 128], BF16)
    ones_d = din("ones", [128, 128], BF16)
    seqind_d = din("seqind", [128, NSEQ], BF16)
    seqindf_d = din("seqindf", [128, NSEQ])
    CM_d = din("CM", [128, NSEQ * 128], BF16)
    CMK_d = din("CMK", [128, TS], BF16)
    cos_d = din("cos", [128, NMAIN, 32])
    sin_d = din("sin", [128, NMAIN, 32])
    coss_d = din("cos_s", [128, 1, 32])
    sins_d = din("sin_s", [128, 1, 32])
    valid_d = din("valid", [128, 1])
    x_s = din("x_s", [128, D])
    state_s = din("state_s", [NSEQ, 4, 128, 512])
    cache_k_s = din("cache_k_s", [NSEQ, 128, 256])
    cache_v_s = din("cache_v_s", [NSEQ, 128, 256])

    y_main = dout("y_main", [NB * 128, D])
    st_p = dout("st_p", [4, 128, 512])
    ck_p = dout("ck_p", [128, 256])
    cv_p = dout("cv_p", [128, 256])
    y_s = dout("y_s", [NSEQ * TS, D])
    st_s = dout("st_s", [NSEQ, 4, 128, 512])
    ck_s = dout("ck_s", [NSEQ, 128, 256])
    cv_s = dout("cv_s", [NSEQ, 128, 256])
    h1s = nc.dram_tensor("h1s", [NMAIN * 128, D], F32, kind="Internal").ap()

    c = Ctx(nc)
    P = c.P
    finals = []

    for q4 in range(4):
        sl = slice(q4 * 4, (q4 + 1) * 4)
        finals.append(c.dma(ck_s[sl, 0:128 - TS, :], cache_k_s[sl, TS:128, :], [], [], P.slot(f"shk{q4}")))
        finals.append(c.dma(cv_s[sl, 0:128 - TS, :], cache_v_s[sl, TS:128, :], [], [], P.slot(f"shv{q4}")))

    ident = c.sb("ident", [128], BF16)
    U = c.sb("U", [128], BF16)
    L = c.sb("L", [128], BF16)
    LU = c.sb("LU", [2, 128], BF16)
    U32 = c.sb("U32", [128], F32)
    U432 = c.sb("U432", [128], F32)
    ones32 = c.sb("ones32", [8], F32)
    U4 = c.sb("U4", [128], BF16)
    ones = c.sb("ones", [128], BF16)
    seqind = c.sb("seqind", [NSEQ], BF16)
    seqindf = c.sb("seqindf", [NSEQ], F32)
    CMK = c.sb("CMK", [TS], BF16)
    hs = c.sb("hs", [1024], F32)
    for t, d in ((ident, ident_d), (U, U_d), (L, L_d), (U4, U4_d), (ones, ones_d), (seqind, seqind_d),
                 (seqindf, seqindf_d), (CMK, CMK_d), (U32, U32_d), (U432, U432_d), (ones32, ones32_d)):
        c.dma(t[:, :], d, [], [t.b], P.slot("c_" + t.b.name))
    c.dma(LU[:, :, :], LU_d.rearrange("p (a b) -> p a b", a=2), [], [LU.b], P.slot("c_LU"))
    persist_mark = c.off


    def alloc_common(t, tag, nxt=3, nho=2, nsig=2):
        t.xt = [c.sb(f"xt{tag}{i}", [1024], F32) for i in range(nxt)]
        t.xt_slots = [P.slot(f"xt{tag}{i}") for i in range(nxt)]
        t.junk = c.sb("junk" + tag, [1024], BF16)
        t.ss1 = c.sb("ss1" + tag, [1], F32)
        t.rr1 = c.sb("rr1" + tag, [1], F32)
        t.xn = c.sb("xn" + tag, [1024], BF16)
        t.sig = [c.sb(f"sig{tag}{i}", [512], F32) for i in range(nsig)]
        t.og = c.sb("og" + tag, [2048], BF16)
        t.ogT = c.sb("ogT" + tag, [16, 128], BF16)
        t.ho = [c.sb(f"ho{tag}{i}", [1024], F32) for i in range(nho)]
        t.ho_slots = [P.slot(f"ho{tag}{i}") for i in range(nho)]

    def drive(gens, prefetch=None, prerms=None, rms_round=4, ratios=None, extra=None, extra_deadline=0, extra_from=1):
        if not PIPELINE:
            if extra is not None:
                for _ in extra:
                    pass
            for k, g in enumerate(gens):
                if prefetch is not None:
                    prefetch(k)
                    prerms(k)
                for _ in g:
                    pass
            return
        prev = None
        if prefetch is not None:
            prefetch(0)
            prerms(0)
        for k, g in enumerate(gens):
            nxt = prefetch is not None and k + 1 < len(gens)
            if nxt:
                prefetch(k + 1)
            fdone = False
            bdone = prev is None
            rounds = 0
            if extra is not None and k >= extra_deadline:
                for _ in extra:
                    pass
                extra = None
            while not (fdone and bdone):
                rounds += 1
                if extra is not None and k >= extra_from:
                    try:
                        next(extra)
                        next(extra)
                    except StopIteration:
                        extra = None
                if rounds == rms_round and nxt:
                    prerms(k + 1)
                    nxt = False
                if not bdone:
                    try:
                        for _ in range(ratios[k] if ratios is not None else 2):
                            next(prev)
                    except StopIteration:
                        bdone = True
                if not fdone:
                    try:
                        if next(g) == "BACK":
                            fdone = True
                    except StopIteration:
                        fdone = True
            if nxt:
                prerms(k + 1)
            prev = g
        if prev is not None:
            for _ in prev:
                pass

    Wa_in = c.sb("Wa_in", [8, 5136], BF16)
    Wa_out = c.sb("Wa_out", [16, 1024], BF16)
    wg2a = c.sb("wg2a", [512], BF16)
    ga = c.sb("ga", [8], F32)
    gao = c.sb("gao", [4], F32)
    markA = c.off

    NSB = 6

    def allocA(tag, sample):
        t = NS()
        nb = 1 if sample else 2
        alloc_common(t, tag, nxt=(1 if sample else 3))
        t.uT = [c.sb(f"uT{tag}{i}", [8, 128], BF16) for i in range(nb)]
        t.junk2 = c.sb("junk2" + tag, [512], BF16)
        t.glT = c.sb("glT" + tag, [128], BF16)
        t.lz = c.sb("lz" + tag, [512], F32)
        t.eb = c.sb("eb" + tag, [512], F32)
        t.enb = c.sb("enb" + tag, [512], F32)
        t.qt = c.sb("qt" + tag, [512], BF16)
        t.kt = [c.sb(f"kt{tag}{i}", [512], BF16) for i in range(nb)]
        t.qkT = [c.sb(f"qkT{tag}{i}", [8, 128], BF16) for i in range(nb)]
        t.ATm = [c.sb(f"ATm{tag}{i}", [4, 128], BF16) for i in range(nb)]
        t.vv = [c.sb(f"v{tag}{h}", [512], BF16) for h in range(4)]
        t.sso = c.sb("sso" + tag, [4], F32)
        t.ro = c.sb("ro" + tag, [4], F32)
        t.sgo = [c.sb(f"sgo{tag}{i}", [512], F32) for i in range(2)]
        if not sample:
            t.ebl = [c.sb(f"ebl{tag}{i}", [4], F32) for i in range(nb)]
            t.S = [c.sb(f"S{h}", [512], F32) for h in range(4)]
            t.Sbf = [c.sb(f"Sbf{h}", [512], BF16) for h in range(4)]
        else:
            t.ebl = [c.sb("ebls" + tag, [4, NSEQ], F32)]
            t.CM = c.sb("CM", [NSEQ, 128], BF16)
            t.qms = [c.sb(f"qms{i}", [NSEQ, 128], BF16) for i in range(2)]
            t.km = [c.sb(f"km{i}", [128], BF16) for i in range(NSB)]
            t.Ss = [c.sb(f"Ss{i}", [512], F32) for i in range(NSB)]
            t.Ss_in = [P.slot(f"Ss_in{i}") for i in range(NSB)]
            t.Ss_out = [P.slot(f"Ss_out{i}") for i in range(NSB)]
            t.Ssbf = [c.sb(f"Ssbf{i}", [512], BF16) for i in range(NSB)]
            t.next_load = 0
        return t

    tA = allocA("A", False)
    wg2f = c.sb("wg2f", [512], F32)
    print("phase A arena words", c.off)

    c.dma(ga[:, :], g_a, [], [ga.b], P.slot("ga"))
    c.dma(gao[:, :], g_ao, [], [gao.b], P.slot("gao"))
    c.dma(wg2f[0:17, :], wg2a_d, [], [wg2f.b], P.slot("wg2"))
    c.cp("dve", wg2a[0:17, :], wg2f[0:17, :], [wg2f.b], [wg2a.b])
    c.memset("pool", tA.glT[0:17, :], 1.0, [tA.glT.b])
    for h in range(4):
        c.memset("pool", tA.S[h][:, :], 0.0, [tA.S[h].b])
        c.memset("pool", tA.Sbf[h][:, :], 0.0, [tA.Sbf[h].b])
    cnt = [0]
    stgA = tA.ho + tA.xt
    stgA_slots = [P.slot(f"stgA{i}") for i in range(len(stgA))]
    for _ in load_weight_gen(c, a_w_in, Wa_in, 8, 512, 3072, ga, 8, stgA, stgA_slots, cnt):
        pass
    for _ in load_weight_gen(c, a_w_in, Wa_in, 8, 5120, 5136, ga, 8, stgA, stgA_slots, cnt):
        pass

    def late_weights():
        cnt2 = [0]
        yield from load_weight_gen(c, a_w_in, Wa_in, 8, 0, 512, ga, 8, tA.ho, stgA_slots[0:2], cnt2)
        yield from load_weight_gen(c, a_w_in, Wa_in, 8, 3072, 5120, ga, 8, tA.ho, stgA_slots[0:2], cnt2)
        yield from load_weight_gen(c, a_w_out, Wa_out, 16, 0, 1024, gao, 4, tA.ho, stgA_slots[0:2], cnt2)

    h1blk = [Buf(f"h1blk{j}") for j in range(NMAIN)]

    def l0_block(t, src, row0, mode, i, dst_rows):
        full = mode != "pre"
        sample = mode == "sample"
        CU = U4 if sample else U
        xb = t.xt[i % len(t.xt)]
        uT = t.uT[i % len(t.uT)]
        ebl = t.ebl[i % len(t.ebl)]
        kt = t.kt[i % len(t.kt)]
        qkT = t.qkT[i % len(t.qkT)]
        ATm = t.ATm[i % len(t.ATm)]
        lz, eb, enb, qt, vv = t.lz, t.eb, t.enb, t.qt, t.vv
        def v_proj(h):
            bv = c.bank()
            proj(c, uT, Wa_in, 1024 + h * 512, 512, bv)
            c.cp("act", vv[h][:, :], c.bk(bv), [c.pb[bv]], [vv[h].b])
            c.free(bv)

        if sample:
            c.dma(xb[:, :], src[row0:row0 + 128, :], [], [xb.b], t.xt_slots[0])
            rms_stats(c, xb, t.junk, t.ss1, t.rr1, t.xn)
        rms_tr(c, t.xn, uT, ident, 0)
        yield
        bk = c.bank()
        for k in range(8):
            c.mm(c.bk(bk)[0:16, 0:128], Wa_in[:, k, 5120:5136], uT[:, k, :], k == 0, k == 7, [uT.b, Wa_in.b], [c.pb[bk]])
        c.cp("act", t.glT[0:16, :], c.bk(bk)[0:16, 0:128], [c.pb[bk]], [t.glT.b])
        c.free(bk)
        if full:
            bq = c.bank()
            proj(c, uT, Wa_in, 0, 512, bq)
        yield
        bkk = c.bank()
        proj(c, uT, Wa_in, 512, 512, bkk)
        yield
        bz = c.bank()
        c.mm(c.bk(bz), t.glT[0:17, :], wg2a[0:17, :], True, True, [t.glT.b, wg2a.b], [c.pb[bz]])
        c.act(lz[:, :], c.bk(bz), AF.Exp, [c.pb[bz]], [lz.b], scale=-1.0)
        c.free(bz)
        c.act(lz[:, :], lz[:, :], AF.Ln, [lz.b], [lz.b], bias=1.0)
        yield
        CU32 = U432 if sample else U32
        bb = c.bank()
        c.mm(c.bk(bb), CU32[:, :], lz[:, :], True, True, [CU32.b, lz.b], [c.pb[bb]])
        if full:
            c.act(eb[:, :], c.bk(bb), AF.Exp, [c.pb[bb]], [eb.b], scale=-1.0 / 16)
        c.act(enb[:, :], c.bk(bb), AF.Exp, [c.pb[bb]], [enb.b], scale=1.0 / 16)
        c.free(bb)
        yield
        bl = c.bank()
        if not sample:
            for h in range(4):
                c.mm(c.bk(bl)[:, h:h + 1], lz[:, h * 128:(h + 1) * 128], ones32[:, 0:1], True, True, [lz.b, ones32.b], [c.pb[bl]])
            c.act(ebl[:, :], c.bk(bl)[:, 0:4], AF.Exp, [c.pb[bl]], [ebl.b], scale=-1.0 / 16)
        else:
            for h in range(4):
                o_ = c.bk(bl)[:, h * NSEQ:(h + 1) * NSEQ]
                c.mm(o_, lz[:, h * 128:(h + 1) * 128], seqindf[:, :], True, True, [lz.b, seqindf.b], [c.pb[bl]])
            c.act(ebl[:, :, :], c.bk(bl)[:, 0:4 * NSEQ].rearrange("p (h s) -> p h s", h=4), AF.Exp, [c.pb[bl]], [ebl.b],
                  scale=-1.0 / 16)
        c.free(bl)
        if full:
            c.stt(qt[:, :], c.bk(bq), 128.0 ** -0.5, eb[:, :], ALU.mult, ALU.mult, [c.pb[bq], eb.b], [qt.b])
            c.free(bq)
        c.tt("dve", kt[:, :], c.bk(bkk), enb[:, :], ALU.mult, [c.pb[bkk], enb.b], [kt.b])
        c.free(bkk)
        v_proj(0)
        yield
        v_proj(1)
        yield
        if full:
            pt = c.bkbf(1).rearrange("p (k t) -> p k t", k=8)
            for h in range(4):
                c.tr(pt[:, h, :], qt[:, h * 128:(h + 1) * 128], ident[:, :], [qt.b, ident.b], [c.pb[1]])
            for h in range(4):
                c.tr(pt[:, 4 + h, :], kt[:, h * 128:(h + 1) * 128], ident[:, :], [kt.b, ident.b], [c.pb[1]])
            c.cp("act", qkT[:, :, :], pt, [c.pb[1]], [qkT.b])
            yield
            ba = c.bank()
            for h in range(4):
                c.mm(c.bk(ba)[:, h * 128:(h + 1) * 128], qkT[:, 4 + h, :], qkT[:, h, :], True, True, [qkT.b], [c.pb[ba]])
            c.tt("dve", ATm[:, :, :], c.bk(ba).rearrange("p (h t) -> p h t", h=4), bc(CU[:, None, :], [128, 4, 128]),
                 ALU.mult, [c.pb[ba], CU.b], [ATm.b])
            c.free(ba)
        yield "BACK"

        for h in range(4):
            bo = None
            if not sample:
                S, Sbf = t.S, t.Sbf
                if full:
                    bo = c.bank()
                    c.mm(c.bk(bo), ATm[:, h, :], vv[h][:, :], True, False, [ATm.b, vv[h].b], [c.pb[bo]])
                    c.mm(c.bk(bo), qkT[:, h, :], Sbf[h][:, :], False, True, [qkT.b, Sbf[h].b], [c.pb[bo]])
                bd = c.bank()
                c.mm(c.bk(bd), kt[:, h * 128:(h + 1) * 128], vv[h][:, :], True, True, [kt.b, vv[h].b], [c.pb[bd]])
                if full:
                    bg = c.bank()
                    proj(c, uT, Wa_in, 3072 + h * 512, 512, bg)
                yield
                c.ts("dve", S[h][:, :], S[h][:, :], ebl[:, h:h + 1], ALU.mult, [S[h].b, ebl.b], [S[h].b])
                c.stt(S[h][:, :], c.bk(bd), ebl[:, h:h + 1], S[h][:, :], ALU.mult, ALU.add, [c.pb[bd], ebl.b, S[h].b], [S[h].b])
                c.free(bd)
                c.cp("pool", Sbf[h][:, :], S[h][:, :], [S[h].b], [Sbf[h].b])
            else:
                qms = t.qms[h % 2]
                c.tt("pool", qms[:, :, :], bc(qkT[:, h:h + 1, :], [128, NSEQ, 128]), t.CM[:, :, :], ALU.mult,
                     [qkT.b, t.CM.b], [qms.b])
                bo = c.bank()
                c.mm(c.bk(bo), ATm[:, h, :], vv[h][:, :], True, False, [ATm.b, vv[h].b], [c.pb[bo]])
                for s in range(NSEQ):
                    idx = h * NSEQ + s
                    Sb, Sbb, km = t.Ss[idx % NSB], t.Ssbf[idx % NSB], t.km[idx % NSB]
                    while t.next_load <= min(idx + NSB - 1, 4 * NSEQ - 1):
                        li = t.next_load
                        c.dma(t.Ss[li % NSB][:, :], state_s[li % NSEQ, li // NSEQ], [], [t.Ss[li % NSB].b], t.Ss_in[li % NSB])
                        t.next_load += 1
                    c.cp("act", Sbb[:, :], Sb[:, :], [Sb.b], [Sbb.b])
                    c.mm(c.bk(bo), qms[:, s, :], Sbb[:, :], False, s == NSEQ - 1, [qms.b, Sbb.b], [c.pb[bo]])
                    c.ts("dve", km[:, :], kt[:, h * 128:(h + 1) * 128], seqindf[:, s:s + 1], ALU.mult, [kt.b, seqindf.b], [km.b])
                    bd = c.bank()
                    c.mm(c.bk(bd), km[:, :], vv[h][:, :], True, True, [km.b, vv[h].b], [c.pb[bd]])
                    c.ts("dve", Sb[:, :], Sb[:, :], ebl[:, h, s:s + 1], ALU.mult, [Sb.b, ebl.b], [Sb.b])
                    c.stt(Sb[:, :], c.bk(bd), ebl[:, h, s:s + 1], Sb[:, :], ALU.mult, ALU.add, [c.pb[bd], ebl.b, Sb.b], [Sb.b])
                    c.free(bd)
                    finals.append(c.dma(st_s[s, h], Sb[:, :], [Sb.b], [], t.Ss_out[idx % NSB], eng="pool"))
                bg = c.bank()
                proj(c, uT, Wa_in, 3072 + h * 512, 512, bg)
            if full:
                c.act(t.junk2[:, 0:512], c.bk(bo), AF.Square, [c.pb[bo]], [t.junk2.b, t.sso.b], accum=t.sso[:, h:h + 1])
                c.act(t.ro[:, h:h + 1], t.sso[:, h:h + 1], AF.Ln, [t.sso.b], [t.ro.b], scale=1.0 / 512, bias=1e-6)
                c.act(t.ro[:, h:h + 1], t.ro[:, h:h + 1], AF.Exp, [t.ro.b], [t.ro.b], scale=-0.5)
                sg = t.sig[h % 2]
                so = t.sgo[h % 2]
                sigmoid_act(c, sg, bg)
                yield
                c.stt(so[:, :], c.bk(bo), t.ro[:, h:h + 1], sg[:, :], ALU.mult, ALU.mult, [c.pb[bo], t.ro.b, sg.b], [so.b])
                c.tt("dve", t.og[:, h * 512:(h + 1) * 512], c.bk(bg), so[:, :], ALU.mult, [c.pb[bg], so.b], [t.og.b])
                c.free(bg)
                c.free(bo)
            if h + 2 < 4:
                v_proj(h + 2)
            yield
        if full and not sample:
            hb = t.ho[i % 2]
            yield
            yield from out_proj(c, t.og, t.ogT, Wa_out, xb, hb, ident)
            c.dma(h1s[dst_rows:dst_rows + 128, :], hb[:, :], [hb.b], [h1blk[dst_rows // 128]], t.ho_slots[i % 2])
        elif sample:
            yield from out_proj(c, t.og, t.ogT, Wa_out, xb, hs, ident)

    gens = []
    srcsA = []
    bi = 0
    for j in range(NPRE):
        gens.append(l0_block(tA, x_pre, j * 128, "pre", bi, None))
        srcsA.append(x_pre[j * 128:(j + 1) * 128, :])
        bi += 1
    for j in range(NMAIN):
        gens.append(l0_block(tA, x_main, j * 128, "full", bi, j * 128))
        srcsA.append(x_main[j * 128:(j + 1) * 128, :])
        bi += 1

    def prefetchA(k):
        xb_ = tA.xt[k % 3]
        c.dma(xb_[:, :], srcsA[k], [], [xb_.b], tA.xt_slots[k % 3])

    def prermsA(k):
        rms_stats(c, tA.xt[k % 3], tA.junk, tA.ss1, tA.rr1, tA.xn)

    ratiosA = [1 if (k - 1) < NPRE else 2 for k in range(len(gens))]
    drive(gens, prefetchA, prermsA, rms_round=2, ratios=ratiosA, extra=late_weights(), extra_deadline=max(NPRE - 1, 0))
    stp_slot = P.slot("st_p")
    for h in range(4):
        finals.append(c.dma(st_p[h], tA.S[h][:, :], [tA.S[h].b], [], stp_slot))

    P.barrier()
    assert not c.hold, c.hold
    c.off = markA
    tAs = allocA("As", True)
    print("phase A(sample) arena words", c.off)
    c.dma(tAs.CM[:, :, :], CM_d.rearrange("p (s t) -> p s t", s=NSEQ), [], [tAs.CM.b], P.slot("c_CM"))
    c.memset("pool", tAs.glT[0:17, :], 1.0, [tAs.glT.b])
    drive([l0_block(tAs, x_s, 0, "sample", 0, None)])

    import os
    if os.environ.get("STOP_AFTER") == "A0":
        P.build(final_waits=finals)
        return nc
    if os.environ.get("STOP_AFTER") == "A":
        P.barrier()
        P.build(final_waits=finals)
        return nc
    P.barrier()
    assert not c.hold, c.hold
    P.new_epoch()
    c.off = persist_mark
    Wb_in = c.sb("Wb_in", [8, 4096], BF16)
    Wb_out = c.sb("Wb_out", [16, 1024], BF16)
    Wkv = c.sb("Wkv", [8, 512], BF16)
    gkv = c.sb("gkv", [8], F32)
    gb = c.sb("gb", [8], F32)
    GK = c.sb("GK", [64], F32)
    GKn = c.sb("GKn", [64], F32)
    GQ = c.sb("GQ", [64], F32)
    GQn = c.sb("GQn", [64], F32)
    ESINK = c.sb("ESINK", [32], F32)
    valid = c.sb("valid", [1], F32)
    markB = c.off

    def allocB(tag, sample, ncs):
        t = NS()
        nb = 1 if sample else 2
        nkv = 1 if sample else 3
        if sample:
            alloc_common(t, tag, nxt=0, nho=1, nsig=1)
        else:
            alloc_common(t, tag)
        t.uT = c.sb("uT" + tag, [8, 128], BF16)
        t.COS = c.sb("COS" + tag, [ncs, 32], F32)
        t.SIN = c.sb("SIN" + tag, [ncs, 32], F32)
        t.CK = c.sb("CK" + tag, [64], F32)
        t.SK = c.sb("SK" + tag, [64], F32)
        t.CQ = c.sb("CQ" + tag, [64], F32)
        t.SQ = c.sb("SQ" + tag, [64], F32)
        t.ksq = c.sb("ksq" + tag, [256], F32)
        t.ssk = c.sb("ssk" + tag, [4], F32)
        t.rk = c.sb("rk" + tag, [4], F32)
        t.kn = c.sb("kn" + tag, [4, 64], F32)
        t.km1 = c.sb("km1" + tag, [4, 64], F32)
        t.km2 = c.sb("km2" + tag, [4, 64], F32)
        t.kr = c.sb("kr" + tag, [4, 64], F32)
        t.vf = c.sb("vf" + tag, [256], F32)
        t.kdup = c.sb("kdup" + tag, [4, 2, 64], BF16)
        t.KT = [c.sb(f"KT{tag}{i}", [4, 128], BF16) for i in range(nkv)]
        t.Vaug = [c.sb(f"Vaug{tag}{i}", [4, 65], BF16) for i in range(nkv)]
        t.qsq = c.sb("qsq" + tag, [512], F32)
        t.ssq = c.sb("ssq" + tag, [32], F32)
        t.rq = c.sb("rq" + tag, [32], F32)
        t.qm1 = [c.sb(f"qm1{tag}{i}", [8, 64], F32) for i in range(nb)]
        t.qm2 = [c.sb(f"qm2{tag}{i}", [8, 64], F32) for i in range(nb)]
        t.qr = c.sb("qr" + tag, [2048], BF16)
        t.QT = [c.sb(f"QT{tag}{i}", [16, 128], BF16) for i in range(nb)]
        t.sgate = [c.sb(f"sgate{tag}{i}", [2048], BF16) for i in range(nb)]
        t.PT2 = [c.sb(f"PT2{tag}{h}", [nb, 4, 128], BF16) for h in range(2)]
        t.den = c.sb("den" + tag, [4], F32)
        t.rec = c.sb("rec" + tag, [4], F32)
        t.onrm = c.sb("onrm" + tag, [4, 64], F32)
        if sample:
            t.KTs = c.sb("KTs", [NSEQ, 4, 128], BF16)
            t.Vs = c.sb("Vs", [NSEQ, 4, 65], BF16)
            t.PTz = c.sb("PTz", [NSEQ * 516 + 64], BF16)
            t.kdups = [c.sb(f"kdups{i}", [4, 2, 64], BF16) for i in range(4)]
            t.kd_slots = [[P.slot(f"kd{i}a"), P.slot(f"kd{i}b")] for i in range(4)]
            t.vs_slots = [P.slot(f"vs_in{i}") for i in range(4)]
            t.vs_ser = [Buf(f"vs_ser{i}") for i in range(4)]
            t.VsB = [Buf(f"VsB{i}") for i in range(NSEQ)]
        return t

    tB = allocB("B", False, NMAIN)
    print("phase B arena words", c.off)

    for t, d in ((gkv, g_kv), (gb, g_b), (GK, gk_d), (GQ, gq_d), (ESINK, sinks_d), (valid, valid_d)):
        c.dma(t[:, :], d, [], [t.b], P.slot("c_" + t.b.name))
    c.dma(tB.COS[:, :, :], cos_d, [], [tB.COS.b], P.slot("c_cos"))
    c.dma(tB.SIN[:, :, :], sin_d, [], [tB.SIN.b], P.slot("c_sin"))
    c.ts("pool", GKn[:, :], GK[:, :], -1.0, ALU.mult, [GK.b], [GKn.b])
    c.ts("pool", GQn[:, :], GQ[:, :], -1.0, ALU.mult, [GQ.b], [GQn.b])
    c.act(ESINK[:, :], ESINK[:, :], AF.Exp, [ESINK.b], [ESINK.b])
    for i in range(3):
        c.memset("pool", tB.Vaug[i][:, :, :], 1.0, [tB.Vaug[i].b])
    if os.environ.get("STOP_AFTER") == "BS":
        P.barrier()
        P.build(final_waits=finals)
        return nc
    cnt = [0]
    stgB = tB.ho + tB.xt
    stgB_slots = [P.slot(f"stgB{i}") for i in range(len(stgB))]
    load_weight(c, wkv, Wkv, 8, 512, gkv, 8, stgB, stgB_slots, cnt)
    load_weight(c, b_w_in, Wb_in, 8, 4096, gb, 8, stgB, stgB_slots, cnt)

    def late_weights_B():
        cnt2 = [0]
        yield from load_weight_gen(c, b_w_out, Wb_out, 16, 0, 1024, None, 1, tB.ho, stgB_slots[0:2], cnt2)

    if os.environ.get("STOP_AFTER") == "BW":
        P.barrier()
        P.build(final_waits=finals)
        return nc

    def rope_tables(t, Ct, St, G, Gn, j):
        cj = t.COS[:, j, :]
        sj = t.SIN[:, j, :]
        c.tt("pool", Ct[:, 0:32], G[:, 0:32], cj, ALU.mult, [G.b, t.COS.b], [Ct.b])
        c.tt("pool", Ct[:, 32:64], G[:, 32:64], cj, ALU.mult, [G.b, t.COS.b], [Ct.b])
        c.tt("pool", St[:, 0:32], Gn[:, 32:64], sj, ALU.mult, [Gn.b, t.SIN.b], [St.b])
        c.tt("pool", St[:, 32:64], G[:, 0:32], sj, ALU.mult, [G.b, t.SIN.b], [St.b])

    ckvs_slot = P.slot("ckvs")

    def l1_block(t, j, mode):
        halo = mode == "halo"
        sample = mode == "sample"
        last = (mode == "full" and j == NMAIN - 1)
        uT = t.uT
        nkv = len(t.KT)
        KTc, Vc = t.KT[j % nkv], t.Vaug[j % nkv]
        KTp, Vp = t.KT[(j - 1) % nkv], t.Vaug[(j - 1) % nkv]
        QT = t.QT[j % len(t.QT)]
        sgate = t.sgate[j % len(t.sgate)]
        PT2 = t.PT2
        if sample:
            xb = hs
            rms_stats(c, xb, t.junk, t.ss1, t.rr1, t.xn)
        else:
            xb = t.xt[j % 3]
        rms_tr(c, t.xn, uT, ident, 0)
        tj = 0 if sample else j
        yield
        OLD_KT = os.environ.get("OLD_KT", "0") == "1"
        OLD_QT = os.environ.get("OLD_QT", "0") == "1"

        def kt_transposes():
            pt = c.bkbf(1).rearrange("p (k t) -> p k t", k=8)
            for gg in range(4):
                c.tr(pt[:, gg, :], t.kdup[:, gg, :, :].rearrange("p a d -> p (a d)"), ident[:, :], [t.kdup.b, ident.b], [c.pb[1]])
            c.cp("act", KTc[:, :, :], pt[:, 0:4, :], [c.pb[1]], [KTc.b])

        def qt_transposes():
            ptq = c.psum[:, 0:2, :].bitcast(BF16).rearrange("p b (k t) -> p (b k) t", k=8)
            for k in range(16):
                c.tr(ptq[:, k, :], qr[:, k * 128:(k + 1) * 128], ident[:, :], [qr.b, ident.b], [c.pb[0], c.pb[1]])
            c.cp("dve", QT[:, 0:8, :], ptq[:, 0:8, :], [c.pb[0]], [QT.b])
            c.cp("act", QT[:, 8:16, :], ptq[:, 8:16, :], [c.pb[1]], [QT.b])


        def kv_section():
            rope_tables(t, t.CK, t.SK, GK, GKn, tj)
            bkv = c.bank()
            proj(c, uT, Wkv, 0, 512, bkv)
            kps = c.bk(bkv)[:, 0:256].rearrange("p (g d) -> p g d", g=4)
            kn, km1, km2, kr, kdup = t.kn, t.km1, t.km2, t.kr, t.kdup
            c.act(t.ksq[:, :], c.bk(bkv)[:, 0:256], AF.Square, [c.pb[bkv]], [t.ksq.b])
            vps = c.bk(bkv)[:, 256:512].rearrange("p (g d) -> p g d", g=4)
            if j == nkv and not sample:
                c.memset("pool", Vc[:, :, 64:65], 1.0, [Vc.b])
            c.cp("act", Vc[:, :, 0:64], vps, [c.pb[bkv]], [Vc.b])
            if last or sample:
                c.cp("act", t.vf[:, :], c.bk(bkv)[:, 256:512], [c.pb[bkv]], [t.vf.b])
            c.P.emit("dve", lambda e: e.tensor_reduce(out=t.ssk[:, :], in_=t.ksq[:, :].rearrange("p (g d) -> p g d", g=4),
                                                      axis=AX.X, op=ALU.add), [t.ksq.b], [t.ssk.b])
            c.act(t.rk[:, :], t.ssk[:, :], AF.Ln, [t.ssk.b], [t.rk.b], scale=1.0 / 64, bias=1e-6)
            c.act(t.rk[:, :], t.rk[:, :], AF.Exp, [t.rk.b], [t.rk.b], scale=-0.5)
            c.tt("dve", kn[:, :, :], kps, bc(t.rk[:, :].unsqueeze(2), [128, 4, 64]), ALU.mult, [c.pb[bkv], t.rk.b], [kn.b])
            c.free(bkv)
            yield
            c.tt("pool", km1[:, :, :], kn[:, :, :], bc(t.CK[:, None, :], [128, 4, 64]), ALU.mult, [kn.b, t.CK.b], [km1.b])
            c.tt("pool", km2[:, :, 0:32], kn[:, :, 32:64], bc(t.SK[:, None, 0:32], [128, 4, 32]), ALU.mult, [kn.b, t.SK.b], [km2.b])
            c.tt("pool", km2[:, :, 32:64], kn[:, :, 0:32], bc(t.SK[:, None, 32:64], [128, 4, 32]), ALU.mult, [kn.b, t.SK.b], [km2.b])
            c.tt("pool", kr[:, :, :], km1[:, :, :], km2[:, :, :], ALU.add, [km1.b, km2.b], [kr.b])
            c.cp("pool", kdup[:, :, 0, :], kr[:, :, :], [kr.b], [kdup.b])
            c.cp("pool", kdup[:, :, 1, :], kr[:, :, :], [kr.b], [kdup.b])
            if halo:
                kt_transposes()
            if halo:
                c.ts("dve", Vc[:, :, :], Vc[:, :, :], valid[:, 0:1], ALU.mult, [Vc.b, valid.b], [Vc.b])
            if last:
                finals.append(c.dma(ck_p, kr[:, :, :].rearrange("p g d -> p (g d)"), [kr.b], [], P.slot("ck_p")))
                finals.append(c.dma(cv_p, t.vf[:, :], [t.vf.b], [], P.slot("cv_p")))
            if sample:
                krf = kr[:, :, :].rearrange("p g d -> p (g d)")
                for s in range(NSEQ):
                    finals.append(c.dma(ck_s[s, 128 - TS:128, :], krf[s * TS:(s + 1) * TS, :], [kr.b], [], ckvs_slot))
                    finals.append(c.dma(cv_s[s, 128 - TS:128, :], t.vf[s * TS:(s + 1) * TS, :], [t.vf.b], [], ckvs_slot))
            yield

        ssq, rq, qr = t.ssq, t.rq, t.qr
        CUT = int(os.environ.get("L1CUT", "0"))
        if CUT == 1 and not halo:
            return
        if halo:
            yield from kv_section()
        if not halo:
            rope_tables(t, t.CQ, t.SQ, GQ, GQn, tj)
            for g in range(4):
                bq = c.bank()
                proj(c, uT, Wb_in, g * 512, 512, bq)
                qps = c.bk(bq).rearrange("p (h d) -> p h d", h=8)
                c.act(t.qsq[:, :], c.bk(bq), AF.Square, [c.pb[bq]], [t.qsq.b])
                c.P.emit("dve", lambda e, g=g: e.tensor_reduce(out=ssq[:, g * 8:(g + 1) * 8],
                                                               in_=t.qsq[:, :].rearrange("p (h d) -> p h d", h=8),
                                                               axis=AX.X, op=ALU.add), [t.qsq.b], [ssq.b])
                c.act(rq[:, g * 8:(g + 1) * 8], ssq[:, g * 8:(g + 1) * 8], AF.Ln, [ssq.b], [rq.b], scale=1.0 / 64, bias=1e-6)
                c.act(rq[:, g * 8:(g + 1) * 8], rq[:, g * 8:(g + 1) * 8], AF.Exp, [rq.b], [rq.b], scale=-0.5)
                m1, m2 = t.qm1[g % len(t.qm1)], t.qm2[g % len(t.qm2)]
                c.tt("dve", m1[:, :, :], qps, bc(t.CQ[:, None, :], [128, 8, 64]), ALU.mult, [c.pb[bq], t.CQ.b], [m1.b])
                c.tt("dve", m2[:, :, 0:32], qps[:, :, 32:64], bc(t.SQ[:, None, 0:32], [128, 8, 32]), ALU.mult, [c.pb[bq], t.SQ.b], [m2.b])
                c.tt("dve", m2[:, :, 32:64], qps[:, :, 0:32], bc(t.SQ[:, None, 32:64], [128, 8, 32]), ALU.mult, [c.pb[bq], t.SQ.b], [m2.b])
                c.free(bq)
                c.tt("dve", m1[:, :, :], m1[:, :, :], m2[:, :, :], ALU.add, [m1.b, m2.b], [m1.b])
                c.tt("pool", qr[:, g * 512:(g + 1) * 512].rearrange("p (h d) -> p h d", h=8), m1[:, :, :],
                     bc(rq[:, g * 8:(g + 1) * 8].unsqueeze(2), [128, 8, 64]), ALU.mult, [m1.b, rq.b], [qr.b])
                yield
            yield from kv_section()
            for g in range(4):
                bg = c.bank()
                proj(c, uT, Wb_in, 2048 + g * 512, 512, bg)
                sg = t.sig[g % len(t.sig)]
                sigmoid_act(c, sg, bg)
                c.tt("dve", sgate[:, g * 512:(g + 1) * 512], c.bk(bg), sg[:, :], ALU.mult, [c.pb[bg], sg.b], [sgate.b])
                c.free(bg)
                yield
            kt_transposes()
            qt_transposes()
        if CUT == 4 and not halo:
            return
        yield "BACK"
        if halo:
            return
        og = t.og
        ogv = og[:, :].rearrange("p (g j f d) -> p g j f d", g=4, j=4, f=2)
        sgv = sgate[:, :].rearrange("p (g j f d) -> p g j f d", g=4, j=4, f=2)
        esv = ESINK[:, :].rearrange("p (g j f) -> p g j f", g=4, j=4)
        den, rec, onrm = t.den, t.rec, t.onrm
        if sample:
            PTz = t.PTz
            pz_diag = PTz[:, 0:NSEQ * 516].rearrange("p (s r) -> p s r", r=516)[:, :, 0:512].rearrange(
                "p s (j q) -> p s j q", q=128)[:, :, :, 0:TS]
            pz_full = PTz[:, 0:NSEQ * 512].rearrange("p (s j q) -> p s j q", s=NSEQ, j=4)
        def scores(g, half):
            lo, hi = half * 64, (half + 1) * 64
            if not sample:
                tiles = ((KTp, Vp, L), (KTc, Vc, U))
            else:
                tiles = ((KTc, Vc, U4),)
                bs = c.bank()
                for s in range(NSEQ):
                    c.mm(c.bk(bs)[:, s * 16:(s + 1) * 16], t.KTs[lo:hi, s, g, :], QT[lo:hi, 4 * g:4 * g + 4, s * TS:(s + 1) * TS],
                         True, True, [t.KTs.b, QT.b], [c.pb[bs]])
                c.act(pz_diag, c.bk(bs)[:, 0:NSEQ * 16].rearrange("p (s j t) -> p s j t", s=NSEQ, j=4), AF.Exp,
                      [c.pb[bs]], [PTz.b], scale=0.125)
                c.free(bs)
                c.tt("pool", pz_diag, pz_diag, bc(CMK[:, None, None, :], [128, NSEQ, 4, TS]), ALU.mult, [PTz.b, CMK.b], [PTz.b])
            p2 = PT2[half]
            for kti, (ktile, vt, mask) in enumerate(tiles):
                bs = c.bank()
                c.mm(c.bk(bs), ktile[lo:hi, g, :], QT[lo:hi, 4 * g:4 * g + 4, :], True, True, [ktile.b, QT.b], [c.pb[bs]])
                c.act(p2[:, kti, :, :], c.bk(bs).rearrange("p (j t) -> p j t", j=4), AF.Exp, [c.pb[bs]], [p2.b], scale=0.125)
                c.free(bs)
            if not sample:
                c.tt("dve", p2[:, :, :, :], p2[:, :, :, :], bc(LU[:, :, None, :], [128, 2, 4, 128]), ALU.mult, [p2.b, LU.b], [p2.b])
            else:
                c.tt("dve", p2[:, 0, :, :], p2[:, 0, :, :], bc(U4[:, None, :], [128, 4, 128]), ALU.mult, [p2.b, U4.b], [p2.b])

        def pv(g, half):
            bo = c.bank()
            ob = c.bk(bo)[:, 0:260].rearrange("p (j e) -> p j e", j=4)
            for jj in range(4):
                if not sample:
                    c.mm(ob[:, jj, :], PT2[half][:, 0, jj, :], Vp[:, g, :], True, False, [PT2[half].b, Vp.b], [c.pb[bo]])
                    c.mm(ob[:, jj, :], PT2[half][:, 1, jj, :], Vc[:, g, :], False, True, [PT2[half].b, Vc.b], [c.pb[bo]])
                else:
                    for s in range(NSEQ):
                        c.mm(ob[:, jj, :], pz_full[:, s, jj, :], t.Vs[:, s, g, :], s == 0, False,
                             [PTz.b, t.VsB[s]], [c.pb[bo]])
                    c.mm(ob[:, jj, :], PT2[half][:, 0, jj, :], Vc[:, g, :], False, True, [PT2[half].b, Vc.b], [c.pb[bo]])
            c.tt("dve", den[:, :], ob[:, :, 64], esv[:, g, :, half], ALU.add, [c.pb[bo], ESINK.b], [den.b])
            c.P.emit("dve", lambda e: e.reciprocal(out=rec[:, :], in_=den[:, :]), [den.b], [rec.b])
            c.tt("dve", onrm[:, :, :], ob[:, :, 0:64], bc(rec[:, :].unsqueeze(2), [128, 4, 64]), ALU.mult, [c.pb[bo], rec.b], [onrm.b])
            c.free(bo)
            c.tt("pool", ogv[:, g, :, half, :], onrm[:, :, :], sgv[:, g, :, half, :], ALU.mult, [onrm.b, sgate.b], [og.b])

        its = [(g, half) for g in range(4) for half in range(2)]
        if sample:
            for (g, half) in its:
                scores(g, half)
                yield
                pv(g, half)
                yield
        else:
            scores(*its[0])
            yield
            for i, (g, half) in enumerate(its):
                if i + 1 < len(its):
                    scores(*its[i + 1])
                    yield
                pv(g, half)
                yield
        if CUT == 5:
            return
        hb = t.ho[j % len(t.ho)]
        yield
        yield from out_proj(c, og, t.ogT, Wb_out, xb, hb, ident)
        if sample:
            finals.append(c.dma(y_s, hb[0:NSEQ * TS, :], [hb.b], [], t.ho_slots[0]))
        else:
            finals.append(c.dma(y_main[(j - 1) * 128:j * 128, :], hb[:, :], [hb.b], [], t.ho_slots[j % 2]))

    gens = [l1_block(tB, 0, "halo")]
    for j in range(1, NMAIN):
        gens.append(l1_block(tB, j, "full"))
    if os.environ.get("STOP_AFTER") == "BH":
        gens = gens[:1]

    def prefetchB(k):
        xb_ = tB.xt[k % 3]
        c.dma(xb_[:, :], h1s[k * 128:(k + 1) * 128, :], [h1blk[k]], [xb_.b], tB.xt_slots[k % 3])

    def prermsB(k):
        rms_stats(c, tB.xt[k % 3], tB.junk, tB.ss1, tB.rr1, tB.xn)

    drive(gens, prefetchB, prermsB, rms_round=3, extra=late_weights_B(), extra_deadline=2, extra_from=0)
    if os.environ.get("STOP_AFTER") == "BH":
        P.barrier()
        P.build(final_waits=[f for f in finals])
        return nc

    if os.environ.get("STOP_AFTER") == "B1":
        P.barrier()
        P.build(final_waits=finals)
        return nc
    P.barrier()
    assert not c.hold, c.hold
    c.off = markB
    tBs = allocB("Bs", True, 1)
    print("phase B(sample) arena words", c.off)
    c.dma(tBs.COS[:, :, :], coss_d, [], [tBs.COS.b], P.slot("c_coss"))
    c.dma(tBs.SIN[:, :, :], sins_d, [], [tBs.SIN.b], P.slot("c_sins"))
    c.memset("pool", tBs.Vaug[0][:, :, :], 1.0, [tBs.Vaug[0].b])
    c.memset("dve", tBs.Vs[:, :, :, :], 1.0, tBs.VsB)
    c.memset("dve", tBs.PTz[:, :], 0.0, [tBs.PTz.b])
    ck4 = cache_k_s.rearrange("s p (g d) -> s p g d", g=4)
    cv4 = cache_v_s.rearrange("s p (g d) -> s p g d", g=4)
    for s in range(NSEQ):
        kd = tBs.kdups[s % 4]
        c.dma(kd[:, :, 0, :], ck4[s], [], [kd.b], tBs.kd_slots[s % 4][0], eng="pool")
        c.dma(kd[:, :, 1, :], ck4[s], [], [kd.b], tBs.kd_slots[s % 4][1], eng="pool")
        c.dma(tBs.Vs[:, s, :, 0:64], cv4[s], [], [tBs.VsB[s], tBs.vs_ser[s % 4]], tBs.vs_slots[s % 4], eng="pool")
        tb = s % 2
        pt = c.bkbf(tb).rearrange("p (k t) -> p k t", k=8)
        for g in range(4):
            c.tr(pt[:, g, :], kd[:, g, :, :].rearrange("p a d -> p (a d)"), ident[:, :], [kd.b, ident.b], [c.pb[tb]])
        c.cp("act", tBs.KTs[:, s, :, :], pt[:, 0:4, :], [c.pb[tb]], [tBs.KTs.b])
    drive([l1_block(tBs, 0, "sample")])

    P.build(final_waits=finals)
    return nc


_CACHE = {}


def _consts():
    bf = ml_dtypes.bfloat16
    i = np.arange(128)
    Um = (i[:, None] <= i[None, :]).astype(np.float32)
    same = (i[:, None] // TS == i[None, :] // TS)
    U4 = (Um * same).astype(np.float32)
    seqind = (i[:, None] // TS == np.arange(NSEQ)[None, :]).astype(np.float32)
    CM = np.broadcast_to((np.arange(NSEQ)[:, None] == (i[None, :] // TS)).astype(np.float32)[None], (128, NSEQ, 128))
    CMK = (i[:, None] >= np.arange(TS)[None, :]).astype(np.float32)
    return {
        "ident": np.eye(128, dtype=np.float32).astype(bf),
        "U": Um.astype(bf),
        "L": Um.T.copy().astype(bf),
        "LU": np.concatenate([Um.T, Um], axis=1).astype(bf),
        "U32": Um.astype(np.float32),
        "U432": U4.astype(np.float32),
        "ones32": np.ones((128, 8), np.float32),
        "U4": U4.astype(bf),
        "ones": np.ones((128, 128), np.float32).astype(bf),
        "seqind": seqind.astype(bf),
        "seqindf": seqind.astype(np.float32),
        "CM": np.ascontiguousarray(CM.reshape(128, NSEQ * 128)).astype(bf),
        "CMK": CMK.astype(bf),
    }


def _pk(v, k):
    return np.ascontiguousarray(np.asarray(v, np.float32).reshape(k, 128).T)


def _rope_tab(pos):
    half = 32
    inv = (10000.0 ** (-np.arange(half, dtype=np.float64) / half)).astype(np.float32)
    ang = (pos.astype(np.float32)[:, None] * inv[None, :]).astype(np.float32).astype(np.float64)
    return np.cos(ang).astype(np.float32), np.sin(ang).astype(np.float32)


def run(inputs, NB):
    f = lambda a: np.asarray(a, np.float32)
    x_prompt = f(inputs["x_prompt"])
    NMAIN = NB + 1
    NPRE = NB - 1
    if NB not in _CACHE:
        _CACHE[NB] = build_program(NB)
    nc = _CACHE[NB]
    shared = {
        "a_w_in": f(inputs["a_w_in"])[0], "a_w_out": f(inputs["a_w_out"])[0],
        "wkv": np.ascontiguousarray(np.concatenate([f(inputs["w_k"]), f(inputs["w_v"])], axis=1)),
        "b_w_in": f(inputs["b_w_in"])[0], "b_w_out": f(inputs["b_w_out"])[0],
        "wg2a": np.ascontiguousarray(np.concatenate([f(inputs["a_w_gate2"])[0], f(inputs["a_b_gate"])[0][None, :]], axis=0)),
        "g_a": _pk(inputs["a_norm"][0], 8), "g_ao": _pk(inputs["a_out_norm"][0], 4),
        "g_kv": _pk(inputs["kv_norm"], 8), "g_b": _pk(inputs["b_norm"][0], 8),
        "gk": np.ascontiguousarray(np.broadcast_to(f(inputs["k_norm"])[None, :], (128, 64))),
        "gq": np.ascontiguousarray(np.broadcast_to(f(inputs["b_q_norm"])[0][None, :], (128, 64))),
        "sinks": np.ascontiguousarray(np.broadcast_to(f(inputs["b_sinks"])[0][None, :], (128, 32))),
        **_consts(),
    }
    PAST = 16384
    pos_s = PAST + (np.arange(128) % TS)
    cs, sn = _rope_tab(pos_s)
    shared["cos_s"] = np.ascontiguousarray(cs.reshape(128, 1, 32))
    shared["sin_s"] = np.ascontiguousarray(sn.reshape(128, 1, 32))
    x_sample = f(inputs["x_sample"])
    state = f(inputs["state_gla"])[0]
    ckc = f(inputs["cache_swa_k"])
    cvc = f(inputs["cache_swa_v"])
    in_maps = []
    for core in range(NCORES):
        b, hf = core // 2, core % 2
        start_blk = hf * NB
        xs = x_prompt[b]
        zeros = np.zeros((128, D), np.float32)
        if hf == 0:
            x_pre = np.zeros((max(NPRE, 1) * 128, D), np.float32)
            x_main = np.concatenate([zeros, xs[0:NB * 128]], axis=0)
        else:
            x_pre = xs[0:NPRE * 128] if NPRE > 0 else np.zeros((128, D), np.float32)
            x_main = xs[(NB - 1) * 128:2 * NB * 128]
        pos = (start_blk - 1) * 128 + np.arange(NMAIN * 128)
        cos, sin = _rope_tab(pos)
        cos = cos.reshape(NMAIN, 128, 32).transpose(1, 0, 2)
        sin = sin.reshape(NMAIN, 128, 32).transpose(1, 0, 2)
        sq = slice(core * NSEQ, (core + 1) * NSEQ)
        xsp = np.zeros((128, D), np.float32)
        xsp[0:NSEQ * TS] = x_sample[sq].reshape(NSEQ * TS, D)
        m = dict(shared)
        m.update({
            "x_pre": np.ascontiguousarray(x_pre), "x_main": np.ascontiguousarray(x_main),
            "cos": np.ascontiguousarray(cos), "sin": np.ascontiguousarray(sin),
            "valid": np.full((128, 1), float(hf), np.float32),
            "x_s": xsp,
            "state_s": np.ascontiguousarray(state[sq]),
            "cache_k_s": np.ascontiguousarray(ckc[sq].reshape(NSEQ, 128, 256)),
            "cache_v_s": np.ascontiguousarray(cvc[sq].reshape(NSEQ, 128, 256)),
        })
        in_maps.append(m)
    res = run_bass_kernel_spmd(nc, in_maps, core_ids=list(range(NCORES)))
    return res.results


def kernel(**inputs):
    NB = 32
    r = run(inputs, NB)
    B, SEQ = 4, 8192
    y_prompt = np.zeros((B, SEQ, D), np.float32)
    st_p = np.zeros((1, B, 4, 128, 512), np.float32)
    ck_p = np.zeros((B, 128, 4, 64), np.float32)
    cv_p = np.zeros((B, 128, 4, 64), np.float32)
    y_sample = np.zeros((128, TS, D), np.float32)
    st_s = np.zeros((1, 128, 4, 128, 512), np.float32)
    ck_s = np.zeros((128, 128, 4, 64), np.float32)
    cv_s = np.zeros((128, 128, 4, 64), np.float32)
    for core in range(NCORES):
        b, hf = core // 2, core % 2
        y_prompt[b, hf * 4096:(hf + 1) * 4096] = r[core]["y_main"]
        if hf == 1:
            st_p[0, b] = r[core]["st_p"]
            ck_p[b] = r[core]["ck_p"].reshape(128, 4, 64)
            cv_p[b] = r[core]["cv_p"].reshape(128, 4, 64)
        sq = slice(core * NSEQ, (core + 1) * NSEQ)
        y_sample[sq] = r[core]["y_s"].reshape(NSEQ, TS, D)
        st_s[0, sq] = r[core]["st_s"]
        ck_s[sq] = r[core]["ck_s"].reshape(NSEQ, 128, 4, 64)
        cv_s[sq] = r[core]["cv_s"].reshape(NSEQ, 128, 4, 64)
    return (y_prompt, y_sample, st_p, st_s, ck_p, cv_p, ck_s, cv_s)
```

```python
import contextlib
import numpy as np
import ml_dtypes
import concourse.bass as bass
import concourse.mybir as mybir
from concourse.bass_utils import run_bass_kernel_spmd

F32 = mybir.dt.float32
BF16 = mybir.dt.bfloat16
AF = mybir.ActivationFunctionType
ALU = mybir.AluOpType
AX = mybir.AxisListType

D = 1024
NCORES = 8
ENGS = ("pe", "act", "dve", "pool", "sp")
PIPELINE = True
SAME_ENGINE_INORDER = ()


class Buf:
    __slots__ = ("name", "last_w", "readers")

    def __init__(self, name):
        self.name = name
        self.last_w = None
        self.readers = []


class Ins:
    __slots__ = ("eng", "fn", "seq", "deps", "flag", "ms", "dma_slot", "dma_cnt", "epoch")

    def __init__(self, eng, fn, seq, epoch):
        self.eng = eng
        self.fn = fn
        self.seq = seq
        self.deps = []
        self.flag = False
        self.ms = None
        self.dma_slot = None
        self.dma_cnt = 0
        self.epoch = epoch


class DmaSlot:
    def __init__(self, name):
        self.name = name
        self.count = 0
        self.sem = None
        self.last = None


class Prog:
    def __init__(self, nc):
        self.nc = nc
        self.streams = {e: [] for e in ENGS}
        self.slots = []
        self.epoch = 0
        self.waited = {}

    def new_epoch(self):
        self.epoch += 1

    def slot(self, name):
        s = DmaSlot(name)
        self.slots.append(s)
        return s

    def _add_dep(self, ins, dep):
        if dep is None or dep is ins:
            return
        if dep.dma_slot is not None:
            key = (ins.eng, "dma", id(dep.dma_slot))
            val = dep.dma_cnt
        else:
            if dep.eng == ins.eng and (dep.eng == "pe" or dep.eng in SAME_ENGINE_INORDER):
                return
            key = (ins.eng, "eng", dep.eng)
            val = dep.seq
        if self.waited.get(key, -1) >= val:
            return
        self.waited[key] = val
        ins.deps.append(dep)
        if dep.dma_slot is None:
            dep.flag = True

    def emit(self, eng, fn, reads=(), writes=(), dma=None):
        st = self.streams[eng]
        ins = Ins(eng, fn, len(st), self.epoch)
        if dma is not None:
            dma.count += 1
            ins.dma_slot = dma
            ins.dma_cnt = dma.count
            dma.last = ins
        for b in reads:
            self._add_dep(ins, b.last_w)
        for b in writes:
            self._add_dep(ins, b.last_w)
            for r in b.readers:
                self._add_dep(ins, r)
        for b in reads:
            b.readers.append(ins)
        for b in writes:
            b.last_w = ins
            b.readers = []
        st.append(ins)
        return ins

    def barrier(self):
        lasts = []
        for e in ENGS:
            for ins in reversed(self.streams[e]):
                if ins.fn is not None:
                    if ins.dma_slot is None:
                        lasts.append(ins)
                    break
        dmas = [s.last for s in self.slots if s.last is not None]
        for e in ENGS:
            ins = self.emit(e, None)
            for d in lasts + dmas:
                if d.eng != e or d.dma_slot is not None:
                    self._add_dep(ins, d)

    def build(self, final_waits=()):
        nc = self.nc
        fin = self.emit("sp", None)
        for d in final_waits:
            if d.dma_slot is None or d.dma_slot.last is d:
                self._add_dep(fin, d)
        n_epochs = self.epoch + 1
        for e in ENGS:
            cnt = {}
            for ins in self.streams[e]:
                if ins.flag:
                    cnt[ins.epoch] = cnt.get(ins.epoch, 0) + 1
                    ins.ms = cnt[ins.epoch]
        with contextlib.ExitStack() as es:
            esem = {}
            for e in ENGS:
                for ep in range(n_epochs):
                    if any(i.flag and i.epoch == ep for i in self.streams[e]):
                        esem[(e, ep)] = es.enter_context(nc.semaphore(f"p_{e}_{ep}"))
            for s in self.slots:
                if s.count > 0:
                    s.sem = es.enter_context(nc.semaphore(f"d_{s.name}"))
            block = es.enter_context(nc.Block())

            def replay(e, eng):
                for ins in self.streams[e]:
                    for d in ins.deps:
                        if d.dma_slot is not None:
                            eng.wait_ge(d.dma_slot.sem, 16 * d.dma_cnt)
                        else:
                            eng.wait_ge(esem[(d.eng, d.epoch)], d.ms)
                    if ins.fn is None:
                        continue
                    bi = ins.fn(eng)
                    if ins.dma_slot is not None:
                        bi.then_inc(ins.dma_slot.sem, 16)
                    elif ins.flag:
                        bi.then_inc(esem[(e, ins.epoch)], 1)

            @block.tensor
            def _(eng):
                replay("pe", eng)

            @block.scalar
            def _(eng):
                replay("act", eng)

            @block.vector
            def _(eng):
                replay("dve", eng)

            @block.gpsimd
            def _(eng):
                replay("pool", eng)

            @block.sync
            def _(eng):
                replay("sp", eng)
        return nc


class T:
    __slots__ = ("ap", "b")

    def __init__(self, ap, b):
        self.ap = ap
        self.b = b

    def __getitem__(self, k):
        return self.ap[k]


ARENA_WORDS = 53200


class Ctx:
    def __init__(self, nc):
        self.nc = nc
        self.P = Prog(nc)
        self.arena = nc.alloc_sbuf_tensor("arena", [128, ARENA_WORDS], F32)
        self.off = 0
        self.psum = nc.alloc_psum_tensor("ps", [128, 8, 512], F32)
        self.pb = [Buf(f"bank{i}") for i in range(8)]
        self.rr = 0
        self.hold = set()

    def sb(self, name, free, dtype):
        n = int(np.prod(free))
        words = n if dtype == F32 else (n + 1) // 2
        words = (words + 7) // 8 * 8
        assert self.off + words <= ARENA_WORDS, f"arena overflow at {name}: {self.off + words}"
        ap = self.arena[:, self.off:self.off + words]
        self.off += words
        if dtype != F32:
            ap = ap.bitcast(dtype)
        ap = ap[:, 0:n]
        if len(free) == 2:
            ap = ap.rearrange("p (a b) -> p a b", a=free[0])
        elif len(free) == 3:
            ap = ap.rearrange("p (a b c) -> p a b c", a=free[0], b=free[1])
        elif len(free) == 4:
            ap = ap.rearrange("p (a b c d) -> p a b c d", a=free[0], b=free[1], c=free[2])
        return T(ap, Buf(name))

    def bank(self):
        for _ in range(6):
            i = 2 + self.rr % 6
            self.rr += 1
            if i not in self.hold:
                self.hold.add(i)
                return i
        raise RuntimeError("all PSUM banks held")

    def free(self, i):
        self.hold.discard(i)

    def bk(self, i):
        return self.psum[:, i, :]

    def bkbf(self, i):
        return self.psum[:, i, :].bitcast(BF16)

    def mm(self, out, lhsT, rhs, start, stop, R, W):
        self.P.emit("pe", lambda e: e.matmul(out, lhsT=lhsT, rhs=rhs, start=start, stop=stop), R, W)

    def tr(self, out, in_, ident, R, W):
        self.P.emit("pe", lambda e: e.transpose(out=out, in_=in_, identity=ident), R, W)

    def act(self, out, in_, func, R, W, scale=1.0, bias=0.0, accum=None):
        if accum is None:
            self.P.emit("act", lambda e: e.activation(out=out, in_=in_, func=func, scale=scale, bias=bias), R, W)
        else:
            self.P.emit("act", lambda e: e.activation(out=out, in_=in_, func=func, scale=scale, bias=bias, accum_out=accum), R, W)

    def ts(self, eng, out, in0, s1, op0, R, W, s2=None, op1=None):
        if op1 is None:
            self.P.emit(eng, lambda e: e.tensor_scalar(out=out, in0=in0, scalar1=s1, scalar2=None, op0=op0), R, W)
        else:
            self.P.emit(eng, lambda e: e.tensor_scalar(out=out, in0=in0, scalar1=s1, scalar2=s2, op0=op0, op1=op1), R, W)

    def tt(self, eng, out, in0, in1, op, R, W):
        self.P.emit(eng, lambda e: e.tensor_tensor(out=out, in0=in0, in1=in1, op=op), R, W)

    def stt(self, out, in0, scalar, in1, op0, op1, R, W):
        self.P.emit("dve", lambda e: e.scalar_tensor_tensor(out=out, in0=in0, scalar=scalar, in1=in1, op0=op0, op1=op1), R, W)

    def cp(self, eng, out, in_, R, W):
        if eng == "act":
            self.P.emit("act", lambda e: e.copy(out=out, in_=in_), R, W)
        else:
            self.P.emit(eng, lambda e: e.tensor_copy(out=out, in_=in_), R, W)

    def dma(self, out, in_, R, W, slot, eng="sp"):
        return self.P.emit(eng, lambda e: e.dma_start(out=out, in_=in_), R, W, dma=slot)

    def memset(self, eng, ap, val, W):
        self.P.emit(eng, lambda e: e.memset(ap, val), (), W)


def bc(ap, shape):
    return ap.to_broadcast(shape)


def load_weight_gen(c, dram, dst, nk, c_lo, c_hi, gain, gain_mod, stg, stg_slots, cnt):
    CH = 1024
    ns = len(stg)
    for k in range(nk):
        for c0 in range(c_lo, c_hi, CH):
            n = min(CH, c_hi - c0)
            i = cnt[0] % ns
            cnt[0] += 1
            c.dma(stg[i][:, 0:n], dram[k * 128:(k + 1) * 128, c0:c0 + n], [], [stg[i].b], stg_slots[i])
            eng = "act" if (cnt[0] % 2 == 0) else "dve"
            o = dst[:, k, c0:c0 + n]
            if gain is None:
                c.cp(eng, o, stg[i][:, 0:n], [stg[i].b], [dst.b])
            else:
                g = gain[:, (k % gain_mod):(k % gain_mod) + 1]
                if eng == "act":
                    c.act(o, stg[i][:, 0:n], AF.Copy, [stg[i].b, gain.b], [dst.b], scale=g)
                else:
                    c.ts("dve", o, stg[i][:, 0:n], g, ALU.mult, [stg[i].b, gain.b], [dst.b])
            yield


def load_weight(c, dram, dst, nk, ncols, gain, gain_mod, stg, stg_slots, cnt):
    for _ in load_weight_gen(c, dram, dst, nk, 0, ncols, gain, gain_mod, stg, stg_slots, cnt):
        pass


def rms_stats(c, xt, junk, ss, rr, xn, dcols=1024):
    c.act(junk[:, 0:dcols], xt[:, :], AF.Square, [xt.b], [junk.b, ss.b], accum=ss[:, 0:1])
    c.act(rr[:, 0:1], ss[:, 0:1], AF.Ln, [ss.b], [rr.b], scale=1.0 / dcols, bias=1e-6)
    c.act(rr[:, 0:1], rr[:, 0:1], AF.Exp, [rr.b], [rr.b], scale=-0.5)
    c.ts("dve", xn[:, :], xt[:, :], rr[:, 0:1], ALU.mult, [xt.b, rr.b], [xn.b])


def rms_tr(c, xn, uT, ident, tbank):
    pt = c.bkbf(tbank).rearrange("p (k t) -> p k t", k=8)
    for k in range(8):
        c.tr(pt[:, k, :], xn[:, k * 128:(k + 1) * 128], ident[:, :], [xn.b, ident.b], [c.pb[tbank]])
    c.cp("dve", uT[:, :, :], pt, [c.pb[tbank]], [uT.b])


def rms_prep(c, xt, junk, ss, rr, xn, uT, ident, tbank, dcols=1024):
    rms_stats(c, xt, junk, ss, rr, xn, dcols)
    rms_tr(c, xn, uT, ident, tbank)


def proj(c, uT, W, c0, n, bank):
    for k in range(8):
        c.mm(c.bk(bank)[:, 0:n], uT[:, k, :], W[:, k, c0:c0 + n], k == 0, k == 7, [uT.b, W.b], [c.pb[bank]])


def sigmoid_act(c, sig, gbank):
    c.act(sig[:, :], c.bk(gbank), AF.Exp, [c.pb[gbank]], [sig.b], scale=-1.0)
    c.act(sig[:, :], sig[:, :], AF.Ln, [sig.b], [sig.b], bias=1.0)
    c.act(sig[:, :], sig[:, :], AF.Exp, [sig.b], [sig.b], scale=-1.0)


def out_proj(c, og, ogT, Wout, xt, ho, ident):
    pt = c.psum[:, 0:2, :].bitcast(BF16).rearrange("p b (k t) -> p (b k) t", k=8)
    for k in range(16):
        c.tr(pt[:, k, :], og[:, k * 128:(k + 1) * 128], ident[:, :], [og.b, ident.b], [c.pb[0], c.pb[1]])
    c.cp("dve", ogT[:, 0:8, :], pt[:, 0:8, :], [c.pb[0]], [ogT.b])
    c.cp("act", ogT[:, 8:16, :], pt[:, 8:16, :], [c.pb[1]], [ogT.b])
    yield
    for half in range(2):
        bk = c.bank()
        for k in range(16):
            c.mm(c.bk(bk), ogT[:, k, :], Wout[:, k, half * 512:(half + 1) * 512], k == 0, k == 15,
                 [ogT.b, Wout.b], [c.pb[bk]])
            if k == 7:
                yield
        c.tt("dve", ho[:, half * 512:(half + 1) * 512], c.bk(bk), xt[:, half * 512:(half + 1) * 512], ALU.add,
             [c.pb[bk], xt.b], [ho.b])
        c.free(bk)
        yield


class NS:
    pass


NSEQ = 16
TS = 4


def build_program(NB):
    NPRE = NB - 1
    NMAIN = NB + 1
    nc = bass.Bass("TRN2", target_bir_lowering=False)

    def din(name, shape, dt=F32):
        return nc.dram_tensor(name, list(shape), dt, kind="ExternalInput").ap()

    def dout(name, shape, dt=F32):
        return nc.dram_tensor(name, list(shape), dt, kind="ExternalOutput").ap()

    x_pre = din("x_pre", [max(NPRE, 1) * 128, D])
    x_main = din("x_main", [NMAIN * 128, D])
    a_w_in = din("a_w_in", [D, 5136])
    a_w_out = din("a_w_out", [2048, D])
    wkv = din("wkv", [D, 512])
    b_w_in = din("b_w_in", [D, 4096])
    b_w_out = din("b_w_out", [2048, D])
    wg2a_d = din("wg2a", [17, 512])
    g_a = din("g_a", [128, 8])
    g_ao = din("g_ao", [128, 4])
    g_kv = din("g_kv", [128, 8])
    g_b = din("g_b", [128, 8])
    gk_d = din("gk", [128, 64])
    gq_d = din("gq", [128, 64])
    sinks_d = din("sinks", [128, 32])
    ident_d = din("ident", [128, 128], BF16)
    U_d = din("U", [128, 128], BF16)
    L_d = din("L", [128, 128], BF16)
    LU_d = din("LU", [128, 256], BF16)
    U32_d = din("U32", [128, 128])
    U432_d = din("U432", [128, 128])
    ones32_d = din("ones32", [128, 8])
    U4_d = din("U4", [128, 128], BF16)
    ones_d = din("ones", [128, 128], BF16)
    seqind_d = din("seqind", [128, NSEQ], BF16)
    seqindf_d = din("seqindf", [128, NSEQ])
    CM_d = din("CM", [128, NSEQ * 128], BF16)
    CMK_d = din("CMK", [128, TS], BF16)
    cos_d = din("cos", [128, NMAIN, 32])
    sin_d = din("sin", [128, NMAIN, 32])
    coss_d = din("cos_s", [128, 1, 32])
    sins_d = din("sin_s", [128, 1, 32])
    valid_d = din("valid", [128, 1])
    x_s = din("x_s", [128, D])
    state_s = din("state_s", [NSEQ, 4, 128, 512])
    cache_k_s = din("cache_k_s", [NSEQ, 128, 256])
    cache_v_s = din("cache_v_s", [NSEQ, 128, 256])

    y_main = dout("y_main", [NB * 128, D])
    st_p = dout("st_p", [4, 128, 512])
    ck_p = dout("ck_p", [128, 256])
    cv_p = dout("cv_p", [128, 256])
    y_s = dout("y_s", [NSEQ * TS, D])
    st_s = dout("st_s", [NSEQ, 4, 128, 512])
    ck_s = dout("ck_s", [NSEQ, 128, 256])
    cv_s = dout("cv_s", [NSEQ, 128, 256])
    h1s = nc.dram_tensor("h1s", [NMAIN * 128, D], F32, kind="Internal").ap()

    c = Ctx(nc)
    P = c.P
    finals = []

    for q4 in range(4):
        sl = slice(q4 * 4, (q4 + 1) * 4)
        finals.append(c.dma(ck_s[sl, 0:128 - TS, :], cache_k_s[sl, TS:128, :], [], [], P.slot(f"shk{q4}")))
        finals.append(c.dma(cv_s[sl, 0:128 - TS, :], cache_v_s[sl, TS:128, :], [], [], P.slot(f"shv{q4}")))

    ident = c.sb("ident", [128], BF16)
    U = c.sb("U", [128], BF16)
    L = c.sb("L", [128], BF16)
    LU = c.sb("LU", [2, 128], BF16)
    U32 = c.sb("U32", [128], F32)
    U432 = c.sb("U432", [128], F32)
    ones32 = c.sb("ones32", [8], F32)
    U4 = c.sb("U4", [128], BF16)
    ones = c.sb("ones", [128], BF16)
    seqind = c.sb("seqind", [NSEQ], BF16)
    seqindf = c.sb("seqindf", [NSEQ], F32)
    CMK = c.sb("CMK", [TS], BF16)
    hs = c.sb("hs", [1024], F32)
    for t, d in ((ident, ident_d), (U, U_d), (L, L_d), (U4, U4_d), (ones, ones_d), (seqind, seqind_d),
                 (seqindf, seqindf_d), (CMK, CMK_d), (U32, U32_d), (U432, U432_d), (ones32, ones32_d)):
        c.dma(t[:, :], d, [], [t.b], P.slot("c_" + t.b.name))
    c.dma(LU[:, :, :], LU_d.rearrange("p (a b) -> p a b", a=2), [], [LU.b], P.slot("c_LU"))
    persist_mark = c.off


    def alloc_common(t, tag, nxt=3, nho=2, nsig=2):
        t.xt = [c.sb(f"xt{tag}{i}", [1024], F32) for i in range(nxt)]
        t.xt_slots = [P.slot(f"xt{tag}{i}") for i in range(nxt)]
        t.junk = c.sb("junk" + tag, [1024], BF16)
        t.ss1 = c.sb("ss1" + tag, [1], F32)
        t.rr1 = c.sb("rr1" + tag, [1], F32)
        t.xn = c.sb("xn" + tag, [1024], BF16)
        t.sig = [c.sb(f"sig{tag}{i}", [512], F32) for i in range(nsig)]
        t.og = c.sb("og" + tag, [2048], BF16)
        t.ogT = c.sb("ogT" + tag, [16, 128], BF16)
        t.ho = [c.sb(f"ho{tag}{i}", [1024], F32) for i in range(nho)]
        t.ho_slots = [P.slot(f"ho{tag}{i}") for i in range(nho)]

    def drive(gens, prefetch=None, prerms=None, rms_round=4, ratios=None, extra=None, extra_deadline=0, extra_from=1):
        if not PIPELINE:
            if extra is not None:
                for _ in extra:
                    pass
            for k, g in enumerate(gens):
                if prefetch is not None:
                    prefetch(k)
                    prerms(k)
                for _ in g:
                    pass
            return
        prev = None
        if prefetch is not None:
            prefetch(0)
            prerms(0)
        for k, g in enumerate(gens):
            nxt = prefetch is not None and k + 1 < len(gens)
            if nxt:
                prefetch(k + 1)
            fdone = False
            bdone = prev is None
            rounds = 0
            if extra is not None and k >= extra_deadline:
                for _ in extra:
                    pass
                extra = None
            while not (fdone and bdone):
                rounds += 1
                if extra is not None and k >= extra_from:
                    try:
                        next(extra)
                        next(extra)
                    except StopIteration:
                        extra = None
                if rounds == rms_round and nxt:
                    prerms(k + 1)
                    nxt = False
                if not bdone:
                    try:
                        for _ in range(ratios[k] if ratios is not None else 2):
                            next(prev)
                    except StopIteration:
                        bdone = True
                if not fdone:
                    try:
                        if next(g) == "BACK":
                            fdone = True
                    except StopIteration:
                        fdone = True
            if nxt:
                prerms(k + 1)
            prev = g
        if prev is not None:
            for _ in prev:
                pass

    Wa_in = c.sb("Wa_in", [8, 5136], BF16)
    Wa_out = c.sb("Wa_out", [16, 1024], BF16)
    wg2a = c.sb("wg2a", [512], BF16)
    ga = c.sb("ga", [8], F32)
    gao = c.sb("gao", [4], F32)
    markA = c.off

    NSB = 6

    def allocA(tag, sample):
        t = NS()
        nb = 1 if sample else 2
        alloc_common(t, tag, nxt=(1 if sample else 3))
        t.uT = [c.sb(f"uT{tag}{i}", [8, 128], BF16) for i in range(nb)]
        t.junk2 = c.sb("junk2" + tag, [512], BF16)
        t.glT = c.sb("glT" + tag, [128], BF16)
        t.lz = c.sb("lz" + tag, [512], F32)
        t.eb = c.sb("eb" + tag, [512], F32)
        t.enb = c.sb("enb" + tag, [512], F32)
        t.qt = c.sb("qt" + tag, [512], BF16)
        t.kt = [c.sb(f"kt{tag}{i}", [512], BF16) for i in range(nb)]
        t.qkT = [c.sb(f"qkT{tag}{i}", [8, 128], BF16) for i in range(nb)]
        t.ATm = [c.sb(f"ATm{tag}{i}", [4, 128], BF16) for i in range(nb)]
        t.vv = [c.sb(f"v{tag}{h}", [512], BF16) for h in range(4)]
        t.sso = c.sb("sso" + tag, [4], F32)
        t.ro = c.sb("ro" + tag, [4], F32)
        t.sgo = [c.sb(f"sgo{tag}{i}", [512], F32) for i in range(2)]
        if not sample:
            t.ebl = [c.sb(f"ebl{tag}{i}", [4], F32) for i in range(nb)]
            t.S = [c.sb(f"S{h}", [512], F32) for h in range(4)]
            t.Sbf = [c.sb(f"Sbf{h}", [512], BF16) for h in range(4)]
        else:
            t.ebl = [c.sb("ebls" + tag, [4, NSEQ], F32)]
            t.CM = c.sb("CM", [NSEQ, 128], BF16)
            t.qms = [c.sb(f"qms{i}", [NSEQ, 128], BF16) for i in range(2)]
            t.km = [c.sb(f"km{i}", [128], BF16) for i in range(NSB)]
            t.Ss = [c.sb(f"Ss{i}", [512], F32) for i in range(NSB)]
            t.Ss_in = [P.slot(f"Ss_in{i}") for i in range(NSB)]
            t.Ss_out = [P.slot(f"Ss_out{i}") for i in range(NSB)]
            t.Ssbf = [c.sb(f"Ssbf{i}", [512], BF16) for i in range(NSB)]
            t.next_load = 0
        return t

    tA = allocA("A", False)
    wg2f = c.sb("wg2f", [512], F32)
    print("phase A arena words", c.off)

    c.dma(ga[:, :], g_a, [], [ga.b], P.slot("ga"))
    c.dma(gao[:, :], g_ao, [], [gao.b], P.slot("gao"))
    c.dma(wg2f[0:17, :], wg2a_d, [], [wg2f.b], P.slot("wg2"))
    c.cp("dve", wg2a[0:17, :], wg2f[0:17, :], [wg2f.b], [wg2a.b])
    c.memset("pool", tA.glT[0:17, :], 1.0, [tA.glT.b])
    for h in range(4):
        c.memset("pool", tA.S[h][:, :], 0.0, [tA.S[h].b])
        c.memset("pool", tA.Sbf[h][:, :], 0.0, [tA.Sbf[h].b])
    cnt = [0]
    stgA = tA.ho + tA.xt
    stgA_slots = [P.slot(f"stgA{i}") for i in range(len(stgA))]
    for _ in load_weight_gen(c, a_w_in, Wa_in, 8, 512, 3072, ga, 8, stgA, stgA_slots, cnt):
        pass
    for _ in load_weight_gen(c, a_w_in, Wa_in, 8, 5120, 5136, ga, 8, stgA, stgA_slots, cnt):
        pass

    def late_weights():
        cnt2 = [0]
        yield from load_weight_gen(c, a_w_in, Wa_in, 8, 0, 512, ga, 8, tA.ho, stgA_slots[0:2], cnt2)
        yield from load_weight_gen(c, a_w_in, Wa_in, 8, 3072, 5120, ga, 8, tA.ho, stgA_slots[0:2], cnt2)
        yield from load_weight_gen(c, a_w_out, Wa_out, 16, 0, 1024, gao, 4, tA.ho, stgA_slots[0:2], cnt2)

    h1blk = [Buf(f"h1blk{j}") for j in range(NMAIN)]

    def l0_block(t, src, row0, mode, i, dst_rows):
        full = mode != "pre"
        sample = mode == "sample"
        CU = U4 if sample else U
        xb = t.xt[i % len(t.xt)]
        uT = t.uT[i % len(t.uT)]
        ebl = t.ebl[i % len(t.ebl)]
        kt = t.kt[i % len(t.kt)]
        qkT = t.qkT[i % len(t.qkT)]
        ATm = t.ATm[i % len(t.ATm)]
        lz, eb, enb, qt, vv = t.lz, t.eb, t.enb, t.qt, t.vv
        if sample:
            c.dma(xb[:, :], src[row0:row0 + 128, :], [], [xb.b], t.xt_slots[0])
            rms_stats(c, xb, t.junk, t.ss1, t.rr1, t.xn)
        rms_tr(c, t.xn, uT, ident, 0)
        yield
        bk = c.bank()
        for k in range(8):
            c.mm(c.bk(bk)[0:16, 0:128], Wa_in[:, k, 5120:5136], uT[:, k, :], k == 0, k == 7, [uT.b, Wa_in.b], [c.pb[bk]])
        c.cp("act", t.glT[0:16, :], c.bk(bk)[0:16, 0:128], [c.pb[bk]], [t.glT.b])
        c.free(bk)
        if full:
            bq = c.bank()
            proj(c, uT, Wa_in, 0, 512, bq)
        yield
        bkk = c.bank()
        proj(c, uT, Wa_in, 512, 512, bkk)
        yield
        bz = c.bank()
        c.mm(c.bk(bz), t.glT[0:17, :], wg2a[0:17, :], True, True, [t.glT.b, wg2a.b], [c.pb[bz]])
        c.act(lz[:, :], c.bk(bz), AF.Exp, [c.pb[bz]], [lz.b], scale=-1.0)
        c.free(bz)
        c.act(lz[:, :], lz[:, :], AF.Ln, [lz.b], [lz.b], bias=1.0)
        yield
        CU32 = U432 if sample else U32
        bb = c.bank()
        c.mm(c.bk(bb), CU32[:, :], lz[:, :], True, True, [CU32.b, lz.b], [c.pb[bb]])
        if full:
            c.act(eb[:, :], c.bk(bb), AF.Exp, [c.pb[bb]], [eb.b], scale=-1.0 / 16)
        c.act(enb[:, :], c.bk(bb), AF.Exp, [c.pb[bb]], [enb.b], scale=1.0 / 16)
        c.free(bb)
        yield
        bl = c.bank()
        if not sample:
            for h in range(4):
                c.mm(c.bk(bl)[:, h:h + 1], lz[:, h * 128:(h + 1) * 128], ones32[:, 0:1], True, True, [lz.b, ones32.b], [c.pb[bl]])
            c.act(ebl[:, :], c.bk(bl)[:, 0:4], AF.Exp, [c.pb[bl]], [ebl.b], scale=-1.0 / 16)
        else:
            for h in range(4):
                o_ = c.bk(bl)[:, h * NSEQ:(h + 1) * NSEQ]
                c.mm(o_, lz[:, h * 128:(h + 1) * 128], seqindf[:, :], True, True, [lz.b, seqindf.b], [c.pb[bl]])
            c.act(ebl[:, :, :], c.bk(bl)[:, 0:4 * NSEQ].rearrange("p (h s) -> p h s", h=4), AF.Exp, [c.pb[bl]], [ebl.b],
                  scale=-1.0 / 16)
        c.free(bl)
        if full:
            c.stt(qt[:, :], c.bk(bq), 128.0 ** -0.5, eb[:, :], ALU.mult, ALU.mult, [c.pb[bq], eb.b], [qt.b])
            c.free(bq)
        c.tt("dve", kt[:, :], c.bk(bkk), enb[:, :], ALU.mult, [c.pb[bkk], enb.b], [kt.b])
        c.free(bkk)
        yield
        if full:
            yield
            pt = c.bkbf(1).rearrange("p (k t) -> p k t", k=8)
            for h in range(4):
                c.tr(pt[:, h, :], qt[:, h * 128:(h + 1) * 128], ident[:, :], [qt.b, ident.b], [c.pb[1]])
            for h in range(4):
                c.tr(pt[:, 4 + h, :], kt[:, h * 128:(h + 1) * 128], ident[:, :], [kt.b, ident.b], [c.pb[1]])
            c.cp("act", qkT[:, :, :], pt, [c.pb[1]], [qkT.b])
            yield
            ba = c.bank()
            for h in range(4):
                c.mm(c.bk(ba)[:, h * 128:(h + 1) * 128], qkT[:, 4 + h, :], qkT[:, h, :], True, True, [qkT.b], [c.pb[ba]])
            c.tt("dve", ATm[:, :, :], c.bk(ba).rearrange("p (h t) -> p h t", h=4), bc(CU[:, None, :], [128, 4, 128]),
                 ALU.mult, [c.pb[ba], CU.b], [ATm.b])
            c.free(ba)
        yield "BACK"

        def v_proj(h):
            bv = c.bank()
            proj(c, uT, Wa_in, 1024 + h * 512, 512, bv)
            c.cp("act", vv[h][:, :], c.bk(bv), [c.pb[bv]], [vv[h].b])
            c.free(bv)

        v_proj(0)
        yield
        v_proj(1)
        yield
        for h in range(4):
            bo = None
            if not sample:
                S, Sbf = t.S, t.Sbf
                if full:
                    bo = c.bank()
                    c.mm(c.bk(bo), ATm[:, h, :], vv[h][:, :], True, False, [ATm.b, vv[h].b], [c.pb[bo]])
                    c.mm(c.bk(bo), qkT[:, h, :], Sbf[h][:, :], False, True, [qkT.b, Sbf[h].b], [c.pb[bo]])
                bd = c.bank()
                c.mm(c.bk(bd), kt[:, h * 128:(h + 1) * 128], vv[h][:, :], True, True, [kt.b, vv[h].b], [c.pb[bd]])
                if full:
                    bg = c.bank()
                    proj(c, uT, Wa_in, 3072 + h * 512, 512, bg)
                yield
                c.ts("dve", S[h][:, :], S[h][:, :], ebl[:, h:h + 1], ALU.mult, [S[h].b, ebl.b], [S[h].b])
                c.stt(S[h][:, :], c.bk(bd), ebl[:, h:h + 1], S[h][:, :], ALU.mult, ALU.add, [c.pb[bd], ebl.b, S[h].b], [S[h].b])
                c.free(bd)
                c.cp("pool", Sbf[h][:, :], S[h][:, :], [S[h].b], [Sbf[h].b])
            else:
                qms = t.qms[h % 2]
                c.tt("pool", qms[:, :, :], bc(qkT[:, h:h + 1, :], [128, NSEQ, 128]), t.CM[:, :, :], ALU.mult,
                     [qkT.b, t.CM.b], [qms.b])
                bo = c.bank()
                c.mm(c.bk(bo), ATm[:, h, :], vv[h][:, :], True, False, [ATm.b, vv[h].b], [c.pb[bo]])
                for s in range(NSEQ):
                    idx = h * NSEQ + s
                    Sb, Sbb, km = t.Ss[idx % NSB], t.Ssbf[idx % NSB], t.km[idx % NSB]
                    while t.next_load <= min(idx + NSB - 1, 4 * NSEQ - 1):
                        li = t.next_load
                        c.dma(t.Ss[li % NSB][:, :], state_s[li % NSEQ, li // NSEQ], [], [t.Ss[li % NSB].b], t.Ss_in[li % NSB])
                        t.next_load += 1
                    c.cp("act", Sbb[:, :], Sb[:, :], [Sb.b], [Sbb.b])
                    c.mm(c.bk(bo), qms[:, s, :], Sbb[:, :], False, s == NSEQ - 1, [qms.b, Sbb.b], [c.pb[bo]])
                    c.ts("dve", km[:, :], kt[:, h * 128:(h + 1) * 128], seqindf[:, s:s + 1], ALU.mult, [kt.b, seqindf.b], [km.b])
                    bd = c.bank()
                    c.mm(c.bk(bd), km[:, :], vv[h][:, :], True, True, [km.b, vv[h].b], [c.pb[bd]])
                    c.ts("dve", Sb[:, :], Sb[:, :], ebl[:, h, s:s + 1], ALU.mult, [Sb.b, ebl.b], [Sb.b])
                    c.stt(Sb[:, :], c.bk(bd), ebl[:, h, s:s + 1], Sb[:, :], ALU.mult, ALU.add, [c.pb[bd], ebl.b, Sb.b], [Sb.b])
                    c.free(bd)
                    finals.append(c.dma(st_s[s, h], Sb[:, :], [Sb.b], [], t.Ss_out[idx % NSB], eng="pool"))
                bg = c.bank()
                proj(c, uT, Wa_in, 3072 + h * 512, 512, bg)
            if full:
                c.act(t.junk2[:, 0:512], c.bk(bo), AF.Square, [c.pb[bo]], [t.junk2.b, t.sso.b], accum=t.sso[:, h:h + 1])
                c.act(t.ro[:, h:h + 1], t.sso[:, h:h + 1], AF.Ln, [t.sso.b], [t.ro.b], scale=1.0 / 512, bias=1e-6)
                c.act(t.ro[:, h:h + 1], t.ro[:, h:h + 1], AF.Exp, [t.ro.b], [t.ro.b], scale=-0.5)
                sg = t.sig[h % 2]
                so = t.sgo[h % 2]
                sigmoid_act(c, sg, bg)
                yield
                c.stt(so[:, :], c.bk(bo), t.ro[:, h:h + 1], sg[:, :], ALU.mult, ALU.mult, [c.pb[bo], t.ro.b, sg.b], [so.b])
                c.tt("dve", t.og[:, h * 512:(h + 1) * 512], c.bk(bg), so[:, :], ALU.mult, [c.pb[bg], so.b], [t.og.b])
                c.free(bg)
                c.free(bo)
            if h + 2 < 4:
                v_proj(h + 2)
            yield
        if full and not sample:
            hb = t.ho[i % 2]
            yield
            yield from out_proj(c, t.og, t.ogT, Wa_out, xb, hb, ident)
            c.dma(h1s[dst_rows:dst_rows + 128, :], hb[:, :], [hb.b], [h1blk[dst_rows // 128]], t.ho_slots[i % 2])
        elif sample:
            yield from out_proj(c, t.og, t.ogT, Wa_out, xb, hs, ident)

    gens = []
    srcsA = []
    bi = 0
    for j in range(NPRE):
        gens.append(l0_block(tA, x_pre, j * 128, "pre", bi, None))
        srcsA.append(x_pre[j * 128:(j + 1) * 128, :])
        bi += 1
    for j in range(NMAIN):
        gens.append(l0_block(tA, x_main, j * 128, "full", bi, j * 128))
        srcsA.append(x_main[j * 128:(j + 1) * 128, :])
        bi += 1

    def prefetchA(k):
        xb_ = tA.xt[k % 3]
        c.dma(xb_[:, :], srcsA[k], [], [xb_.b], tA.xt_slots[k % 3])

    def prermsA(k):
        rms_stats(c, tA.xt[k % 3], tA.junk, tA.ss1, tA.rr1, tA.xn)

    ratiosA = [1 if (k - 1) < NPRE else 2 for k in range(len(gens))]
    drive(gens, prefetchA, prermsA, rms_round=2, ratios=ratiosA, extra=late_weights(), extra_deadline=max(NPRE - 1, 0))
    stp_slot = P.slot("st_p")
    for h in range(4):
        finals.append(c.dma(st_p[h], tA.S[h][:, :], [tA.S[h].b], [], stp_slot))

    P.barrier()
    assert not c.hold, c.hold
    c.off = markA
    tAs = allocA("As", True)
    print("phase A(sample) arena words", c.off)
    c.dma(tAs.CM[:, :, :], CM_d.rearrange("p (s t) -> p s t", s=NSEQ), [], [tAs.CM.b], P.slot("c_CM"))
    c.memset("pool", tAs.glT[0:17, :], 1.0, [tAs.glT.b])
    drive([l0_block(tAs, x_s, 0, "sample", 0, None)])

    import os
    if os.environ.get("STOP_AFTER") == "A0":
        P.build(final_waits=finals)
        return nc
    if os.environ.get("STOP_AFTER") == "A":
        P.barrier()
        P.build(final_waits=finals)
        return nc
    P.barrier()
    assert not c.hold, c.hold
    P.new_epoch()
    c.off = persist_mark
    Wb_in = c.sb("Wb_in", [8, 4096], BF16)
    Wb_out = c.sb("Wb_out", [16, 1024], BF16)
    Wkv = c.sb("Wkv", [8, 512], BF16)
    gkv = c.sb("gkv", [8], F32)
    gb = c.sb("gb", [8], F32)
    GK = c.sb("GK", [64], F32)
    GKn = c.sb("GKn", [64], F32)
    GQ = c.sb("GQ", [64], F32)
    GQn = c.sb("GQn", [64], F32)
    ESINK = c.sb("ESINK", [32], F32)
    valid = c.sb("valid", [1], F32)
    markB = c.off

    def allocB(tag, sample, ncs):
        t = NS()
        nb = 1 if sample else 2
        nkv = 1 if sample else 3
        if sample:
            alloc_common(t, tag, nxt=0, nho=1, nsig=1)
        else:
            alloc_common(t, tag)
        t.uT = c.sb("uT" + tag, [8, 128], BF16)
        t.COS = c.sb("COS" + tag, [ncs, 32], F32)
        t.SIN = c.sb("SIN" + tag, [ncs, 32], F32)
        t.CK = c.sb("CK" + tag, [64], F32)
        t.SK = c.sb("SK" + tag, [64], F32)
        t.CQ = c.sb("CQ" + tag, [64], F32)
        t.SQ = c.sb("SQ" + tag, [64], F32)
        t.ksq = c.sb("ksq" + tag, [256], F32)
        t.ssk = c.sb("ssk" + tag, [4], F32)
        t.rk = c.sb("rk" + tag, [4], F32)
        t.kn = c.sb("kn" + tag, [4, 64], F32)
        t.km1 = c.sb("km1" + tag, [4, 64], F32)
        t.km2 = c.sb("km2" + tag, [4, 64], F32)
        t.kr = c.sb("kr" + tag, [4, 64], F32)
        t.vf = c.sb("vf" + tag, [256], F32)
        t.kdup = c.sb("kdup" + tag, [4, 2, 64], BF16)
        t.KT = [c.sb(f"KT{tag}{i}", [4, 128], BF16) for i in range(nkv)]
        t.Vaug = [c.sb(f"Vaug{tag}{i}", [4, 65], BF16) for i in range(nkv)]
        t.qsq = c.sb("qsq" + tag, [512], F32)
        t.ssq = c.sb("ssq" + tag, [32], F32)
        t.rq = c.sb("rq" + tag, [32], F32)
        t.qm1 = [c.sb(f"qm1{tag}{i}", [8, 64], F32) for i in range(nb)]
        t.qm2 = [c.sb(f"qm2{tag}{i}", [8, 64], F32) for i in range(nb)]
        t.qr = c.sb("qr" + tag, [2048], BF16)
        t.QT = [c.sb(f"QT{tag}{i}", [16, 128], BF16) for i in range(nb)]
        t.sgate = [c.sb(f"sgate{tag}{i}", [2048], BF16) for i in range(nb)]
        t.PT2 = [c.sb(f"PT2{tag}{h}", [nb, 4, 128], BF16) for h in range(2)]
        t.den = c.sb("den" + tag, [4], F32)
        t.rec = c.sb("rec" + tag, [4], F32)
        t.onrm = c.sb("onrm" + tag, [4, 64], F32)
        if sample:
            t.KTs = c.sb("KTs", [NSEQ, 4, 128], BF16)
            t.Vs = c.sb("Vs", [NSEQ, 4, 65], BF16)
            t.PTz = c.sb("PTz", [NSEQ * 516 + 64], BF16)
            t.kdups = [c.sb(f"kdups{i}", [4, 2, 64], BF16) for i in range(4)]
            t.kd_slots = [[P.slot(f"kd{i}a"), P.slot(f"kd{i}b")] for i in range(4)]
            t.vs_slots = [P.slot(f"vs_in{i}") for i in range(4)]
            t.vs_ser = [Buf(f"vs_ser{i}") for i in range(4)]
            t.VsB = [Buf(f"VsB{i}") for i in range(NSEQ)]
        return t

    tB = allocB("B", False, NMAIN)
    print("phase B arena words", c.off)

    for t, d in ((gkv, g_kv), (gb, g_b), (GK, gk_d), (GQ, gq_d), (ESINK, sinks_d), (valid, valid_d)):
        c.dma(t[:, :], d, [], [t.b], P.slot("c_" + t.b.name))
    c.dma(tB.COS[:, :, :], cos_d, [], [tB.COS.b], P.slot("c_cos"))
    c.dma(tB.SIN[:, :, :], sin_d, [], [tB.SIN.b], P.slot("c_sin"))
    c.ts("pool", GKn[:, :], GK[:, :], -1.0, ALU.mult, [GK.b], [GKn.b])
    c.ts("pool", GQn[:, :], GQ[:, :], -1.0, ALU.mult, [GQ.b], [GQn.b])
    c.act(ESINK[:, :], ESINK[:, :], AF.Exp, [ESINK.b], [ESINK.b])
    for i in range(3):
        c.memset("pool", tB.Vaug[i][:, :, :], 1.0, [tB.Vaug[i].b])
    if os.environ.get("STOP_AFTER") == "BS":
        P.barrier()
        P.build(final_waits=finals)
        return nc
    cnt = [0]
    stgB = tB.ho + tB.xt
    stgB_slots = [P.slot(f"stgB{i}") for i in range(len(stgB))]
    load_weight(c, wkv, Wkv, 8, 512, gkv, 8, stgB, stgB_slots, cnt)
    load_weight(c, b_w_in, Wb_in, 8, 4096, gb, 8, stgB, stgB_slots, cnt)

    def late_weights_B():
        cnt2 = [0]
        yield from load_weight_gen(c, b_w_out, Wb_out, 16, 0, 1024, None, 1, tB.ho, stgB_slots[0:2], cnt2)

    if os.environ.get("STOP_AFTER") == "BW":
        P.barrier()
        P.build(final_waits=finals)
        return nc

    def rope_tables(t, Ct, St, G, Gn, j):
        cj = t.COS[:, j, :]
        sj = t.SIN[:, j, :]
        c.tt("pool", Ct[:, 0:32], G[:, 0:32], cj, ALU.mult, [G.b, t.COS.b], [Ct.b])
        c.tt("pool", Ct[:, 32:64], G[:, 32:64], cj, ALU.mult, [G.b, t.COS.b], [Ct.b])
        c.tt("pool", St[:, 0:32], Gn[:, 32:64], sj, ALU.mult, [Gn.b, t.SIN.b], [St.b])
        c.tt("pool", St[:, 32:64], G[:, 0:32], sj, ALU.mult, [G.b, t.SIN.b], [St.b])

    ckvs_slot = P.slot("ckvs")

    def l1_block(t, j, mode):
        halo = mode == "halo"
        sample = mode == "sample"
        last = (mode == "full" and j == NMAIN - 1)
        uT = t.uT
        nkv = len(t.KT)
        KTc, Vc = t.KT[j % nkv], t.Vaug[j % nkv]
        KTp, Vp = t.KT[(j - 1) % nkv], t.Vaug[(j - 1) % nkv]
        QT = t.QT[j % len(t.QT)]
        sgate = t.sgate[j % len(t.sgate)]
        PT2 = t.PT2
        if sample:
            xb = hs
            rms_stats(c, xb, t.junk, t.ss1, t.rr1, t.xn)
        else:
            xb = t.xt[j % 3]
        rms_tr(c, t.xn, uT, ident, 0)
        tj = 0 if sample else j
        yield
        OLD_KT = os.environ.get("OLD_KT", "0") == "1"
        OLD_QT = os.environ.get("OLD_QT", "0") == "1"

        def kt_transposes():
            pt = c.bkbf(1).rearrange("p (k t) -> p k t", k=8)
            for gg in range(4):
                c.tr(pt[:, gg, :], t.kdup[:, gg, :, :].rearrange("p a d -> p (a d)"), ident[:, :], [t.kdup.b, ident.b], [c.pb[1]])
            c.cp("act", KTc[:, :, :], pt[:, 0:4, :], [c.pb[1]], [KTc.b])

        def qt_transposes():
            ptq = c.psum[:, 0:2, :].bitcast(BF16).rearrange("p b (k t) -> p (b k) t", k=8)
            for k in range(16):
                c.tr(ptq[:, k, :], qr[:, k * 128:(k + 1) * 128], ident[:, :], [qr.b, ident.b], [c.pb[0], c.pb[1]])
            c.cp("dve", QT[:, 0:8, :], ptq[:, 0:8, :], [c.pb[0]], [QT.b])
            c.cp("act", QT[:, 8:16, :], ptq[:, 8:16, :], [c.pb[1]], [QT.b])


        def kv_section():
            rope_tables(t, t.CK, t.SK, GK, GKn, tj)
            bkv = c.bank()
            proj(c, uT, Wkv, 0, 512, bkv)
            kps = c.bk(bkv)[:, 0:256].rearrange("p (g d) -> p g d", g=4)
            kn, km1, km2, kr, kdup = t.kn, t.km1, t.km2, t.kr, t.kdup
            c.act(t.ksq[:, :], c.bk(bkv)[:, 0:256], AF.Square, [c.pb[bkv]], [t.ksq.b])
            vps = c.bk(bkv)[:, 256:512].rearrange("p (g d) -> p g d", g=4)
            if j == nkv and not sample:
                c.memset("pool", Vc[:, :, 64:65], 1.0, [Vc.b])
            c.cp("act", Vc[:, :, 0:64], vps, [c.pb[bkv]], [Vc.b])
            if last or sample:
                c.cp("act", t.vf[:, :], c.bk(bkv)[:, 256:512], [c.pb[bkv]], [t.vf.b])
            c.P.emit("dve", lambda e: e.tensor_reduce(out=t.ssk[:, :], in_=t.ksq[:, :].rearrange("p (g d) -> p g d", g=4),
                                                      axis=AX.X, op=ALU.add), [t.ksq.b], [t.ssk.b])
            c.act(t.rk[:, :], t.ssk[:, :], AF.Ln, [t.ssk.b], [t.rk.b], scale=1.0 / 64, bias=1e-6)
            c.act(t.rk[:, :], t.rk[:, :], AF.Exp, [t.rk.b], [t.rk.b], scale=-0.5)
            c.tt("dve", kn[:, :, :], kps, bc(t.rk[:, :].unsqueeze(2), [128, 4, 64]), ALU.mult, [c.pb[bkv], t.rk.b], [kn.b])
            c.free(bkv)
            yield
            c.tt("pool", km1[:, :, :], kn[:, :, :], bc(t.CK[:, None, :], [128, 4, 64]), ALU.mult, [kn.b, t.CK.b], [km1.b])
            c.tt("pool", km2[:, :, 0:32], kn[:, :, 32:64], bc(t.SK[:, None, 0:32], [128, 4, 32]), ALU.mult, [kn.b, t.SK.b], [km2.b])
            c.tt("pool", km2[:, :, 32:64], kn[:, :, 0:32], bc(t.SK[:, None, 32:64], [128, 4, 32]), ALU.mult, [kn.b, t.SK.b], [km2.b])
            c.tt("pool", kr[:, :, :], km1[:, :, :], km2[:, :, :], ALU.add, [km1.b, km2.b], [kr.b])
            c.cp("pool", kdup[:, :, 0, :], kr[:, :, :], [kr.b], [kdup.b])
            c.cp("pool", kdup[:, :, 1, :], kr[:, :, :], [kr.b], [kdup.b])
            if halo:
                kt_transposes()
            if halo:
                c.ts("dve", Vc[:, :, :], Vc[:, :, :], valid[:, 0:1], ALU.mult, [Vc.b, valid.b], [Vc.b])
            if last:
                finals.append(c.dma(ck_p, kr[:, :, :].rearrange("p g d -> p (g d)"), [kr.b], [], P.slot("ck_p")))
                finals.append(c.dma(cv_p, t.vf[:, :], [t.vf.b], [], P.slot("cv_p")))
            if sample:
                krf = kr[:, :, :].rearrange("p g d -> p (g d)")
                for s in range(NSEQ):
                    finals.append(c.dma(ck_s[s, 128 - TS:128, :], krf[s * TS:(s + 1) * TS, :], [kr.b], [], ckvs_slot))
                    finals.append(c.dma(cv_s[s, 128 - TS:128, :], t.vf[s * TS:(s + 1) * TS, :], [t.vf.b], [], ckvs_slot))
            yield

        ssq, rq, qr = t.ssq, t.rq, t.qr
        CUT = int(os.environ.get("L1CUT", "0"))
        if CUT == 1 and not halo:
            return
        if halo:
            yield from kv_section()
        if not halo:
            rope_tables(t, t.CQ, t.SQ, GQ, GQn, tj)
            for g in range(4):
                bq = c.bank()
                proj(c, uT, Wb_in, g * 512, 512, bq)
                qps = c.bk(bq).rearrange("p (h d) -> p h d", h=8)
                c.act(t.qsq[:, :], c.bk(bq), AF.Square, [c.pb[bq]], [t.qsq.b])
                c.P.emit("dve", lambda e, g=g: e.tensor_reduce(out=ssq[:, g * 8:(g + 1) * 8],
                                                               in_=t.qsq[:, :].rearrange("p (h d) -> p h d", h=8),
                                                               axis=AX.X, op=ALU.add), [t.qsq.b], [ssq.b])
                c.act(rq[:, g * 8:(g + 1) * 8], ssq[:, g * 8:(g + 1) * 8], AF.Ln, [ssq.b], [rq.b], scale=1.0 / 64, bias=1e-6)
                c.act(rq[:, g * 8:(g + 1) * 8], rq[:, g * 8:(g + 1) * 8], AF.Exp, [rq.b], [rq.b], scale=-0.5)
                m1, m2 = t.qm1[g % len(t.qm1)], t.qm2[g % len(t.qm2)]
                c.tt("dve", m1[:, :, :], qps, bc(t.CQ[:, None, :], [128, 8, 64]), ALU.mult, [c.pb[bq], t.CQ.b], [m1.b])
                c.tt("dve", m2[:, :, 0:32], qps[:, :, 32:64], bc(t.SQ[:, None, 0:32], [128, 8, 32]), ALU.mult, [c.pb[bq], t.SQ.b], [m2.b])
                c.tt("dve", m2[:, :, 32:64], qps[:, :, 0:32], bc(t.SQ[:, None, 32:64], [128, 8, 32]), ALU.mult, [c.pb[bq], t.SQ.b], [m2.b])
                c.free(bq)
                c.tt("dve", m1[:, :, :], m1[:, :, :], m2[:, :, :], ALU.add, [m1.b, m2.b], [m1.b])
                c.tt("dve", qr[:, g * 512:(g + 1) * 512].rearrange("p (h d) -> p h d", h=8), m1[:, :, :],
                     bc(rq[:, g * 8:(g + 1) * 8].unsqueeze(2), [128, 8, 64]), ALU.mult, [m1.b, rq.b], [qr.b])
                yield
            yield from kv_section()
            for g in range(4):
                bg = c.bank()
                proj(c, uT, Wb_in, 2048 + g * 512, 512, bg)
                sg = t.sig[g % len(t.sig)]
                sigmoid_act(c, sg, bg)
                c.tt("dve", sgate[:, g * 512:(g + 1) * 512], c.bk(bg), sg[:, :], ALU.mult, [c.pb[bg], sg.b], [sgate.b])
                c.free(bg)
                yield
            kt_transposes()
            qt_transposes()
        if CUT == 4 and not halo:
            return
        yield "BACK"
        if halo:
            return
        og = t.og
        ogv = og[:, :].rearrange("p (g j f d) -> p g j f d", g=4, j=4, f=2)
        sgv = sgate[:, :].rearrange("p (g j f d) -> p g j f d", g=4, j=4, f=2)
        esv = ESINK[:, :].rearrange("p (g j f) -> p g j f", g=4, j=4)
        den, rec, onrm = t.den, t.rec, t.onrm
        if sample:
            PTz = t.PTz
            pz_diag = PTz[:, 0:NSEQ * 516].rearrange("p (s r) -> p s r", r=516)[:, :, 0:512].rearrange(
                "p s (j q) -> p s j q", q=128)[:, :, :, 0:TS]
            pz_full = PTz[:, 0:NSEQ * 512].rearrange("p (s j q) -> p s j q", s=NSEQ, j=4)
        def scores(g, half):
            lo, hi = half * 64, (half + 1) * 64
            if not sample:
                tiles = ((KTp, Vp, L), (KTc, Vc, U))
            else:
                tiles = ((KTc, Vc, U4),)
                bs = c.bank()
                for s in range(NSEQ):
                    c.mm(c.bk(bs)[:, s * 16:(s + 1) * 16], t.KTs[lo:hi, s, g, :], QT[lo:hi, 4 * g:4 * g + 4, s * TS:(s + 1) * TS],
                         True, True, [t.KTs.b, QT.b], [c.pb[bs]])
                c.act(pz_diag, c.bk(bs)[:, 0:NSEQ * 16].rearrange("p (s j t) -> p s j t", s=NSEQ, j=4), AF.Exp,
                      [c.pb[bs]], [PTz.b], scale=0.125)
                c.free(bs)
                c.tt("pool", pz_diag, pz_diag, bc(CMK[:, None, None, :], [128, NSEQ, 4, TS]), ALU.mult, [PTz.b, CMK.b], [PTz.b])
            p2 = PT2[half]
            for kti, (ktile, vt, mask) in enumerate(tiles):
                bs = c.bank()
                c.mm(c.bk(bs), ktile[lo:hi, g, :], QT[lo:hi, 4 * g:4 * g + 4, :], True, True, [ktile.b, QT.b], [c.pb[bs]])
                c.act(p2[:, kti, :, :], c.bk(bs).rearrange("p (j t) -> p j t", j=4), AF.Exp, [c.pb[bs]], [p2.b], scale=0.125)
                c.free(bs)
            if not sample:
                c.tt("dve", p2[:, :, :, :], p2[:, :, :, :], bc(LU[:, :, None, :], [128, 2, 4, 128]), ALU.mult, [p2.b, LU.b], [p2.b])
            else:
                c.tt("dve", p2[:, 0, :, :], p2[:, 0, :, :], bc(U4[:, None, :], [128, 4, 128]), ALU.mult, [p2.b, U4.b], [p2.b])

        def pv(g, half):
            bo = c.bank()
            ob = c.bk(bo)[:, 0:260].rearrange("p (j e) -> p j e", j=4)
            for jj in range(4):
                if not sample:
                    c.mm(ob[:, jj, :], PT2[half][:, 0, jj, :], Vp[:, g, :], True, False, [PT2[half].b, Vp.b], [c.pb[bo]])
                    c.mm(ob[:, jj, :], PT2[half][:, 1, jj, :], Vc[:, g, :], False, True, [PT2[half].b, Vc.b], [c.pb[bo]])
                else:
                    for s in range(NSEQ):
                        c.mm(ob[:, jj, :], pz_full[:, s, jj, :], t.Vs[:, s, g, :], s == 0, False,
                             [PTz.b, t.VsB[s]], [c.pb[bo]])
                    c.mm(ob[:, jj, :], PT2[half][:, 0, jj, :], Vc[:, g, :], False, True, [PT2[half].b, Vc.b], [c.pb[bo]])
            c.tt("dve", den[:, :], ob[:, :, 64], esv[:, g, :, half], ALU.add, [c.pb[bo], ESINK.b], [den.b])
            c.P.emit("dve", lambda e: e.reciprocal(out=rec[:, :], in_=den[:, :]), [den.b], [rec.b])
            c.tt("dve", onrm[:, :, :], ob[:, :, 0:64], bc(rec[:, :].unsqueeze(2), [128, 4, 64]), ALU.mult, [c.pb[bo], rec.b], [onrm.b])
            c.free(bo)
            c.tt("pool", ogv[:, g, :, half, :], onrm[:, :, :], sgv[:, g, :, half, :], ALU.mult, [onrm.b, sgate.b], [og.b])

        its = [(g, half) for g in range(4) for half in range(2)]
        if sample:
            for (g, half) in its:
                scores(g, half)
                yield
                pv(g, half)
                yield
        else:
            scores(*its[0])
            yield
            for i, (g, half) in enumerate(its):
                if i + 1 < len(its):
                    scores(*its[i + 1])
                    yield
                pv(g, half)
                yield
        if CUT == 5:
            return
        hb = t.ho[j % len(t.ho)]
        yield
        yield from out_proj(c, og, t.ogT, Wb_out, xb, hb, ident)
        if sample:
            finals.append(c.dma(y_s, hb[0:NSEQ * TS, :], [hb.b], [], t.ho_slots[0]))
        else:
            finals.append(c.dma(y_main[(j - 1) * 128:j * 128, :], hb[:, :], [hb.b], [], t.ho_slots[j % 2]))

    gens = [l1_block(tB, 0, "halo")]
    for j in range(1, NMAIN):
        gens.append(l1_block(tB, j, "full"))
    if os.environ.get("STOP_AFTER") == "BH":
        gens = gens[:1]

    def prefetchB(k):
        xb_ = tB.xt[k % 3]
        c.dma(xb_[:, :], h1s[k * 128:(k + 1) * 128, :], [h1blk[k]], [xb_.b], tB.xt_slots[k % 3])

    def prermsB(k):
        rms_stats(c, tB.xt[k % 3], tB.junk, tB.ss1, tB.rr1, tB.xn)

    drive(gens, prefetchB, prermsB, rms_round=3, extra=late_weights_B(), extra_deadline=2, extra_from=0)
    if os.environ.get("STOP_AFTER") == "BH":
        P.barrier()
        P.build(final_waits=[f for f in finals])
        return nc

    if os.environ.get("STOP_AFTER") == "B1":
        P.barrier()
        P.build(final_waits=finals)
        return nc
    P.barrier()
    assert not c.hold, c.hold
    c.off = markB
    tBs = allocB("Bs", True, 1)
    print("phase B(sample) arena words", c.off)
    c.dma(tBs.COS[:, :, :], coss_d, [], [tBs.COS.b], P.slot("c_coss"))
    c.dma(tBs.SIN[:, :, :], sins_d, [], [tBs.SIN.b], P.slot("c_sins"))
    c.memset("pool", tBs.Vaug[0][:, :, :], 1.0, [tBs.Vaug[0].b])
    c.memset("dve", tBs.Vs[:, :, :, :], 1.0, tBs.VsB)
    c.memset("dve", tBs.PTz[:, :], 0.0, [tBs.PTz.b])
    ck4 = cache_k_s.rearrange("s p (g d) -> s p g d", g=4)
    cv4 = cache_v_s.rearrange("s p (g d) -> s p g d", g=4)
    for s in range(NSEQ):
        kd = tBs.kdups[s % 4]
        c.dma(kd[:, :, 0, :], ck4[s], [], [kd.b], tBs.kd_slots[s % 4][0], eng="pool")
        c.dma(kd[:, :, 1, :], ck4[s], [], [kd.b], tBs.kd_slots[s % 4][1], eng="pool")
        c.dma(tBs.Vs[:, s, :, 0:64], cv4[s], [], [tBs.VsB[s], tBs.vs_ser[s % 4]], tBs.vs_slots[s % 4], eng="pool")
        tb = s % 2
        pt = c.bkbf(tb).rearrange("p (k t) -> p k t", k=8)
        for g in range(4):
            c.tr(pt[:, g, :], kd[:, g, :, :].rearrange("p a d -> p (a d)"), ident[:, :], [kd.b, ident.b], [c.pb[tb]])
        c.cp("act", tBs.KTs[:, s, :, :], pt[:, 0:4, :], [c.pb[tb]], [tBs.KTs.b])
    drive([l1_block(tBs, 0, "sample")])

    P.build(final_waits=finals)
    return nc


_CACHE = {}


def _consts():
    bf = ml_dtypes.bfloat16
    i = np.arange(128)
    Um = (i[:, None] <= i[None, :]).astype(np.float32)
    same = (i[:, None] // TS == i[None, :] // TS)
    U4 = (Um * same).astype(np.float32)
    seqind = (i[:, None] // TS == np.arange(NSEQ)[None, :]).astype(np.float32)
    CM = np.broadcast_to((np.arange(NSEQ)[:, None] == (i[None, :] // TS)).astype(np.float32)[None], (128, NSEQ, 128))
    CMK = (i[:, None] >= np.arange(TS)[None, :]).astype(np.float32)
    return {
        "ident": np.eye(128, dtype=np.float32).astype(bf),
        "U": Um.astype(bf),
        "L": Um.T.copy().astype(bf),
        "LU": np.concatenate([Um.T, Um], axis=1).astype(bf),
        "U32": Um.astype(np.float32),
        "U432": U4.astype(np.float32),
        "ones32": np.ones((128, 8), np.float32),
        "U4": U4.astype(bf),
        "ones": np.ones((128, 128), np.float32).astype(bf),
        "seqind": seqind.astype(bf),
        "seqindf": seqind.astype(np.float32),
        "CM": np.ascontiguousarray(CM.reshape(128, NSEQ * 128)).astype(bf),
        "CMK": CMK.astype(bf),
    }


def _pk(v, k):
    return np.ascontiguousarray(np.asarray(v, np.float32).reshape(k, 128).T)


def _rope_tab(pos):
    half = 32
    inv = (10000.0 ** (-np.arange(half, dtype=np.float64) / half)).astype(np.float32)
    ang = (pos.astype(np.float32)[:, None] * inv[None, :]).astype(np.float32).astype(np.float64)
    return np.cos(ang).astype(np.float32), np.sin(ang).astype(np.float32)


def run(inputs, NB):
    f = lambda a: np.asarray(a, np.float32)
    x_prompt = f(inputs["x_prompt"])
    NMAIN = NB + 1
    NPRE = NB - 1
    if NB not in _CACHE:
        _CACHE[NB] = build_program(NB)
    nc = _CACHE[NB]
    shared = {
        "a_w_in": f(inputs["a_w_in"])[0], "a_w_out": f(inputs["a_w_out"])[0],
        "wkv": np.ascontiguousarray(np.concatenate([f(inputs["w_k"]), f(inputs["w_v"])], axis=1)),
        "b_w_in": f(inputs["b_w_in"])[0], "b_w_out": f(inputs["b_w_out"])[0],
        "wg2a": np.ascontiguousarray(np.concatenate([f(inputs["a_w_gate2"])[0], f(inputs["a_b_gate"])[0][None, :]], axis=0)),
        "g_a": _pk(inputs["a_norm"][0], 8), "g_ao": _pk(inputs["a_out_norm"][0], 4),
        "g_kv": _pk(inputs["kv_norm"], 8), "g_b": _pk(inputs["b_norm"][0], 8),
        "gk": np.ascontiguousarray(np.broadcast_to(f(inputs["k_norm"])[None, :], (128, 64))),
        "gq": np.ascontiguousarray(np.broadcast_to(f(inputs["b_q_norm"])[0][None, :], (128, 64))),
        "sinks": np.ascontiguousarray(np.broadcast_to(f(inputs["b_sinks"])[0][None, :], (128, 32))),
        **_consts(),
    }
    PAST = 16384
    pos_s = PAST + (np.arange(128) % TS)
    cs, sn = _rope_tab(pos_s)
    shared["cos_s"] = np.ascontiguousarray(cs.reshape(128, 1, 32))
    shared["sin_s"] = np.ascontiguousarray(sn.reshape(128, 1, 32))
    x_sample = f(inputs["x_sample"])
    state = f(inputs["state_gla"])[0]
    ckc = f(inputs["cache_swa_k"])
    cvc = f(inputs["cache_swa_v"])
    in_maps = []
    for core in range(NCORES):
        b, hf = core // 2, core % 2
        start_blk = hf * NB
        xs = x_prompt[b]
        zeros = np.zeros((128, D), np.float32)
        if hf == 0:
            x_pre = np.zeros((max(NPRE, 1) * 128, D), np.float32)
            x_main = np.concatenate([zeros, xs[0:NB * 128]], axis=0)
        else:
            x_pre = xs[0:NPRE * 128] if NPRE > 0 else np.zeros((128, D), np.float32)
            x_main = xs[(NB - 1) * 128:2 * NB * 128]
        pos = (start_blk - 1) * 128 + np.arange(NMAIN * 128)
        cos, sin = _rope_tab(pos)
        cos = cos.reshape(NMAIN, 128, 32).transpose(1, 0, 2)
        sin = sin.reshape(NMAIN, 128, 32).transpose(1, 0, 2)
        sq = slice(core * NSEQ, (core + 1) * NSEQ)
        xsp = np.zeros((128, D), np.float32)
        xsp[0:NSEQ * TS] = x_sample[sq].reshape(NSEQ * TS, D)
        m = dict(shared)
        m.update({
            "x_pre": np.ascontiguousarray(x_pre), "x_main": np.ascontiguousarray(x_main),
            "cos": np.ascontiguousarray(cos), "sin": np.ascontiguousarray(sin),
            "valid": np.full((128, 1), float(hf), np.float32),
            "x_s": xsp,
            "state_s": np.ascontiguousarray(state[sq]),
            "cache_k_s": np.ascontiguousarray(ckc[sq].reshape(NSEQ, 128, 256)),
            "cache_v_s": np.ascontiguousarray(cvc[sq].reshape(NSEQ, 128, 256)),
        })
        in_maps.append(m)
    res = run_bass_kernel_spmd(nc, in_maps, core_ids=list(range(NCORES)))
    return res.results


def kernel(**inputs):
    NB = 32
    r = run(inputs, NB)
    B, SEQ = 4, 8192
    y_prompt = np.zeros((B, SEQ, D), np.float32)
    st_p = np.zeros((1, B, 4, 128, 512), np.float32)
    ck_p = np.zeros((B, 128, 4, 64), np.float32)
    cv_p = np.zeros((B, 128, 4, 64), np.float32)
    y_sample = np.zeros((128, TS, D), np.float32)
    st_s = np.zeros((1, 128, 4, 128, 512), np.float32)
    ck_s = np.zeros((128, 128, 4, 64), np.float32)
    cv_s = np.zeros((128, 128, 4, 64), np.float32)
    for core in range(NCORES):
        b, hf = core // 2, core % 2
        y_prompt[b, hf * 4096:(hf + 1) * 4096] = r[core]["y_main"]
        if hf == 1:
            st_p[0, b] = r[core]["st_p"]
            ck_p[b] = r[core]["ck_p"].reshape(128, 4, 64)
            cv_p[b] = r[core]["cv_p"].reshape(128, 4, 64)
        sq = slice(core * NSEQ, (core + 1) * NSEQ)
        y_sample[sq] = r[core]["y_s"].reshape(NSEQ, TS, D)
        st_s[0, sq] = r[core]["st_s"]
        ck_s[sq] = r[core]["ck_s"].reshape(NSEQ, 128, 4, 64)
        cv_s[sq] = r[core]["cv_s"].reshape(NSEQ, 128, 4, 64)
    return (y_prompt, y_sample, st_p, st_s, ck_p, cv_p, ck_s, cv_s)
```

```python
import contextlib
import numpy as np
import ml_dtypes
import concourse.bass as bass
import concourse.mybir as mybir
from concourse.bass_utils import run_bass_kernel_spmd

F32 = mybir.dt.float32
BF16 = mybir.dt.bfloat16
AF = mybir.ActivationFunctionType
ALU = mybir.AluOpType
AX = mybir.AxisListType

D = 1024
NCORES = 8
ENGS = ("pe", "act", "dve", "pool", "sp")
PIPELINE = True
SAME_ENGINE_INORDER = ()


class Buf:
    __slots__ = ("name", "last_w", "readers")

    def __init__(self, name):
        self.name = name
        self.last_w = None
        self.readers = []


class Ins:
    __slots__ = ("eng", "fn", "seq", "deps", "flag", "ms", "dma_slot", "dma_cnt", "epoch")

    def __init__(self, eng, fn, seq, epoch):
        self.eng = eng
        self.fn = fn
        self.seq = seq
        self.deps = []
        self.flag = False
        self.ms = None
        self.dma_slot = None
        self.dma_cnt = 0
        self.epoch = epoch


class DmaSlot:
    def __init__(self, name):
        self.name = name
        self.count = 0
        self.sem = None
        self.last = None


class Prog:
    def __init__(self, nc):
        self.nc = nc
        self.streams = {e: [] for e in ENGS}
        self.slots = []
        self.epoch = 0
        self.waited = {}

    def new_epoch(self):
        self.epoch += 1

    def slot(self, name):
        s = DmaSlot(name)
        self.slots.append(s)
        return s

    def _add_dep(self, ins, dep):
        if dep is None or dep is ins:
            return
        if dep.dma_slot is not None:
            key = (ins.eng, "dma", id(dep.dma_slot))
            val = dep.dma_cnt
        else:
            if dep.eng == ins.eng and (dep.eng == "pe" or dep.eng in SAME_ENGINE_INORDER):
                return
            key = (ins.eng, "eng", dep.eng)
            val = dep.seq
        if self.waited.get(key, -1) >= val:
            return
        self.waited[key] = val
        ins.deps.append(dep)
        if dep.dma_slot is None:
            dep.flag = True

    def emit(self, eng, fn, reads=(), writes=(), dma=None):
        st = self.streams[eng]
        ins = Ins(eng, fn, len(st), self.epoch)
        if dma is not None:
            dma.count += 1
            ins.dma_slot = dma
            ins.dma_cnt = dma.count
            dma.last = ins
        for b in reads:
            self._add_dep(ins, b.last_w)
        for b in writes:
            self._add_dep(ins, b.last_w)
            for r in b.readers:
                self._add_dep(ins, r)
        for b in reads:
            b.readers.append(ins)
        for b in writes:
            b.last_w = ins
            b.readers = []
        st.append(ins)
        return ins

    def barrier(self):
        lasts = []
        for e in ENGS:
            for ins in reversed(self.streams[e]):
                if ins.fn is not None:
                    if ins.dma_slot is None:
                        lasts.append(ins)
                    break
        dmas = [s.last for s in self.slots if s.last is not None]
        for e in ENGS:
            ins = self.emit(e, None)
            for d in lasts + dmas:
                if d.eng != e or d.dma_slot is not None:
                    self._add_dep(ins, d)

    def build(self, final_waits=()):
        nc = self.nc
        fin = self.emit("sp", None)
        for d in final_waits:
            if d.dma_slot is None or d.dma_slot.last is d:
                self._add_dep(fin, d)
        n_epochs = self.epoch + 1
        for e in ENGS:
            cnt = {}
            for ins in self.streams[e]:
                if ins.flag:
                    cnt[ins.epoch] = cnt.get(ins.epoch, 0) + 1
                    ins.ms = cnt[ins.epoch]
        with contextlib.ExitStack() as es:
            esem = {}
            for e in ENGS:
                for ep in range(n_epochs):
                    if any(i.flag and i.epoch == ep for i in self.streams[e]):
                        esem[(e, ep)] = es.enter_context(nc.semaphore(f"p_{e}_{ep}"))
            for s in self.slots:
                if s.count > 0:
                    s.sem = es.enter_context(nc.semaphore(f"d_{s.name}"))
            block = es.enter_context(nc.Block())

            def replay(e, eng):
                for ins in self.streams[e]:
                    for d in ins.deps:
                        if d.dma_slot is not None:
                            eng.wait_ge(d.dma_slot.sem, 16 * d.dma_cnt)
                        else:
                            eng.wait_ge(esem[(d.eng, d.epoch)], d.ms)
                    if ins.fn is None:
                        continue
                    bi = ins.fn(eng)
                    if ins.dma_slot is not None:
                        bi.then_inc(ins.dma_slot.sem, 16)
                    elif ins.flag:
                        bi.then_inc(esem[(e, ins.epoch)], 1)

            @block.tensor
            def _(eng):
                replay("pe", eng)

            @block.scalar
            def _(eng):
                replay("act", eng)

            @block.vector
            def _(eng):
                replay("dve", eng)

            @block.gpsimd
            def _(eng):
                replay("pool", eng)

            @block.sync
            def _(eng):
                replay("sp", eng)
        return nc


class T:
    __slots__ = ("ap", "b")

    def __init__(self, ap, b):
        self.ap = ap
        self.b = b

    def __getitem__(self, k):
        return self.ap[k]


ARENA_WORDS = 53200


class Ctx:
    def __init__(self, nc):
        self.nc = nc
        self.P = Prog(nc)
        self.arena = nc.alloc_sbuf_tensor("arena", [128, ARENA_WORDS], F32)
        self.off = 0
        self.psum = nc.alloc_psum_tensor("ps", [128, 8, 512], F32)
        self.pb = [Buf(f"bank{i}") for i in range(8)]
        self.rr = 0
        self.hold = set()

    def sb(self, name, free, dtype):
        n = int(np.prod(free))
        words = n if dtype == F32 else (n + 1) // 2
        words = (words + 7) // 8 * 8
        assert self.off + words <= ARENA_WORDS, f"arena overflow at {name}: {self.off + words}"
        ap = self.arena[:, self.off:self.off + words]
        self.off += words
        if dtype != F32:
            ap = ap.bitcast(dtype)
        ap = ap[:, 0:n]
        if len(free) == 2:
            ap = ap.rearrange("p (a b) -> p a b", a=free[0])
        elif len(free) == 3:
            ap = ap.rearrange("p (a b c) -> p a b c", a=free[0], b=free[1])
        elif len(free) == 4:
            ap = ap.rearrange("p (a b c d) -> p a b c d", a=free[0], b=free[1], c=free[2])
        return T(ap, Buf(name))

    def bank(self):
        for _ in range(6):
            i = 2 + self.rr % 6
            self.rr += 1
            if i not in self.hold:
                self.hold.add(i)
                return i
        raise RuntimeError("all PSUM banks held")

    def free(self, i):
        self.hold.discard(i)

    def bk(self, i):
        return self.psum[:, i, :]

    def bkbf(self, i):
        return self.psum[:, i, :].bitcast(BF16)

    def mm(self, out, lhsT, rhs, start, stop, R, W):
        self.P.emit("pe", lambda e: e.matmul(out, lhsT=lhsT, rhs=rhs, start=start, stop=stop), R, W)

    def tr(self, out, in_, ident, R, W):
        self.P.emit("pe", lambda e: e.transpose(out=out, in_=in_, identity=ident), R, W)

    def act(self, out, in_, func, R, W, scale=1.0, bias=0.0, accum=None):
        if accum is None:
            self.P.emit("act", lambda e: e.activation(out=out, in_=in_, func=func, scale=scale, bias=bias), R, W)
        else:
            self.P.emit("act", lambda e: e.activation(out=out, in_=in_, func=func, scale=scale, bias=bias, accum_out=accum), R, W)

    def ts(self, eng, out, in0, s1, op0, R, W, s2=None, op1=None):
        if op1 is None:
            self.P.emit(eng, lambda e: e.tensor_scalar(out=out, in0=in0, scalar1=s1, scalar2=None, op0=op0), R, W)
        else:
            self.P.emit(eng, lambda e: e.tensor_scalar(out=out, in0=in0, scalar1=s1, scalar2=s2, op0=op0, op1=op1), R, W)

    def tt(self, eng, out, in0, in1, op, R, W):
        self.P.emit(eng, lambda e: e.tensor_tensor(out=out, in0=in0, in1=in1, op=op), R, W)

    def stt(self, out, in0, scalar, in1, op0, op1, R, W):
        self.P.emit("dve", lambda e: e.scalar_tensor_tensor(out=out, in0=in0, scalar=scalar, in1=in1, op0=op0, op1=op1), R, W)

    def cp(self, eng, out, in_, R, W):
        if eng == "act":
            self.P.emit("act", lambda e: e.copy(out=out, in_=in_), R, W)
        else:
            self.P.emit(eng, lambda e: e.tensor_copy(out=out, in_=in_), R, W)

    def dma(self, out, in_, R, W, slot, eng="sp"):
        return self.P.emit(eng, lambda e: e.dma_start(out=out, in_=in_), R, W, dma=slot)

    def memset(self, eng, ap, val, W):
        self.P.emit(eng, lambda e: e.memset(ap, val), (), W)


def bc(ap, shape):
    return ap.to_broadcast(shape)


def load_weight_gen(c, dram, dst, nk, c_lo, c_hi, gain, gain_mod, stg, stg_slots, cnt):
    CH = 1024
    ns = len(stg)
    for k in range(nk):
        for c0 in range(c_lo, c_hi, CH):
            n = min(CH, c_hi - c0)
            i = cnt[0] % ns
            cnt[0] += 1
            c.dma(stg[i][:, 0:n], dram[k * 128:(k + 1) * 128, c0:c0 + n], [], [stg[i].b], stg_slots[i])
            eng = "act" if (cnt[0] % 2 == 0) else "dve"
            o = dst[:, k, c0:c0 + n]
            if gain is None:
                c.cp(eng, o, stg[i][:, 0:n], [stg[i].b], [dst.b])
            else:
                g = gain[:, (k % gain_mod):(k % gain_mod) + 1]
                if eng == "act":
                    c.act(o, stg[i][:, 0:n], AF.Copy, [stg[i].b, gain.b], [dst.b], scale=g)
                else:
                    c.ts("dve", o, stg[i][:, 0:n], g, ALU.mult, [stg[i].b, gain.b], [dst.b])
            yield


def load_weight(c, dram, dst, nk, ncols, gain, gain_mod, stg, stg_slots, cnt):
    for _ in load_weight_gen(c, dram, dst, nk, 0, ncols, gain, gain_mod, stg, stg_slots, cnt):
        pass


def rms_stats(c, xt, junk, ss, rr, xn, dcols=1024):
    c.act(junk[:, 0:dcols], xt[:, :], AF.Square, [xt.b], [junk.b, ss.b], accum=ss[:, 0:1])
    c.act(rr[:, 0:1], ss[:, 0:1], AF.Ln, [ss.b], [rr.b], scale=1.0 / dcols, bias=1e-6)
    c.act(rr[:, 0:1], rr[:, 0:1], AF.Exp, [rr.b], [rr.b], scale=-0.5)
    c.ts("dve", xn[:, :], xt[:, :], rr[:, 0:1], ALU.mult, [xt.b, rr.b], [xn.b])


def rms_tr(c, xn, uT, ident, tbank):
    pt = c.bkbf(tbank).rearrange("p (k t) -> p k t", k=8)
    for k in range(8):
        c.tr(pt[:, k, :], xn[:, k * 128:(k + 1) * 128], ident[:, :], [xn.b, ident.b], [c.pb[tbank]])
    c.cp("dve", uT[:, :, :], pt, [c.pb[tbank]], [uT.b])


def rms_prep(c, xt, junk, ss, rr, xn, uT, ident, tbank, dcols=1024):
    rms_stats(c, xt, junk, ss, rr, xn, dcols)
    rms_tr(c, xn, uT, ident, tbank)


def proj(c, uT, W, c0, n, bank):
    for k in range(8):
        c.mm(c.bk(bank)[:, 0:n], uT[:, k, :], W[:, k, c0:c0 + n], k == 0, k == 7, [uT.b, W.b], [c.pb[bank]])


def sigmoid_act(c, sig, gbank):
    c.act(sig[:, :], c.bk(gbank), AF.Exp, [c.pb[gbank]], [sig.b], scale=-1.0)
    c.act(sig[:, :], sig[:, :], AF.Ln, [sig.b], [sig.b], bias=1.0)
    c.act(sig[:, :], sig[:, :], AF.Exp, [sig.b], [sig.b], scale=-1.0)


def out_proj(c, og, ogT, Wout, xt, ho, ident):
    pt = c.psum[:, 0:2, :].bitcast(BF16).rearrange("p b (k t) -> p (b k) t", k=8)
    for k in range(16):
        c.tr(pt[:, k, :], og[:, k * 128:(k + 1) * 128], ident[:, :], [og.b, ident.b], [c.pb[0], c.pb[1]])
    c.cp("dve", ogT[:, 0:8, :], pt[:, 0:8, :], [c.pb[0]], [ogT.b])
    c.cp("act", ogT[:, 8:16, :], pt[:, 8:16, :], [c.pb[1]], [ogT.b])
    yield
    for half in range(2):
        bk = c.bank()
        for k in range(16):
            c.mm(c.bk(bk), ogT[:, k, :], Wout[:, k, half * 512:(half + 1) * 512], k == 0, k == 15,
                 [ogT.b, Wout.b], [c.pb[bk]])
            if k == 7:
                yield
        c.tt("dve", ho[:, half * 512:(half + 1) * 512], c.bk(bk), xt[:, half * 512:(half + 1) * 512], ALU.add,
             [c.pb[bk], xt.b], [ho.b])
        c.free(bk)
        yield


class NS:
    pass


NSEQ = 16
TS = 4


def build_program(NB):
    NPRE = NB - 1
    NMAIN = NB + 1
    nc = bass.Bass("TRN2", target_bir_lowering=False)

    def din(name, shape, dt=F32):
        return nc.dram_tensor(name, list(shape), dt, kind="ExternalInput").ap()

    def dout(name, shape, dt=F32):
        return nc.dram_tensor(name, list(shape), dt, kind="ExternalOutput").ap()

    x_pre = din("x_pre", [max(NPRE, 1) * 128, D])
    x_main = din("x_main", [NMAIN * 128, D])
    a_w_in = din("a_w_in", [D, 5136])
    a_w_out = din("a_w_out", [2048, D])
    wkv = din("wkv", [D, 512])
    b_w_in = din("b_w_in", [D, 4096])
    b_w_out = din("b_w_out", [2048, D])
    wg2a_d = din("wg2a", [17, 512])
    g_a = din("g_a", [128, 8])
    g_ao = din("g_ao", [128, 4])
    g_kv = din("g_kv", [128, 8])
    g_b = din("g_b", [128, 8])
    gk_d = din("gk", [128, 64])
    gq_d = din("gq", [128, 64])
    sinks_d = din("sinks", [128, 32])
    ident_d = din("ident", [128, 128], BF16)
    U_d = din("U", [128, 128], BF16)
    L_d = din("L", [128, 128], BF16)
    LU_d = din("LU", [128, 256], BF16)
    U32_d = din("U32", [128, 128])
    U432_d = din("U432", [128, 128])
    ones32_d = din("ones32", [128, 8])
    U4_d = din("U4", [128, 128], BF16)
    ones_d = din("ones", [128, 128], BF16)
    seqind_d = din("seqind", [128, NSEQ], BF16)
    seqindf_d = din("seqindf", [128, NSEQ])
    CM_d = din("CM", [128, NSEQ * 128], BF16)
    CMK_d = din("CMK", [128, TS], BF16)
    cos_d = din("cos", [128, NMAIN, 32])
    sin_d = din("sin", [128, NMAIN, 32])
    coss_d = din("cos_s", [128, 1, 32])
    sins_d = din("sin_s", [128, 1, 32])
    valid_d = din("valid", [128, 1])
    x_s = din("x_s", [128, D])
    state_s = din("state_s", [NSEQ, 4, 128, 512])
    cache_k_s = din("cache_k_s", [NSEQ, 128, 256])
    cache_v_s = din("cache_v_s", [NSEQ, 128, 256])

    y_main = dout("y_main", [NB * 128, D])
    st_p = dout("st_p", [4, 128, 512])
    ck_p = dout("ck_p", [128, 256])
    cv_p = dout("cv_p", [128, 256])
    y_s = dout("y_s", [NSEQ * TS, D])
    st_s = dout("st_s", [NSEQ, 4, 128, 512])
    ck_s = dout("ck_s", [NSEQ, 128, 256])
    cv_s = dout("cv_s", [NSEQ, 128, 256])
    h1s = nc.dram_tensor("h1s", [NMAIN * 128, D], F32, kind="Internal").ap()

    c = Ctx(nc)
    P = c.P
    finals = []

    for q4 in range(4):
        sl = slice(q4 * 4, (q4 + 1) * 4)
        finals.append(c.dma(ck_s[sl, 0:128 - TS, :], cache_k_s[sl, TS:128, :], [], [], P.slot(f"shk{q4}")))
        finals.append(c.dma(cv_s[sl, 0:128 - TS, :], cache_v_s[sl, TS:128, :], [], [], P.slot(f"shv{q4}")))

    ident = c.sb("ident", [128], BF16)
    U = c.sb("U", [128], BF16)
    L = c.sb("L", [128], BF16)
    LU = c.sb("LU", [2, 128], BF16)
    U32 = c.sb("U32", [128], F32)
    U432 = c.sb("U432", [128], F32)
    ones32 = c.sb("ones32", [8], F32)
    U4 = c.sb("U4", [128], BF16)
    ones = c.sb("ones", [128], BF16)
    seqind = c.sb("seqind", [NSEQ], BF16)
    seqindf = c.sb("seqindf", [NSEQ], F32)
    CMK = c.sb("CMK", [TS], BF16)
    hs = c.sb("hs", [1024], F32)
    for t, d in ((ident, ident_d), (U, U_d), (L, L_d), (U4, U4_d), (ones, ones_d), (seqind, seqind_d),
                 (seqindf, seqindf_d), (CMK, CMK_d), (U32, U32_d), (U432, U432_d), (ones32, ones32_d)):
        c.dma(t[:, :], d, [], [t.b], P.slot("c_" + t.b.name))
    c.dma(LU[:, :, :], LU_d.rearrange("p (a b) -> p a b", a=2), [], [LU.b], P.slot("c_LU"))
    persist_mark = c.off


    def alloc_common(t, tag, nxt=3, nho=2, nsig=2):
        t.xt = [c.sb(f"xt{tag}{i}", [1024], F32) for i in range(nxt)]
        t.xt_slots = [P.slot(f"xt{tag}{i}") for i in range(nxt)]
        t.junk = c.sb("junk" + tag, [1024], BF16)
        t.ss1 = c.sb("ss1" + tag, [1], F32)
        t.rr1 = c.sb("rr1" + tag, [1], F32)
        t.xn = c.sb("xn" + tag, [1024], BF16)
        t.sig = [c.sb(f"sig{tag}{i}", [512], F32) for i in range(nsig)]
        t.og = c.sb("og" + tag, [2048], BF16)
        t.ogT = c.sb("ogT" + tag, [16, 128], BF16)
        t.ho = [c.sb(f"ho{tag}{i}", [1024], F32) for i in range(nho)]
        t.ho_slots = [P.slot(f"ho{tag}{i}") for i in range(nho)]

    def drive(gens, prefetch=None, prerms=None, rms_round=4, ratios=None, extra=None, extra_deadline=0, extra_from=1):
        if not PIPELINE:
            if extra is not None:
                for _ in extra:
                    pass
            for k, g in enumerate(gens):
                if prefetch is not None:
                    prefetch(k)
                    prerms(k)
                for _ in g:
                    pass
            return
        prev = None
        if prefetch is not None:
            prefetch(0)
            prerms(0)
        for k, g in enumerate(gens):
            nxt = prefetch is not None and k + 1 < len(gens)
            if nxt:
                prefetch(k + 1)
            fdone = False
            bdone = prev is None
            rounds = 0
            if extra is not None and k >= extra_deadline:
                for _ in extra:
                    pass
                extra = None
            while not (fdone and bdone):
                rounds += 1
                if extra is not None and k >= extra_from:
                    try:
                        next(extra)
                        next(extra)
                    except StopIteration:
                        extra = None
                if rounds == rms_round and nxt:
                    prerms(k + 1)
                    nxt = False
                if not bdone:
                    try:
                        for _ in range(ratios[k] if ratios is not None else 2):
                            next(prev)
                    except StopIteration:
                        bdone = True
                if not fdone:
                    try:
                        if next(g) == "BACK":
                            fdone = True
                    except StopIteration:
                        fdone = True
            if nxt:
                prerms(k + 1)
            prev = g
        if prev is not None:
            for _ in prev:
                pass

    Wa_in = c.sb("Wa_in", [8, 5136], BF16)
    Wa_out = c.sb("Wa_out", [16, 1024], BF16)
    wg2a = c.sb("wg2a", [512], BF16)
    ga = c.sb("ga", [8], F32)
    gao = c.sb("gao", [4], F32)
    markA = c.off

    NSB = 6

    def allocA(tag, sample):
        t = NS()
        nb = 1 if sample else 2
        alloc_common(t, tag, nxt=(1 if sample else 3))
        t.uT = [c.sb(f"uT{tag}{i}", [8, 128], BF16) for i in range(nb)]
        t.junk2 = c.sb("junk2" + tag, [512], BF16)
        t.glT = c.sb("glT" + tag, [128], BF16)
        t.lz = c.sb("lz" + tag, [512], F32)
        t.eb = c.sb("eb" + tag, [512], F32)
        t.enb = c.sb("enb" + tag, [512], F32)
        t.qt = c.sb("qt" + tag, [512], BF16)
        t.kt = [c.sb(f"kt{tag}{i}", [512], BF16) for i in range(nb)]
        t.qkT = [c.sb(f"qkT{tag}{i}", [8, 128], BF16) for i in range(nb)]
        t.ATm = [c.sb(f"ATm{tag}{i}", [4, 128], BF16) for i in range(nb)]
        t.vv = [c.sb(f"v{tag}{h}", [512], BF16) for h in range(4)]
        t.sso = c.sb("sso" + tag, [4], F32)
        t.ro = c.sb("ro" + tag, [4], F32)
        t.sgo = [c.sb(f"sgo{tag}{i}", [512], F32) for i in range(2)]
        if not sample:
            t.ebl = [c.sb(f"ebl{tag}{i}", [4], F32) for i in range(nb)]
            t.S = [c.sb(f"S{h}", [512], F32) for h in range(4)]
            t.Sbf = [c.sb(f"Sbf{h}", [512], BF16) for h in range(4)]
        else:
            t.ebl = [c.sb("ebls" + tag, [4, NSEQ], F32)]
            t.CM = c.sb("CM", [NSEQ, 128], BF16)
            t.qms = [c.sb(f"qms{i}", [NSEQ, 128], BF16) for i in range(2)]
            t.km = [c.sb(f"km{i}", [128], BF16) for i in range(NSB)]
            t.Ss = [c.sb(f"Ss{i}", [512], F32) for i in range(NSB)]
            t.Ss_in = [P.slot(f"Ss_in{i}") for i in range(NSB)]
            t.Ss_out = [P.slot(f"Ss_out{i}") for i in range(NSB)]
            t.Ssbf = [c.sb(f"Ssbf{i}", [512], BF16) for i in range(NSB)]
            t.next_load = 0
        return t

    tA = allocA("A", False)
    wg2f = c.sb("wg2f", [512], F32)
    print("phase A arena words", c.off)

    c.dma(ga[:, :], g_a, [], [ga.b], P.slot("ga"))
    c.dma(gao[:, :], g_ao, [], [gao.b], P.slot("gao"))
    c.dma(wg2f[0:17, :], wg2a_d, [], [wg2f.b], P.slot("wg2"))
    c.cp("dve", wg2a[0:17, :], wg2f[0:17, :], [wg2f.b], [wg2a.b])
    c.memset("pool", tA.glT[0:17, :], 1.0, [tA.glT.b])
    for h in range(4):
        c.memset("pool", tA.S[h][:, :], 0.0, [tA.S[h].b])
        c.memset("pool", tA.Sbf[h][:, :], 0.0, [tA.Sbf[h].b])
    cnt = [0]
    stgA = tA.ho + tA.xt
    stgA_slots = [P.slot(f"stgA{i}") for i in range(len(stgA))]
    for _ in load_weight_gen(c, a_w_in, Wa_in, 8, 512, 3072, ga, 8, stgA, stgA_slots, cnt):
        pass
    for _ in load_weight_gen(c, a_w_in, Wa_in, 8, 5120, 5136, ga, 8, stgA, stgA_slots, cnt):
        pass

    def late_weights():
        cnt2 = [0]
        yield from load_weight_gen(c, a_w_in, Wa_in, 8, 0, 512, ga, 8, tA.ho, stgA_slots[0:2], cnt2)
        yield from load_weight_gen(c, a_w_in, Wa_in, 8, 3072, 5120, ga, 8, tA.ho, stgA_slots[0:2], cnt2)
        yield from load_weight_gen(c, a_w_out, Wa_out, 16, 0, 1024, gao, 4, tA.ho, stgA_slots[0:2], cnt2)

    h1blk = [Buf(f"h1blk{j}") for j in range(NMAIN)]

    def l0_block(t, src, row0, mode, i, dst_rows):
        full = mode != "pre"
        sample = mode == "sample"
        CU = U4 if sample else U
        xb = t.xt[i % len(t.xt)]
        uT = t.uT[i % len(t.uT)]
        ebl = t.ebl[i % len(t.ebl)]
        kt = t.kt[i % len(t.kt)]
        qkT = t.qkT[i % len(t.qkT)]
        ATm = t.ATm[i % len(t.ATm)]
        lz, eb, enb, qt, vv = t.lz, t.eb, t.enb, t.qt, t.vv
        if sample:
            c.dma(xb[:, :], src[row0:row0 + 128, :], [], [xb.b], t.xt_slots[0])
            rms_stats(c, xb, t.junk, t.ss1, t.rr1, t.xn)
        rms_tr(c, t.xn, uT, ident, 0)
        yield
        bk = c.bank()
        for k in range(8):
            c.mm(c.bk(bk)[0:16, 0:128], Wa_in[:, k, 5120:5136], uT[:, k, :], k == 0, k == 7, [uT.b, Wa_in.b], [c.pb[bk]])
        c.cp("act", t.glT[0:16, :], c.bk(bk)[0:16, 0:128], [c.pb[bk]], [t.glT.b])
        c.free(bk)
        if full:
            bq = c.bank()
            proj(c, uT, Wa_in, 0, 512, bq)
        yield
        bkk = c.bank()
        proj(c, uT, Wa_in, 512, 512, bkk)
        yield
        bz = c.bank()
        c.mm(c.bk(bz), t.glT[0:17, :], wg2a[0:17, :], True, True, [t.glT.b, wg2a.b], [c.pb[bz]])
        c.act(lz[:, :], c.bk(bz), AF.Exp, [c.pb[bz]], [lz.b], scale=-1.0)
        c.free(bz)
        c.act(lz[:, :], lz[:, :], AF.Ln, [lz.b], [lz.b], bias=1.0)
        yield
        CU32 = U432 if sample else U32
        bb = c.bank()
        c.mm(c.bk(bb), CU32[:, :], lz[:, :], True, True, [CU32.b, lz.b], [c.pb[bb]])
        if full:
            c.act(eb[:, :], c.bk(bb), AF.Exp, [c.pb[bb]], [eb.b], scale=-1.0 / 16)
        c.act(enb[:, :], c.bk(bb), AF.Exp, [c.pb[bb]], [enb.b], scale=1.0 / 16)
        c.free(bb)
        yield
        bl = c.bank()
        if not sample:
            for h in range(4):
                c.mm(c.bk(bl)[:, h:h + 1], lz[:, h * 128:(h + 1) * 128], ones32[:, 0:1], True, True, [lz.b, ones32.b], [c.pb[bl]])
            c.act(ebl[:, :], c.bk(bl)[:, 0:4], AF.Exp, [c.pb[bl]], [ebl.b], scale=-1.0 / 16)
        else:
            for h in range(4):
                o_ = c.bk(bl)[:, h * NSEQ:(h + 1) * NSEQ]
                c.mm(o_, lz[:, h * 128:(h + 1) * 128], seqindf[:, :], True, True, [lz.b, seqindf.b], [c.pb[bl]])
            c.act(ebl[:, :, :], c.bk(bl)[:, 0:4 * NSEQ].rearrange("p (h s) -> p h s", h=4), AF.Exp, [c.pb[bl]], [ebl.b],
                  scale=-1.0 / 16)
        c.free(bl)
        if full:
            c.stt(qt[:, :], c.bk(bq), 128.0 ** -0.5, eb[:, :], ALU.mult, ALU.mult, [c.pb[bq], eb.b], [qt.b])
            c.free(bq)
        c.tt("dve", kt[:, :], c.bk(bkk), enb[:, :], ALU.mult, [c.pb[bkk], enb.b], [kt.b])
        c.free(bkk)
        yield
        if full:
            yield
            pt = c.bkbf(1).rearrange("p (k t) -> p k t", k=8)
            for h in range(4):
                c.tr(pt[:, h, :], qt[:, h * 128:(h + 1) * 128], ident[:, :], [qt.b, ident.b], [c.pb[1]])
            for h in range(4):
                c.tr(pt[:, 4 + h, :], kt[:, h * 128:(h + 1) * 128], ident[:, :], [kt.b, ident.b], [c.pb[1]])
            c.cp("act", qkT[:, :, :], pt, [c.pb[1]], [qkT.b])
            yield
            ba = c.bank()
            for h in range(4):
                c.mm(c.bk(ba)[:, h * 128:(h + 1) * 128], qkT[:, 4 + h, :], qkT[:, h, :], True, True, [qkT.b], [c.pb[ba]])
            c.tt("dve", ATm[:, :, :], c.bk(ba).rearrange("p (h t) -> p h t", h=4), bc(CU[:, None, :], [128, 4, 128]),
                 ALU.mult, [c.pb[ba], CU.b], [ATm.b])
            c.free(ba)
        yield "BACK"

        def v_proj(h):
            bv = c.bank()
            proj(c, uT, Wa_in, 1024 + h * 512, 512, bv)
            c.cp("act", vv[h][:, :], c.bk(bv), [c.pb[bv]], [vv[h].b])
            c.free(bv)

        v_proj(0)
        yield
        v_proj(1)
        yield
        for h in range(4):
            bo = None
            if not sample:
                S, Sbf = t.S, t.Sbf
                if full:
                    bo = c.bank()
                    c.mm(c.bk(bo), ATm[:, h, :], vv[h][:, :], True, False, [ATm.b, vv[h].b], [c.pb[bo]])
                    c.mm(c.bk(bo), qkT[:, h, :], Sbf[h][:, :], False, True, [qkT.b, Sbf[h].b], [c.pb[bo]])
                bd = c.bank()
                c.mm(c.bk(bd), kt[:, h * 128:(h + 1) * 128], vv[h][:, :], True, True, [kt.b, vv[h].b], [c.pb[bd]])
                if full:
                    bg = c.bank()
                    proj(c, uT, Wa_in, 3072 + h * 512, 512, bg)
                yield
                c.ts("dve", S[h][:, :], S[h][:, :], ebl[:, h:h + 1], ALU.mult, [S[h].b, ebl.b], [S[h].b])
                c.stt(S[h][:, :], c.bk(bd), ebl[:, h:h + 1], S[h][:, :], ALU.mult, ALU.add, [c.pb[bd], ebl.b, S[h].b], [S[h].b])
                c.free(bd)
                c.cp("pool", Sbf[h][:, :], S[h][:, :], [S[h].b], [Sbf[h].b])
            else:
                qms = t.qms[h % 2]
                c.tt("pool", qms[:, :, :], bc(qkT[:, h:h + 1, :], [128, NSEQ, 128]), t.CM[:, :, :], ALU.mult,
                     [qkT.b, t.CM.b], [qms.b])
                bo = c.bank()
                c.mm(c.bk(bo), ATm[:, h, :], vv[h][:, :], True, False, [ATm.b, vv[h].b], [c.pb[bo]])
                for s in range(NSEQ):
                    idx = h * NSEQ + s
                    Sb, Sbb, km = t.Ss[idx % NSB], t.Ssbf[idx % NSB], t.km[idx % NSB]
                    while t.next_load <= min(idx + NSB - 1, 4 * NSEQ - 1):
                        li = t.next_load
                        c.dma(t.Ss[li % NSB][:, :], state_s[li % NSEQ, li // NSEQ], [], [t.Ss[li % NSB].b], t.Ss_in[li % NSB])
                        t.next_load += 1
                    c.cp("act", Sbb[:, :], Sb[:, :], [Sb.b], [Sbb.b])
                    c.mm(c.bk(bo), qms[:, s, :], Sbb[:, :], False, s == NSEQ - 1, [qms.b, Sbb.b], [c.pb[bo]])
                    c.ts("dve", km[:, :], kt[:, h * 128:(h + 1) * 128], seqindf[:, s:s + 1], ALU.mult, [kt.b, seqindf.b], [km.b])
                    bd = c.bank()
                    c.mm(c.bk(bd), km[:, :], vv[h][:, :], True, True, [km.b, vv[h].b], [c.pb[bd]])
                    c.ts("dve", Sb[:, :], Sb[:, :], ebl[:, h, s:s + 1], ALU.mult, [Sb.b, ebl.b], [Sb.b])
                    c.stt(Sb[:, :], c.bk(bd), ebl[:, h, s:s + 1], Sb[:, :], ALU.mult, ALU.add, [c.pb[bd], ebl.b, Sb.b], [Sb.b])
                    c.free(bd)
                    finals.append(c.dma(st_s[s, h], Sb[:, :], [Sb.b], [], t.Ss_out[idx % NSB], eng="pool"))
                bg = c.bank()
                proj(c, uT, Wa_in, 3072 + h * 512, 512, bg)
            if full:
                c.act(t.junk2[:, 0:512], c.bk(bo), AF.Square, [c.pb[bo]], [t.junk2.b, t.sso.b], accum=t.sso[:, h:h + 1])
                c.act(t.ro[:, h:h + 1], t.sso[:, h:h + 1], AF.Ln, [t.sso.b], [t.ro.b], scale=1.0 / 512, bias=1e-6)
                c.act(t.ro[:, h:h + 1], t.ro[:, h:h + 1], AF.Exp, [t.ro.b], [t.ro.b], scale=-0.5)
                sg = t.sig[h % 2]
                so = t.sgo[h % 2]
                sigmoid_act(c, sg, bg)
                yield
                c.stt(so[:, :], c.bk(bo), t.ro[:, h:h + 1], sg[:, :], ALU.mult, ALU.mult, [c.pb[bo], t.ro.b, sg.b], [so.b])
                c.tt("dve", t.og[:, h * 512:(h + 1) * 512], c.bk(bg), so[:, :], ALU.mult, [c.pb[bg], so.b], [t.og.b])
                c.free(bg)
                c.free(bo)
            if h + 2 < 4:
                v_proj(h + 2)
            yield
        if full and not sample:
            hb = t.ho[i % 2]
            yield
            yield from out_proj(c, t.og, t.ogT, Wa_out, xb, hb, ident)
            c.dma(h1s[dst_rows:dst_rows + 128, :], hb[:, :], [hb.b], [h1blk[dst_rows // 128]], t.ho_slots[i % 2])
        elif sample:
            yield from out_proj(c, t.og, t.ogT, Wa_out, xb, hs, ident)

    gens = []
    srcsA = []
    bi = 0
    for j in range(NPRE):
        gens.append(l0_block(tA, x_pre, j * 128, "pre", bi, None))
        srcsA.append(x_pre[j * 128:(j + 1) * 128, :])
        bi += 1
    for j in range(NMAIN):
        gens.append(l0_block(tA, x_main, j * 128, "full", bi, j * 128))
        srcsA.append(x_main[j * 128:(j + 1) * 128, :])
        bi += 1

    def prefetchA(k):
        xb_ = tA.xt[k % 3]
        c.dma(xb_[:, :], srcsA[k], [], [xb_.b], tA.xt_slots[k % 3])

    def prermsA(k):
        rms_stats(c, tA.xt[k % 3], tA.junk, tA.ss1, tA.rr1, tA.xn)

    ratiosA = [1 if (k - 1) < NPRE else 2 for k in range(len(gens))]
    drive(gens, prefetchA, prermsA, rms_round=2, ratios=ratiosA, extra=late_weights(), extra_deadline=max(NPRE - 1, 0))
    stp_slot = P.slot("st_p")
    for h in range(4):
        finals.append(c.dma(st_p[h], tA.S[h][:, :], [tA.S[h].b], [], stp_slot))

    P.barrier()
    assert not c.hold, c.hold
    c.off = markA
    tAs = allocA("As", True)
    print("phase A(sample) arena words", c.off)
    c.dma(tAs.CM[:, :, :], CM_d.rearrange("p (s t) -> p s t", s=NSEQ), [], [tAs.CM.b], P.slot("c_CM"))
    c.memset("pool", tAs.glT[0:17, :], 1.0, [tAs.glT.b])
    drive([l0_block(tAs, x_s, 0, "sample", 0, None)])

    import os
    if os.environ.get("STOP_AFTER") == "A0":
        P.build(final_waits=finals)
        return nc
    if os.environ.get("STOP_AFTER") == "A":
        P.barrier()
        P.build(final_waits=finals)
        return nc
    P.barrier()
    assert not c.hold, c.hold
    P.new_epoch()
    c.off = persist_mark
    Wb_in = c.sb("Wb_in", [8, 4096], BF16)
    Wb_out = c.sb("Wb_out", [16, 1024], BF16)
    Wkv = c.sb("Wkv", [8, 512], BF16)
    gkv = c.sb("gkv", [8], F32)
    gb = c.sb("gb", [8], F32)
    GK = c.sb("GK", [64], F32)
    GKn = c.sb("GKn", [64], F32)
    GQ = c.sb("GQ", [64], F32)
    GQn = c.sb("GQn", [64], F32)
    ESINK = c.sb("ESINK", [32], F32)
    valid = c.sb("valid", [1], F32)
    markB = c.off

    def allocB(tag, sample, ncs):
        t = NS()
        nb = 1 if sample else 2
        nkv = 1 if sample else 3
        if sample:
            alloc_common(t, tag, nxt=0, nho=1, nsig=1)
        else:
            alloc_common(t, tag)
        t.uT = c.sb("uT" + tag, [8, 128], BF16)
        t.COS = c.sb("COS" + tag, [ncs, 32], F32)
        t.SIN = c.sb("SIN" + tag, [ncs, 32], F32)
        t.CK = c.sb("CK" + tag, [64], F32)
        t.SK = c.sb("SK" + tag, [64], F32)
        t.CQ = c.sb("CQ" + tag, [64], F32)
        t.SQ = c.sb("SQ" + tag, [64], F32)
        t.ksq = c.sb("ksq" + tag, [256], F32)
        t.ssk = c.sb("ssk" + tag, [4], F32)
        t.rk = c.sb("rk" + tag, [4], F32)
        t.kn = c.sb("kn" + tag, [4, 64], F32)
        t.km1 = c.sb("km1" + tag, [4, 64], F32)
        t.km2 = c.sb("km2" + tag, [4, 64], F32)
        t.kr = c.sb("kr" + tag, [4, 64], F32)
        t.vf = c.sb("vf" + tag, [256], F32)
        t.kdup = c.sb("kdup" + tag, [4, 2, 64], BF16)
        t.KT = [c.sb(f"KT{tag}{i}", [4, 128], BF16) for i in range(nkv)]
        t.Vaug = [c.sb(f"Vaug{tag}{i}", [4, 65], BF16) for i in range(nkv)]
        t.qsq = c.sb("qsq" + tag, [512], F32)
        t.ssq = c.sb("ssq" + tag, [32], F32)
        t.rq = c.sb("rq" + tag, [32], F32)
        t.qm1 = [c.sb(f"qm1{tag}{i}", [8, 64], F32) for i in range(nb)]
        t.qm2 = [c.sb(f"qm2{tag}{i}", [8, 64], F32) for i in range(nb)]
        t.qr = c.sb("qr" + tag, [2048], BF16)
        t.QT = [c.sb(f"QT{tag}{i}", [16, 128], BF16) for i in range(nb)]
        t.sgate = [c.sb(f"sgate{tag}{i}", [2048], BF16) for i in range(nb)]
        t.PT2 = [c.sb(f"PT2{tag}{h}", [nb, 4, 128], BF16) for h in range(2)]
        t.den = c.sb("den" + tag, [4], F32)
        t.rec = c.sb("rec" + tag, [4], F32)
        t.onrm = c.sb("onrm" + tag, [4, 64], F32)
        if sample:
            t.KTs = c.sb("KTs", [NSEQ, 4, 128], BF16)
            t.Vs = c.sb("Vs", [NSEQ, 4, 65], BF16)
            t.PTz = c.sb("PTz", [NSEQ * 516 + 64], BF16)
            t.kdups = [c.sb(f"kdups{i}", [4, 2, 64], BF16) for i in range(4)]
            t.kd_slots = [[P.slot(f"kd{i}a"), P.slot(f"kd{i}b")] for i in range(4)]
            t.vs_slots = [P.slot(f"vs_in{i}") for i in range(4)]
            t.vs_ser = [Buf(f"vs_ser{i}") for i in range(4)]
            t.VsB = [Buf(f"VsB{i}") for i in range(NSEQ)]
        return t

    tB = allocB("B", False, NMAIN)
    print("phase B arena words", c.off)

    for t, d in ((gkv, g_kv), (gb, g_b), (GK, gk_d), (GQ, gq_d), (ESINK, sinks_d), (valid, valid_d)):
        c.dma(t[:, :], d, [], [t.b], P.slot("c_" + t.b.name))
    c.dma(tB.COS[:, :, :], cos_d, [], [tB.COS.b], P.slot("c_cos"))
    c.dma(tB.SIN[:, :, :], sin_d, [], [tB.SIN.b], P.slot("c_sin"))
    c.ts("pool", GKn[:, :], GK[:, :], -1.0, ALU.mult, [GK.b], [GKn.b])
    c.ts("pool", GQn[:, :], GQ[:, :], -1.0, ALU.mult, [GQ.b], [GQn.b])
    c.act(ESINK[:, :], ESINK[:, :], AF.Exp, [ESINK.b], [ESINK.b])
    for i in range(3):
        c.memset("pool", tB.Vaug[i][:, :, :], 1.0, [tB.Vaug[i].b])
    if os.environ.get("STOP_AFTER") == "BS":
        P.barrier()
        P.build(final_waits=finals)
        return nc
    cnt = [0]
    stgB = tB.ho + tB.xt
    stgB_slots = [P.slot(f"stgB{i}") for i in range(len(stgB))]
    load_weight(c, wkv, Wkv, 8, 512, gkv, 8, stgB, stgB_slots, cnt)
    load_weight(c, b_w_in, Wb_in, 8, 4096, gb, 8, stgB, stgB_slots, cnt)

    def late_weights_B():
        cnt2 = [0]
        yield from load_weight_gen(c, b_w_out, Wb_out, 16, 0, 1024, None, 1, tB.ho, stgB_slots[0:2], cnt2)

    if os.environ.get("STOP_AFTER") == "BW":
        P.barrier()
        P.build(final_waits=finals)
        return nc

    def rope_tables(t, Ct, St, G, Gn, j):
        cj = t.COS[:, j, :]
        sj = t.SIN[:, j, :]
        c.tt("pool", Ct[:, 0:32], G[:, 0:32], cj, ALU.mult, [G.b, t.COS.b], [Ct.b])
        c.tt("pool", Ct[:, 32:64], G[:, 32:64], cj, ALU.mult, [G.b, t.COS.b], [Ct.b])
        c.tt("pool", St[:, 0:32], Gn[:, 32:64], sj, ALU.mult, [Gn.b, t.SIN.b], [St.b])
        c.tt("pool", St[:, 32:64], G[:, 0:32], sj, ALU.mult, [G.b, t.SIN.b], [St.b])

    ckvs_slot = P.slot("ckvs")

    def l1_block(t, j, mode):
        halo = mode == "halo"
        sample = mode == "sample"
        last = (mode == "full" and j == NMAIN - 1)
        uT = t.uT
        nkv = len(t.KT)
        KTc, Vc = t.KT[j % nkv], t.Vaug[j % nkv]
        KTp, Vp = t.KT[(j - 1) % nkv], t.Vaug[(j - 1) % nkv]
        QT = t.QT[j % len(t.QT)]
        sgate = t.sgate[j % len(t.sgate)]
        PT2 = t.PT2
        if sample:
            xb = hs
            rms_stats(c, xb, t.junk, t.ss1, t.rr1, t.xn)
        else:
            xb = t.xt[j % 3]
        rms_tr(c, t.xn, uT, ident, 0)
        tj = 0 if sample else j
        yield
        OLD_KT = os.environ.get("OLD_KT", "0") == "1"
        OLD_QT = os.environ.get("OLD_QT", "0") == "1"

        def kt_transposes():
            pt = c.bkbf(1).rearrange("p (k t) -> p k t", k=8)
            for gg in range(4):
                c.tr(pt[:, gg, :], t.kdup[:, gg, :, :].rearrange("p a d -> p (a d)"), ident[:, :], [t.kdup.b, ident.b], [c.pb[1]])
            c.cp("act", KTc[:, :, :], pt[:, 0:4, :], [c.pb[1]], [KTc.b])

        def qt_transposes():
            ptq = c.psum[:, 0:2, :].bitcast(BF16).rearrange("p b (k t) -> p (b k) t", k=8)
            for k in range(16):
                c.tr(ptq[:, k, :], qr[:, k * 128:(k + 1) * 128], ident[:, :], [qr.b, ident.b], [c.pb[0], c.pb[1]])
            c.cp("dve", QT[:, 0:8, :], ptq[:, 0:8, :], [c.pb[0]], [QT.b])
            c.cp("act", QT[:, 8:16, :], ptq[:, 8:16, :], [c.pb[1]], [QT.b])


        def kv_section():
            rope_tables(t, t.CK, t.SK, GK, GKn, tj)
            bkv = c.bank()
            proj(c, uT, Wkv, 0, 512, bkv)
            kps = c.bk(bkv)[:, 0:256].rearrange("p (g d) -> p g d", g=4)
            kn, km1, km2, kr, kdup = t.kn, t.km1, t.km2, t.kr, t.kdup
            c.act(t.ksq[:, :], c.bk(bkv)[:, 0:256], AF.Square, [c.pb[bkv]], [t.ksq.b])
            vps = c.bk(bkv)[:, 256:512].rearrange("p (g d) -> p g d", g=4)
            if j == nkv and not sample:
                c.memset("pool", Vc[:, :, 64:65], 1.0, [Vc.b])
            c.cp("act", Vc[:, :, 0:64], vps, [c.pb[bkv]], [Vc.b])
            if last or sample:
                c.cp("act", t.vf[:, :], c.bk(bkv)[:, 256:512], [c.pb[bkv]], [t.vf.b])
            c.P.emit("dve", lambda e: e.tensor_reduce(out=t.ssk[:, :], in_=t.ksq[:, :].rearrange("p (g d) -> p g d", g=4),
                                                      axis=AX.X, op=ALU.add), [t.ksq.b], [t.ssk.b])
            c.act(t.rk[:, :], t.ssk[:, :], AF.Ln, [t.ssk.b], [t.rk.b], scale=1.0 / 64, bias=1e-6)
            c.act(t.rk[:, :], t.rk[:, :], AF.Exp, [t.rk.b], [t.rk.b], scale=-0.5)
            c.tt("dve", kn[:, :, :], kps, bc(t.rk[:, :].unsqueeze(2), [128, 4, 64]), ALU.mult, [c.pb[bkv], t.rk.b], [kn.b])
            c.free(bkv)
            yield
            c.tt("pool", km1[:, :, :], kn[:, :, :], bc(t.CK[:, None, :], [128, 4, 64]), ALU.mult, [kn.b, t.CK.b], [km1.b])
            c.tt("pool", km2[:, :, 0:32], kn[:, :, 32:64], bc(t.SK[:, None, 0:32], [128, 4, 32]), ALU.mult, [kn.b, t.SK.b], [km2.b])
            c.tt("pool", km2[:, :, 32:64], kn[:, :, 0:32], bc(t.SK[:, None, 32:64], [128, 4, 32]), ALU.mult, [kn.b, t.SK.b], [km2.b])
            c.tt("pool", kr[:, :, :], km1[:, :, :], km2[:, :, :], ALU.add, [km1.b, km2.b], [kr.b])
            c.cp("pool", kdup[:, :, 0, :], kr[:, :, :], [kr.b], [kdup.b])
            c.cp("pool", kdup[:, :, 1, :], kr[:, :, :], [kr.b], [kdup.b])
            if halo:
                kt_transposes()
            if halo:
                c.ts("dve", Vc[:, :, :], Vc[:, :, :], valid[:, 0:1], ALU.mult, [Vc.b, valid.b], [Vc.b])
            if last:
                finals.append(c.dma(ck_p, kr[:, :, :].rearrange("p g d -> p (g d)"), [kr.b], [], P.slot("ck_p")))
                finals.append(c.dma(cv_p, t.vf[:, :], [t.vf.b], [], P.slot("cv_p")))
            if sample:
                krf = kr[:, :, :].rearrange("p g d -> p (g d)")
                for s in range(NSEQ):
                    finals.append(c.dma(ck_s[s, 128 - TS:128, :], krf[s * TS:(s + 1) * TS, :], [kr.b], [], ckvs_slot))
                    finals.append(c.dma(cv_s[s, 128 - TS:128, :], t.vf[s * TS:(s + 1) * TS, :], [t.vf.b], [], ckvs_slot))
            yield

        ssq, rq, qr = t.ssq, t.rq, t.qr
        CUT = int(os.environ.get("L1CUT", "0"))
        if CUT == 1 and not halo:
            return
        if halo:
            yield from kv_section()
        if not halo:
            rope_tables(t, t.CQ, t.SQ, GQ, GQn, tj)
            for g in range(4):
                bq = c.bank()
                proj(c, uT, Wb_in, g * 512, 512, bq)
                qps = c.bk(bq).rearrange("p (h d) -> p h d", h=8)
                c.act(t.qsq[:, :], c.bk(bq), AF.Square, [c.pb[bq]], [t.qsq.b])
                c.P.emit("dve", lambda e, g=g: e.tensor_reduce(out=ssq[:, g * 8:(g + 1) * 8],
                                                               in_=t.qsq[:, :].rearrange("p (h d) -> p h d", h=8),
                                                               axis=AX.X, op=ALU.add), [t.qsq.b], [ssq.b])
                c.act(rq[:, g * 8:(g + 1) * 8], ssq[:, g * 8:(g + 1) * 8], AF.Ln, [ssq.b], [rq.b], scale=1.0 / 64, bias=1e-6)
                c.act(rq[:, g * 8:(g + 1) * 8], rq[:, g * 8:(g + 1) * 8], AF.Exp, [rq.b], [rq.b], scale=-0.5)
                m1, m2 = t.qm1[g % len(t.qm1)], t.qm2[g % len(t.qm2)]
                c.tt("dve", m1[:, :, :], qps, bc(t.CQ[:, None, :], [128, 8, 64]), ALU.mult, [c.pb[bq], t.CQ.b], [m1.b])
                c.tt("dve", m2[:, :, 0:32], qps[:, :, 32:64], bc(t.SQ[:, None, 0:32], [128, 8, 32]), ALU.mult, [c.pb[bq], t.SQ.b], [m2.b])
                c.tt("dve", m2[:, :, 32:64], qps[:, :, 0:32], bc(t.SQ[:, None, 32:64], [128, 8, 32]), ALU.mult, [c.pb[bq], t.SQ.b], [m2.b])
                c.free(bq)
                c.tt("dve", m1[:, :, :], m1[:, :, :], m2[:, :, :], ALU.add, [m1.b, m2.b], [m1.b])
                c.tt("pool", qr[:, g * 512:(g + 1) * 512].rearrange("p (h d) -> p h d", h=8), m1[:, :, :],
                     bc(rq[:, g * 8:(g + 1) * 8].unsqueeze(2), [128, 8, 64]), ALU.mult, [m1.b, rq.b], [qr.b])
                yield
            yield from kv_section()
            for g in range(4):
                bg = c.bank()
                proj(c, uT, Wb_in, 2048 + g * 512, 512, bg)
                sg = t.sig[g % len(t.sig)]
                sigmoid_act(c, sg, bg)
                c.tt("dve", sgate[:, g * 512:(g + 1) * 512], c.bk(bg), sg[:, :], ALU.mult, [c.pb[bg], sg.b], [sgate.b])
                c.free(bg)
                yield
            kt_transposes()
            qt_transposes()
        if CUT == 4 and not halo:
            return
        yield "BACK"
        if halo:
            return
        og = t.og
        ogv = og[:, :].rearrange("p (g j f d) -> p g j f d", g=4, j=4, f=2)
        sgv = sgate[:, :].rearrange("p (g j f d) -> p g j f d", g=4, j=4, f=2)
        esv = ESINK[:, :].rearrange("p (g j f) -> p g j f", g=4, j=4)
        den, rec, onrm = t.den, t.rec, t.onrm
        if sample:
            PTz = t.PTz
            pz_diag = PTz[:, 0:NSEQ * 516].rearrange("p (s r) -> p s r", r=516)[:, :, 0:512].rearrange(
                "p s (j q) -> p s j q", q=128)[:, :, :, 0:TS]
            pz_full = PTz[:, 0:NSEQ * 512].rearrange("p (s j q) -> p s j q", s=NSEQ, j=4)
        def scores(g, half):
            lo, hi = half * 64, (half + 1) * 64
            if not sample:
                tiles = ((KTp, Vp, L), (KTc, Vc, U))
            else:
                tiles = ((KTc, Vc, U4),)
                bs = c.bank()
                for s in range(NSEQ):
                    c.mm(c.bk(bs)[:, s * 16:(s + 1) * 16], t.KTs[lo:hi, s, g, :], QT[lo:hi, 4 * g:4 * g + 4, s * TS:(s + 1) * TS],
                         True, True, [t.KTs.b, QT.b], [c.pb[bs]])
                c.act(pz_diag, c.bk(bs)[:, 0:NSEQ * 16].rearrange("p (s j t) -> p s j t", s=NSEQ, j=4), AF.Exp,
                      [c.pb[bs]], [PTz.b], scale=0.125)
                c.free(bs)
                c.tt("pool", pz_diag, pz_diag, bc(CMK[:, None, None, :], [128, NSEQ, 4, TS]), ALU.mult, [PTz.b, CMK.b], [PTz.b])
            p2 = PT2[half]
            for kti, (ktile, vt, mask) in enumerate(tiles):
                bs = c.bank()
                c.mm(c.bk(bs), ktile[lo:hi, g, :], QT[lo:hi, 4 * g:4 * g + 4, :], True, True, [ktile.b, QT.b], [c.pb[bs]])
                c.act(p2[:, kti, :, :], c.bk(bs).rearrange("p (j t) -> p j t", j=4), AF.Exp, [c.pb[bs]], [p2.b], scale=0.125)
                c.free(bs)
            if not sample:
                c.tt("dve", p2[:, :, :, :], p2[:, :, :, :], bc(LU[:, :, None, :], [128, 2, 4, 128]), ALU.mult, [p2.b, LU.b], [p2.b])
            else:
                c.tt("dve", p2[:, 0, :, :], p2[:, 0, :, :], bc(U4[:, None, :], [128, 4, 128]), ALU.mult, [p2.b, U4.b], [p2.b])

        def pv(g, half):
            bo = c.bank()
            ob = c.bk(bo)[:, 0:260].rearrange("p (j e) -> p j e", j=4)
            for jj in range(4):
                if not sample:
                    c.mm(ob[:, jj, :], PT2[half][:, 0, jj, :], Vp[:, g, :], True, False, [PT2[half].b, Vp.b], [c.pb[bo]])
                    c.mm(ob[:, jj, :], PT2[half][:, 1, jj, :], Vc[:, g, :], False, True, [PT2[half].b, Vc.b], [c.pb[bo]])
                else:
                    for s in range(NSEQ):
                        c.mm(ob[:, jj, :], pz_full[:, s, jj, :], t.Vs[:, s, g, :], s == 0, False,
                             [PTz.b, t.VsB[s]], [c.pb[bo]])
                    c.mm(ob[:, jj, :], PT2[half][:, 0, jj, :], Vc[:, g, :], False, True, [PT2[half].b, Vc.b], [c.pb[bo]])
            c.tt("dve", den[:, :], ob[:, :, 64], esv[:, g, :, half], ALU.add, [c.pb[bo], ESINK.b], [den.b])
            c.P.emit("dve", lambda e: e.reciprocal(out=rec[:, :], in_=den[:, :]), [den.b], [rec.b])
            c.tt("dve", onrm[:, :, :], ob[:, :, 0:64], bc(rec[:, :].unsqueeze(2), [128, 4, 64]), ALU.mult, [c.pb[bo], rec.b], [onrm.b])
            c.free(bo)
            c.tt("pool", ogv[:, g, :, half, :], onrm[:, :, :], sgv[:, g, :, half, :], ALU.mult, [onrm.b, sgate.b], [og.b])

        its = [(g, half) for g in range(4) for half in range(2)]
        if sample:
            for (g, half) in its:
                scores(g, half)
                yield
                pv(g, half)
                yield
        else:
            scores(*its[0])
            yield
            for i, (g, half) in enumerate(its):
                if i + 1 < len(its):
                    scores(*its[i + 1])
                    yield
                pv(g, half)
                yield
        if CUT == 5:
            return
        hb = t.ho[j % len(t.ho)]
        yield
        yield from out_proj(c, og, t.ogT, Wb_out, xb, hb, ident)
        if sample:
            finals.append(c.dma(y_s, hb[0:NSEQ * TS, :], [hb.b], [], t.ho_slots[0]))
        else:
            finals.append(c.dma(y_main[(j - 1) * 128:j * 128, :], hb[:, :], [hb.b], [], t.ho_slots[j % 2]))

    gens = [l1_block(tB, 0, "halo")]
    for j in range(1, NMAIN):
        gens.append(l1_block(tB, j, "full"))
    if os.environ.get("STOP_AFTER") == "BH":
        gens = gens[:1]

    def prefetchB(k):
        xb_ = tB.xt[k % 3]
        c.dma(xb_[:, :], h1s[k * 128:(k + 1) * 128, :], [h1blk[k]], [xb_.b], tB.xt_slots[k % 3])

    def prermsB(k):
        rms_stats(c, tB.xt[k % 3], tB.junk, tB.ss1, tB.rr1, tB.xn)

    drive(gens, prefetchB, prermsB, rms_round=5, extra=late_weights_B(), extra_deadline=2, extra_from=0)
    if os.environ.get("STOP_AFTER") == "BH":
        P.barrier()
        P.build(final_waits=[f for f in finals])
        return nc

    if os.environ.get("STOP_AFTER") == "B1":
        P.barrier()
        P.build(final_waits=finals)
        return nc
    P.barrier()
    assert not c.hold, c.hold
    c.off = markB
    tBs = allocB("Bs", True, 1)
    print("phase B(sample) arena words", c.off)
    c.dma(tBs.COS[:, :, :], coss_d, [], [tBs.COS.b], P.slot("c_coss"))
    c.dma(tBs.SIN[:, :, :], sins_d, [], [tBs.SIN.b], P.slot("c_sins"))
    c.memset("pool", tBs.Vaug[0][:, :, :], 1.0, [tBs.Vaug[0].b])
    c.memset("dve", tBs.Vs[:, :, :, :], 1.0, tBs.VsB)
    c.memset("dve", tBs.PTz[:, :], 0.0, [tBs.PTz.b])
    ck4 = cache_k_s.rearrange("s p (g d) -> s p g d", g=4)
    cv4 = cache_v_s.rearrange("s p (g d) -> s p g d", g=4)
    for s in range(NSEQ):
        kd = tBs.kdups[s % 4]
        c.dma(kd[:, :, 0, :], ck4[s], [], [kd.b], tBs.kd_slots[s % 4][0], eng="pool")
        c.dma(kd[:, :, 1, :], ck4[s], [], [kd.b], tBs.kd_slots[s % 4][1], eng="pool")
        c.dma(tBs.Vs[:, s, :, 0:64], cv4[s], [], [tBs.VsB[s], tBs.vs_ser[s % 4]], tBs.vs_slots[s % 4], eng="pool")
        tb = s % 2
        pt = c.bkbf(tb).rearrange("p (k t) -> p k t", k=8)
        for g in range(4):
            c.tr(pt[:, g, :], kd[:, g, :, :].rearrange("p a d -> p (a d)"), ident[:, :], [kd.b, ident.b], [c.pb[tb]])
        c.cp("act", tBs.KTs[:, s, :, :], pt[:, 0:4, :], [c.pb[tb]], [tBs.KTs.b])
    drive([l1_block(tBs, 0, "sample")])

    P.build(final_waits=finals)
    return nc


_CACHE = {}


def _consts():
    bf = ml_dtypes.bfloat16
    i = np.arange(128)
    Um = (i[:, None] <= i[None, :]).astype(np.float32)
    same = (i[:, None] // TS == i[None, :] // TS)
    U4 = (Um * same).astype(np.float32)
    seqind = (i[:, None] // TS == np.arange(NSEQ)[None, :]).astype(np.float32)
    CM = np.broadcast_to((np.arange(NSEQ)[:, None] == (i[None, :] // TS)).astype(np.float32)[None], (128, NSEQ, 128))
    CMK = (i[:, None] >= np.arange(TS)[None, :]).astype(np.float32)
    return {
        "ident": np.eye(128, dtype=np.float32).astype(bf),
        "U": Um.astype(bf),
        "L": Um.T.copy().astype(bf),
        "LU": np.concatenate([Um.T, Um], axis=1).astype(bf),
        "U32": Um.astype(np.float32),
        "U432": U4.astype(np.float32),
        "ones32": np.ones((128, 8), np.float32),
        "U4": U4.astype(bf),
        "ones": np.ones((128, 128), np.float32).astype(bf),
        "seqind": seqind.astype(bf),
        "seqindf": seqind.astype(np.float32),
        "CM": np.ascontiguousarray(CM.reshape(128, NSEQ * 128)).astype(bf),
        "CMK": CMK.astype(bf),
    }


def _pk(v, k):
    return np.ascontiguousarray(np.asarray(v, np.float32).reshape(k, 128).T)


def _rope_tab(pos):
    half = 32
    inv = (10000.0 ** (-np.arange(half, dtype=np.float64) / half)).astype(np.float32)
    ang = (pos.astype(np.float32)[:, None] * inv[None, :]).astype(np.float32).astype(np.float64)
    return np.cos(ang).astype(np.float32), np.sin(ang).astype(np.float32)


def run(inputs, NB):
    f = lambda a: np.asarray(a, np.float32)
    x_prompt = f(inputs["x_prompt"])
    NMAIN = NB + 1
    NPRE = NB - 1
    if NB not in _CACHE:
        _CACHE[NB] = build_program(NB)
    nc = _CACHE[NB]
    shared = {
        "a_w_in": f(inputs["a_w_in"])[0], "a_w_out": f(inputs["a_w_out"])[0],
        "wkv": np.ascontiguousarray(np.concatenate([f(inputs["w_k"]), f(inputs["w_v"])], axis=1)),
        "b_w_in": f(inputs["b_w_in"])[0], "b_w_out": f(inputs["b_w_out"])[0],
        "wg2a": np.ascontiguousarray(np.concatenate([f(inputs["a_w_gate2"])[0], f(inputs["a_b_gate"])[0][None, :]], axis=0)),
        "g_a": _pk(inputs["a_norm"][0], 8), "g_ao": _pk(inputs["a_out_norm"][0], 4),
        "g_kv": _pk(inputs["kv_norm"], 8), "g_b": _pk(inputs["b_norm"][0], 8),
        "gk": np.ascontiguousarray(np.broadcast_to(f(inputs["k_norm"])[None, :], (128, 64))),
        "gq": np.ascontiguousarray(np.broadcast_to(f(inputs["b_q_norm"])[0][None, :], (128, 64))),
        "sinks": np.ascontiguousarray(np.broadcast_to(f(inputs["b_sinks"])[0][None, :], (128, 32))),
        **_consts(),
    }
    PAST = 16384
    pos_s = PAST + (np.arange(128) % TS)
    cs, sn = _rope_tab(pos_s)
    shared["cos_s"] = np.ascontiguousarray(cs.reshape(128, 1, 32))
    shared["sin_s"] = np.ascontiguousarray(sn.reshape(128, 1, 32))
    x_sample = f(inputs["x_sample"])
    state = f(inputs["state_gla"])[0]
    ckc = f(inputs["cache_swa_k"])
    cvc = f(inputs["cache_swa_v"])
    in_maps = []
    for core in range(NCORES):
        b, hf = core // 2, core % 2
        start_blk = hf * NB
        xs = x_prompt[b]
        zeros = np.zeros((128, D), np.float32)
        if hf == 0:
            x_pre = np.zeros((max(NPRE, 1) * 128, D), np.float32)
            x_main = np.concatenate([zeros, xs[0:NB * 128]], axis=0)
        else:
            x_pre = xs[0:NPRE * 128] if NPRE > 0 else np.zeros((128, D), np.float32)
            x_main = xs[(NB - 1) * 128:2 * NB * 128]
        pos = (start_blk - 1) * 128 + np.arange(NMAIN * 128)
        cos, sin = _rope_tab(pos)
        cos = cos.reshape(NMAIN, 128, 32).transpose(1, 0, 2)
        sin = sin.reshape(NMAIN, 128, 32).transpose(1, 0, 2)
        sq = slice(core * NSEQ, (core + 1) * NSEQ)
        xsp = np.zeros((128, D), np.float32)
        xsp[0:NSEQ * TS] = x_sample[sq].reshape(NSEQ * TS, D)
        m = dict(shared)
        m.update({
            "x_pre": np.ascontiguousarray(x_pre), "x_main": np.ascontiguousarray(x_main),
            "cos": np.ascontiguousarray(cos), "sin": np.ascontiguousarray(sin),
            "valid": np.full((128, 1), float(hf), np.float32),
            "x_s": xsp,
            "state_s": np.ascontiguousarray(state[sq]),
            "cache_k_s": np.ascontiguousarray(ckc[sq].reshape(NSEQ, 128, 256)),
            "cache_v_s": np.ascontiguousarray(cvc[sq].reshape(NSEQ, 128, 256)),
        })
        in_maps.append(m)
    res = run_bass_kernel_spmd(nc, in_maps, core_ids=list(range(NCORES)))
    return res.results


def kernel(**inputs):
    NB = 32
    r = run(inputs, NB)
    B, SEQ = 4, 8192
    y_prompt = np.zeros((B, SEQ, D), np.float32)
    st_p = np.zeros((1, B, 4, 128, 512), np.float32)
    ck_p = np.zeros((B, 128, 4, 64), np.float32)
    cv_p = np.zeros((B, 128, 4, 64), np.float32)
    y_sample = np.zeros((128, TS, D), np.float32)
    st_s = np.zeros((1, 128, 4, 128, 512), np.float32)
    ck_s = np.zeros((128, 128, 4, 64), np.float32)
    cv_s = np.zeros((128, 128, 4, 64), np.float32)
    for core in range(NCORES):
        b, hf = core // 2, core % 2
        y_prompt[b, hf * 4096:(hf + 1) * 4096] = r[core]["y_main"]
        if hf == 1:
            st_p[0, b] = r[core]["st_p"]
            ck_p[b] = r[core]["ck_p"].reshape(128, 4, 64)
            cv_p[b] = r[core]["cv_p"].reshape(128, 4, 64)
        sq = slice(core * NSEQ, (core + 1) * NSEQ)
        y_sample[sq] = r[core]["y_s"].reshape(NSEQ, TS, D)
        st_s[0, sq] = r[core]["st_s"]
        ck_s[sq] = r[core]["ck_s"].reshape(NSEQ, 128, 4, 64)
        cv_s[sq] = r[core]["cv_s"].reshape(NSEQ, 128, 4, 64)
    return (y_prompt, y_sample, st_p, st_s, ck_p, cv_p, ck_s, cv_s)
```

```python
import contextlib
import numpy as np
import ml_dtypes
import concourse.bass as bass
import concourse.mybir as mybir
from concourse.bass_utils import run_bass_kernel_spmd

F32 = mybir.dt.float32
BF16 = mybir.dt.bfloat16
AF = mybir.ActivationFunctionType
ALU = mybir.AluOpType
AX = mybir.AxisListType

D = 1024
NCORES = 8
ENGS = ("pe", "act", "dve", "pool", "sp")
PIPELINE = True
SAME_ENGINE_INORDER = ()


class Buf:
    __slots__ = ("name", "last_w", "readers")

    def __init__(self, name):
        self.name = name
        self.last_w = None
        self.readers = []


class Ins:
    __slots__ = ("eng", "fn", "seq", "deps", "flag", "ms", "dma_slot", "dma_cnt", "epoch")

    def __init__(self, eng, fn, seq, epoch):
        self.eng = eng
        self.fn = fn
        self.seq = seq
        self.deps = []
        self.flag = False
        self.ms = None
        self.dma_slot = None
        self.dma_cnt = 0
        self.epoch = epoch


class DmaSlot:
    def __init__(self, name):
        self.name = name
        self.count = 0
        self.sem = None
        self.last = None


class Prog:
    def __init__(self, nc):
        self.nc = nc
        self.streams = {e: [] for e in ENGS}
        self.slots = []
        self.epoch = 0
        self.waited = {}

    def new_epoch(self):
        self.epoch += 1

    def slot(self, name):
        s = DmaSlot(name)
        self.slots.append(s)
        return s

    def _add_dep(self, ins, dep):
        if dep is None or dep is ins:
            return
        if dep.dma_slot is not None:
            key = (ins.eng, "dma", id(dep.dma_slot))
            val = dep.dma_cnt
        else:
            if dep.eng == ins.eng and (dep.eng == "pe" or dep.eng in SAME_ENGINE_INORDER):
                return
            key = (ins.eng, "eng", dep.eng)
            val = dep.seq
        if self.waited.get(key, -1) >= val:
            return
        self.waited[key] = val
        ins.deps.append(dep)
        if dep.dma_slot is None:
            dep.flag = True

    def emit(self, eng, fn, reads=(), writes=(), dma=None):
        st = self.streams[eng]
        ins = Ins(eng, fn, len(st), self.epoch)
        if dma is not None:
            dma.count += 1
            ins.dma_slot = dma
            ins.dma_cnt = dma.count
            dma.last = ins
        for b in reads:
            self._add_dep(ins, b.last_w)
        for b in writes:
            self._add_dep(ins, b.last_w)
            for r in b.readers:
                self._add_dep(ins, r)
        for b in reads:
            b.readers.append(ins)
        for b in writes:
            b.last_w = ins
            b.readers = []
        st.append(ins)
        return ins

    def barrier(self):
        lasts = []
        for e in ENGS:
            for ins in reversed(self.streams[e]):
                if ins.fn is not None:
                    if ins.dma_slot is None:
                        lasts.append(ins)
                    break
        dmas = [s.last for s in self.slots if s.last is not None]
        for e in ENGS:
            ins = self.emit(e, None)
            for d in lasts + dmas:
                if d.eng != e or d.dma_slot is not None:
                    self._add_dep(ins, d)

    def build(self, final_waits=()):
        nc = self.nc
        fin = self.emit("sp", None)
        for d in final_waits:
            if d.dma_slot is None or d.dma_slot.last is d:
                self._add_dep(fin, d)
        n_epochs = self.epoch + 1
        for e in ENGS:
            cnt = {}
            for ins in self.streams[e]:
                if ins.flag:
                    cnt[ins.epoch] = cnt.get(ins.epoch, 0) + 1
                    ins.ms = cnt[ins.epoch]
        with contextlib.ExitStack() as es:
            esem = {}
            for e in ENGS:
                for ep in range(n_epochs):
                    if any(i.flag and i.epoch == ep for i in self.streams[e]):
                        esem[(e, ep)] = es.enter_context(nc.semaphore(f"p_{e}_{ep}"))
            for s in self.slots:
                if s.count > 0:
                    s.sem = es.enter_context(nc.semaphore(f"d_{s.name}"))
            block = es.enter_context(nc.Block())

            def replay(e, eng):
                for ins in self.streams[e]:
                    for d in ins.deps:
                        if d.dma_slot is not None:
                            eng.wait_ge(d.dma_slot.sem, 16 * d.dma_cnt)
                        else:
                            eng.wait_ge(esem[(d.eng, d.epoch)], d.ms)
                    if ins.fn is None:
                        continue
                    bi = ins.fn(eng)
                    if ins.dma_slot is not None:
                        bi.then_inc(ins.dma_slot.sem, 16)
                    elif ins.flag:
                        bi.then_inc(esem[(e, ins.epoch)], 1)

            @block.tensor
            def _(eng):
                replay("pe", eng)

            @block.scalar
            def _(eng):
                replay("act", eng)

            @block.vector
            def _(eng):
                replay("dve", eng)

            @block.gpsimd
            def _(eng):
                replay("pool", eng)

            @block.sync
            def _(eng):
                replay("sp", eng)
        return nc


class T:
    __slots__ = ("ap", "b")

    def __init__(self, ap, b):
        self.ap = ap
        self.b = b

    def __getitem__(self, k):
        return self.ap[k]


ARENA_WORDS = 53200


class Ctx:
    def __init__(self, nc):
        self.nc = nc
        self.P = Prog(nc)
        self.arena = nc.alloc_sbuf_tensor("arena", [128, ARENA_WORDS], F32)
        self.off = 0
        self.psum = nc.alloc_psum_tensor("ps", [128, 8, 512], F32)
        self.pb = [Buf(f"bank{i}") for i in range(8)]
        self.rr = 0
        self.hold = set()

    def sb(self, name, free, dtype):
        n = int(np.prod(free))
        words = n if dtype == F32 else (n + 1) // 2
        words = (words + 7) // 8 * 8
        assert self.off + words <= ARENA_WORDS, f"arena overflow at {name}: {self.off + words}"
        ap = self.arena[:, self.off:self.off + words]
        self.off += words
        if dtype != F32:
            ap = ap.bitcast(dtype)
        ap = ap[:, 0:n]
        if len(free) == 2:
            ap = ap.rearrange("p (a b) -> p a b", a=free[0])
        elif len(free) == 3:
            ap = ap.rearrange("p (a b c) -> p a b c", a=free[0], b=free[1])
        elif len(free) == 4:
            ap = ap.rearrange("p (a b c d) -> p a b c d", a=free[0], b=free[1], c=free[2])
        return T(ap, Buf(name))

    def bank(self):
        for _ in range(6):
            i = 2 + self.rr % 6
            self.rr += 1
            if i not in self.hold:
                self.hold.add(i)
                return i
        raise RuntimeError("all PSUM banks held")

    def free(self, i):
        self.hold.discard(i)

    def bk(self, i):
        return self.psum[:, i, :]

    def bkbf(self, i):
        return self.psum[:, i, :].bitcast(BF16)

    def mm(self, out, lhsT, rhs, start, stop, R, W):
        self.P.emit("pe", lambda e: e.matmul(out, lhsT=lhsT, rhs=rhs, start=start, stop=stop), R, W)

    def tr(self, out, in_, ident, R, W):
        self.P.emit("pe", lambda e: e.transpose(out=out, in_=in_, identity=ident), R, W)

    def act(self, out, in_, func, R, W, scale=1.0, bias=0.0, accum=None):
        if accum is None:
            self.P.emit("act", lambda e: e.activation(out=out, in_=in_, func=func, scale=scale, bias=bias), R, W)
        else:
            self.P.emit("act", lambda e: e.activation(out=out, in_=in_, func=func, scale=scale, bias=bias, accum_out=accum), R, W)

    def ts(self, eng, out, in0, s1, op0, R, W, s2=None, op1=None):
        if op1 is None:
            self.P.emit(eng, lambda e: e.tensor_scalar(out=out, in0=in0, scalar1=s1, scalar2=None, op0=op0), R, W)
        else:
            self.P.emit(eng, lambda e: e.tensor_scalar(out=out, in0=in0, scalar1=s1, scalar2=s2, op0=op0, op1=op1), R, W)

    def tt(self, eng, out, in0, in1, op, R, W):
        self.P.emit(eng, lambda e: e.tensor_tensor(out=out, in0=in0, in1=in1, op=op), R, W)

    def stt(self, out, in0, scalar, in1, op0, op1, R, W):
        self.P.emit("dve", lambda e: e.scalar_tensor_tensor(out=out, in0=in0, scalar=scalar, in1=in1, op0=op0, op1=op1), R, W)

    def cp(self, eng, out, in_, R, W):
        if eng == "act":
            self.P.emit("act", lambda e: e.copy(out=out, in_=in_), R, W)
        else:
            self.P.emit(eng, lambda e: e.tensor_copy(out=out, in_=in_), R, W)

    def dma(self, out, in_, R, W, slot, eng="sp"):
        return self.P.emit(eng, lambda e: e.dma_start(out=out, in_=in_), R, W, dma=slot)

    def memset(self, eng, ap, val, W):
        self.P.emit(eng, lambda e: e.memset(ap, val), (), W)


def bc(ap, shape):
    return ap.to_broadcast(shape)


def load_weight_gen(c, dram, dst, nk, c_lo, c_hi, gain, gain_mod, stg, stg_slots, cnt):
    CH = 1024
    ns = len(stg)
    for k in range(nk):
        for c0 in range(c_lo, c_hi, CH):
            n = min(CH, c_hi - c0)
            i = cnt[0] % ns
            cnt[0] += 1
            c.dma(stg[i][:, 0:n], dram[k * 128:(k + 1) * 128, c0:c0 + n], [], [stg[i].b], stg_slots[i])
            eng = "act" if (cnt[0] % 2 == 0) else "dve"
            o = dst[:, k, c0:c0 + n]
            if gain is None:
                c.cp(eng, o, stg[i][:, 0:n], [stg[i].b], [dst.b])
            else:
                g = gain[:, (k % gain_mod):(k % gain_mod) + 1]
                if eng == "act":
                    c.act(o, stg[i][:, 0:n], AF.Copy, [stg[i].b, gain.b], [dst.b], scale=g)
                else:
                    c.ts("dve", o, stg[i][:, 0:n], g, ALU.mult, [stg[i].b, gain.b], [dst.b])
            yield


def load_weight(c, dram, dst, nk, ncols, gain, gain_mod, stg, stg_slots, cnt):
    for _ in load_weight_gen(c, dram, dst, nk, 0, ncols, gain, gain_mod, stg, stg_slots, cnt):
        pass


def rms_stats(c, xt, junk, ss, rr, xn, dcols=1024):
    c.act(junk[:, 0:dcols], xt[:, :], AF.Square, [xt.b], [junk.b, ss.b], accum=ss[:, 0:1])
    c.act(rr[:, 0:1], ss[:, 0:1], AF.Ln, [ss.b], [rr.b], scale=1.0 / dcols, bias=1e-6)
    c.act(rr[:, 0:1], rr[:, 0:1], AF.Exp, [rr.b], [rr.b], scale=-0.5)
    c.ts("dve", xn[:, :], xt[:, :], rr[:, 0:1], ALU.mult, [xt.b, rr.b], [xn.b])


def rms_tr(c, xn, uT, ident, tbank):
    pt = c.bkbf(tbank).rearrange("p (k t) -> p k t", k=8)
    for k in range(8):
        c.tr(pt[:, k, :], xn[:, k * 128:(k + 1) * 128], ident[:, :], [xn.b, ident.b], [c.pb[tbank]])
    c.cp("dve", uT[:, :, :], pt, [c.pb[tbank]], [uT.b])


def rms_prep(c, xt, junk, ss, rr, xn, uT, ident, tbank, dcols=1024):
    rms_stats(c, xt, junk, ss, rr, xn, dcols)
    rms_tr(c, xn, uT, ident, tbank)


def proj(c, uT, W, c0, n, bank):
    for k in range(8):
        c.mm(c.bk(bank)[:, 0:n], uT[:, k, :], W[:, k, c0:c0 + n], k == 0, k == 7, [uT.b, W.b], [c.pb[bank]])


def sigmoid_act(c, sig, gbank):
    c.act(sig[:, :], c.bk(gbank), AF.Exp, [c.pb[gbank]], [sig.b], scale=-1.0)
    c.act(sig[:, :], sig[:, :], AF.Ln, [sig.b], [sig.b], bias=1.0)
    c.act(sig[:, :], sig[:, :], AF.Exp, [sig.b], [sig.b], scale=-1.0)


def out_proj(c, og, ogT, Wout, xt, ho, ident):
    pt = c.psum[:, 0:2, :].bitcast(BF16).rearrange("p b (k t) -> p (b k) t", k=8)
    for k in range(16):
        c.tr(pt[:, k, :], og[:, k * 128:(k + 1) * 128], ident[:, :], [og.b, ident.b], [c.pb[0], c.pb[1]])
    c.cp("dve", ogT[:, 0:8, :], pt[:, 0:8, :], [c.pb[0]], [ogT.b])
    c.cp("act", ogT[:, 8:16, :], pt[:, 8:16, :], [c.pb[1]], [ogT.b])
    yield
    for half in range(2):
        bk = c.bank()
        for k in range(16):
            c.mm(c.bk(bk), ogT[:, k, :], Wout[:, k, half * 512:(half + 1) * 512], k == 0, k == 15,
                 [ogT.b, Wout.b], [c.pb[bk]])
            if k == 7:
                yield
        c.tt("dve", ho[:, half * 512:(half + 1) * 512], c.bk(bk), xt[:, half * 512:(half + 1) * 512], ALU.add,
             [c.pb[bk], xt.b], [ho.b])
        c.free(bk)
        yield


class NS:
    pass


NSEQ = 16
TS = 4


def build_program(NB):
    NPRE = NB - 1
    NMAIN = NB + 1
    nc = bass.Bass("TRN2", target_bir_lowering=False)

    def din(name, shape, dt=F32):
        return nc.dram_tensor(name, list(shape), dt, kind="ExternalInput").ap()

    def dout(name, shape, dt=F32):
        return nc.dram_tensor(name, list(shape), dt, kind="ExternalOutput").ap()

    x_pre = din("x_pre", [max(NPRE, 1) * 128, D])
    x_main = din("x_main", [NMAIN * 128, D])
    a_w_in = din("a_w_in", [D, 5136])
    a_w_out = din("a_w_out", [2048, D])
    wkv = din("wkv", [D, 512])
    b_w_in = din("b_w_in", [D, 4096])
    b_w_out = din("b_w_out", [2048, D])
    wg2a_d = din("wg2a", [17, 512])
    g_a = din("g_a", [128, 8])
    g_ao = din("g_ao", [128, 4])
    g_kv = din("g_kv", [128, 8])
    g_b = din("g_b", [128, 8])
    gk_d = din("gk", [128, 64])
    gq_d = din("gq", [128, 64])
    sinks_d = din("sinks", [128, 32])
    ident_d = din("ident", [128, 128], BF16)
    U_d = din("U", [128, 128], BF16)
    L_d = din("L", [128, 128], BF16)
    LU_d = din("LU", [128, 256], BF16)
    U32_d = din("U32", [128, 128])
    U432_d = din("U432", [128, 128])
    ones32_d = din("ones32", [128, 8])
    U4_d = din("U4", [128, 128], BF16)
    ones_d = din("ones", [128, 128], BF16)
    seqind_d = din("seqind", [128, NSEQ], BF16)
    seqindf_d = din("seqindf", [128, NSEQ])
    CM_d = din("CM", [128, NSEQ * 128], BF16)
    CMK_d = din("CMK", [128, TS], BF16)
    cos_d = din("cos", [128, NMAIN, 32])
    sin_d = din("sin", [128, NMAIN, 32])
    coss_d = din("cos_s", [128, 1, 32])
    sins_d = din("sin_s", [128, 1, 32])
    valid_d = din("valid", [128, 1])
    x_s = din("x_s", [128, D])
    state_s = din("state_s", [NSEQ, 4, 128, 512])
    cache_k_s = din("cache_k_s", [NSEQ, 128, 256])
    cache_v_s = din("cache_v_s", [NSEQ, 128, 256])

    y_main = dout("y_main", [NB * 128, D])
    st_p = dout("st_p", [4, 128, 512])
    ck_p = dout("ck_p", [128, 256])
    cv_p = dout("cv_p", [128, 256])
    y_s = dout("y_s", [NSEQ * TS, D])
    st_s = dout("st_s", [NSEQ, 4, 128, 512])
    ck_s = dout("ck_s", [NSEQ, 128, 256])
    cv_s = dout("cv_s", [NSEQ, 128, 256])
    h1s = nc.dram_tensor("h1s", [NMAIN * 128, D], F32, kind="Internal").ap()

    c = Ctx(nc)
    P = c.P
    finals = []

    for q4 in range(4):
        sl = slice(q4 * 4, (q4 + 1) * 4)
        finals.append(c.dma(ck_s[sl, 0:128 - TS, :], cache_k_s[sl, TS:128, :], [], [], P.slot(f"shk{q4}")))
        finals.append(c.dma(cv_s[sl, 0:128 - TS, :], cache_v_s[sl, TS:128, :], [], [], P.slot(f"shv{q4}")))

    ident = c.sb("ident", [128], BF16)
    U = c.sb("U", [128], BF16)
    L = c.sb("L", [128], BF16)
    LU = c.sb("LU", [2, 128], BF16)
    U32 = c.sb("U32", [128], F32)
    U432 = c.sb("U432", [128], F32)
    ones32 = c.sb("ones32", [8], F32)
    U4 = c.sb("U4", [128], BF16)
    ones = c.sb("ones", [128], BF16)
    seqind = c.sb("seqind", [NSEQ], BF16)
    seqindf = c.sb("seqindf", [NSEQ], F32)
    CMK = c.sb("CMK", [TS], BF16)
    hs = c.sb("hs", [1024], F32)
    for t, d in ((ident, ident_d), (U, U_d), (L, L_d), (U4, U4_d), (ones, ones_d), (seqind, seqind_d),
                 (seqindf, seqindf_d), (CMK, CMK_d), (U32, U32_d), (U432, U432_d), (ones32, ones32_d)):
        c.dma(t[:, :], d, [], [t.b], P.slot("c_" + t.b.name))
    c.dma(LU[:, :, :], LU_d.rearrange("p (a b) -> p a b", a=2), [], [LU.b], P.slot("c_LU"))
    persist_mark = c.off


    def alloc_common(t, tag, nxt=3, nho=2, nsig=2):
        t.xt = [c.sb(f"xt{tag}{i}", [1024], F32) for i in range(nxt)]
        t.xt_slots = [P.slot(f"xt{tag}{i}") for i in range(nxt)]
        t.junk = c.sb("junk" + tag, [1024], BF16)
        t.ss1 = c.sb("ss1" + tag, [1], F32)
        t.rr1 = c.sb("rr1" + tag, [1], F32)
        t.xn = c.sb("xn" + tag, [1024], BF16)
        t.sig = [c.sb(f"sig{tag}{i}", [512], F32) for i in range(nsig)]
        t.og = c.sb("og" + tag, [2048], BF16)
        t.ogT = c.sb("ogT" + tag, [16, 128], BF16)
        t.ho = [c.sb(f"ho{tag}{i}", [1024], F32) for i in range(nho)]
        t.ho_slots = [P.slot(f"ho{tag}{i}") for i in range(nho)]

    def drive(gens, prefetch=None, prerms=None, rms_round=4, ratios=None, extra=None, extra_deadline=0, extra_from=1):
        if not PIPELINE:
            if extra is not None:
                for _ in extra:
                    pass
            for k, g in enumerate(gens):
                if prefetch is not None:
                    prefetch(k)
                    prerms(k)
                for _ in g:
                    pass
            return
        prev = None
        if prefetch is not None:
            prefetch(0)
            prerms(0)
        for k, g in enumerate(gens):
            nxt = prefetch is not None and k + 1 < len(gens)
            if nxt:
                prefetch(k + 1)
            fdone = False
            bdone = prev is None
            rounds = 0
            if extra is not None and k >= extra_deadline:
                for _ in extra:
                    pass
                extra = None
            while not (fdone and bdone):
                rounds += 1
                if extra is not None and k >= extra_from:
                    try:
                        next(extra)
                        next(extra)
                    except StopIteration:
                        extra = None
                if rounds == rms_round and nxt:
                    prerms(k + 1)
                    nxt = False
                if not bdone:
                    try:
                        for _ in range(ratios[k] if ratios is not None else 2):
                            next(prev)
                    except StopIteration:
                        bdone = True
                if not fdone:
                    try:
                        if next(g) == "BACK":
                            fdone = True
                    except StopIteration:
                        fdone = True
            if nxt:
                prerms(k + 1)
            prev = g
        if prev is not None:
            for _ in prev:
                pass

    Wa_in = c.sb("Wa_in", [8, 5136], BF16)
    Wa_out = c.sb("Wa_out", [16, 1024], BF16)
    wg2a = c.sb("wg2a", [512], BF16)
    ga = c.sb("ga", [8], F32)
    gao = c.sb("gao", [4], F32)
    markA = c.off

    NSB = 6

    def allocA(tag, sample):
        t = NS()
        nb = 1 if sample else 2
        alloc_common(t, tag, nxt=(1 if sample else 3))
        t.uT = [c.sb(f"uT{tag}{i}", [8, 128], BF16) for i in range(nb)]
        t.junk2 = c.sb("junk2" + tag, [512], BF16)
        t.glT = c.sb("glT" + tag, [128], BF16)
        t.lz = c.sb("lz" + tag, [512], F32)
        t.eb = c.sb("eb" + tag, [512], F32)
        t.enb = c.sb("enb" + tag, [512], F32)
        t.qt = c.sb("qt" + tag, [512], BF16)
        t.kt = [c.sb(f"kt{tag}{i}", [512], BF16) for i in range(nb)]
        t.qkT = [c.sb(f"qkT{tag}{i}", [8, 128], BF16) for i in range(nb)]
        t.ATm = [c.sb(f"ATm{tag}{i}", [4, 128], BF16) for i in range(nb)]
        t.vv = [c.sb(f"v{tag}{h}", [512], BF16) for h in range(4)]
        t.sso = c.sb("sso" + tag, [4], F32)
        t.ro = c.sb("ro" + tag, [4], F32)
        t.sgo = [c.sb(f"sgo{tag}{i}", [512], F32) for i in range(2)]
        if not sample:
            t.ebl = [c.sb(f"ebl{tag}{i}", [4], F32) for i in range(nb)]
            t.S = [c.sb(f"S{h}", [512], F32) for h in range(4)]
            t.Sbf = [c.sb(f"Sbf{h}", [512], BF16) for h in range(4)]
        else:
            t.ebl = [c.sb("ebls" + tag, [4, NSEQ], F32)]
            t.CM = c.sb("CM", [NSEQ, 128], BF16)
            t.qms = [c.sb(f"qms{i}", [NSEQ, 128], BF16) for i in range(2)]
            t.km = [c.sb(f"km{i}", [128], BF16) for i in range(NSB)]
            t.Ss = [c.sb(f"Ss{i}", [512], F32) for i in range(NSB)]
            t.Ss_in = [P.slot(f"Ss_in{i}") for i in range(NSB)]
            t.Ss_out = [P.slot(f"Ss_out{i}") for i in range(NSB)]
            t.Ssbf = [c.sb(f"Ssbf{i}", [512], BF16) for i in range(NSB)]
            t.next_load = 0
        return t

    tA = allocA("A", False)
    wg2f = c.sb("wg2f", [512], F32)
    print("phase A arena words", c.off)

    c.dma(ga[:, :], g_a, [], [ga.b], P.slot("ga"))
    c.dma(gao[:, :], g_ao, [], [gao.b], P.slot("gao"))
    c.dma(wg2f[0:17, :], wg2a_d, [], [wg2f.b], P.slot("wg2"))
    c.cp("dve", wg2a[0:17, :], wg2f[0:17, :], [wg2f.b], [wg2a.b])
    c.memset("pool", tA.glT[0:17, :], 1.0, [tA.glT.b])
    for h in range(4):
        c.memset("pool", tA.S[h][:, :], 0.0, [tA.S[h].b])
        c.memset("pool", tA.Sbf[h][:, :], 0.0, [tA.Sbf[h].b])
    cnt = [0]
    stgA = tA.ho + tA.xt
    stgA_slots = [P.slot(f"stgA{i}") for i in range(len(stgA))]
    for _ in load_weight_gen(c, a_w_in, Wa_in, 8, 512, 3072, ga, 8, stgA, stgA_slots, cnt):
        pass
    for _ in load_weight_gen(c, a_w_in, Wa_in, 8, 5120, 5136, ga, 8, stgA, stgA_slots, cnt):
        pass

    def late_weights():
        cnt2 = [0]
        yield from load_weight_gen(c, a_w_in, Wa_in, 8, 0, 512, ga, 8, tA.ho, stgA_slots[0:2], cnt2)
        yield from load_weight_gen(c, a_w_in, Wa_in, 8, 3072, 5120, ga, 8, tA.ho, stgA_slots[0:2], cnt2)
        yield from load_weight_gen(c, a_w_out, Wa_out, 16, 0, 1024, gao, 4, tA.ho, stgA_slots[0:2], cnt2)

    h1blk = [Buf(f"h1blk{j}") for j in range(NMAIN)]

    def l0_block(t, src, row0, mode, i, dst_rows):
        full = mode != "pre"
        sample = mode == "sample"
        CU = U4 if sample else U
        xb = t.xt[i % len(t.xt)]
        uT = t.uT[i % len(t.uT)]
        ebl = t.ebl[i % len(t.ebl)]
        kt = t.kt[i % len(t.kt)]
        qkT = t.qkT[i % len(t.qkT)]
        ATm = t.ATm[i % len(t.ATm)]
        lz, eb, enb, qt, vv = t.lz, t.eb, t.enb, t.qt, t.vv
        if sample:
            c.dma(xb[:, :], src[row0:row0 + 128, :], [], [xb.b], t.xt_slots[0])
            rms_stats(c, xb, t.junk, t.ss1, t.rr1, t.xn)
        rms_tr(c, t.xn, uT, ident, 0)
        yield
        bk = c.bank()
        for k in range(8):
            c.mm(c.bk(bk)[0:16, 0:128], Wa_in[:, k, 5120:5136], uT[:, k, :], k == 0, k == 7, [uT.b, Wa_in.b], [c.pb[bk]])
        c.cp("act", t.glT[0:16, :], c.bk(bk)[0:16, 0:128], [c.pb[bk]], [t.glT.b])
        c.free(bk)
        if full:
            bq = c.bank()
            proj(c, uT, Wa_in, 0, 512, bq)
        yield
        bkk = c.bank()
        proj(c, uT, Wa_in, 512, 512, bkk)
        yield
        bz = c.bank()
        c.mm(c.bk(bz), t.glT[0:17, :], wg2a[0:17, :], True, True, [t.glT.b, wg2a.b], [c.pb[bz]])
        c.act(lz[:, :], c.bk(bz), AF.Exp, [c.pb[bz]], [lz.b], scale=-1.0)
        c.free(bz)
        c.act(lz[:, :], lz[:, :], AF.Ln, [lz.b], [lz.b], bias=1.0)
        yield
        CU32 = U432 if sample else U32
        bb = c.bank()
        c.mm(c.bk(bb), CU32[:, :], lz[:, :], True, True, [CU32.b, lz.b], [c.pb[bb]])
        if full:
            c.act(eb[:, :], c.bk(bb), AF.Exp, [c.pb[bb]], [eb.b], scale=-1.0 / 16)
        c.act(enb[:, :], c.bk(bb), AF.Exp, [c.pb[bb]], [enb.b], scale=1.0 / 16)
        c.free(bb)
        yield
        bl = c.bank()
        if not sample:
            for h in range(4):
                c.mm(c.bk(bl)[:, h:h + 1], lz[:, h * 128:(h + 1) * 128], ones32[:, 0:1], True, True, [lz.b, ones32.b], [c.pb[bl]])
            c.act(ebl[:, :], c.bk(bl)[:, 0:4], AF.Exp, [c.pb[bl]], [ebl.b], scale=-1.0 / 16)
        else:
            for h in range(4):
                o_ = c.bk(bl)[:, h * NSEQ:(h + 1) * NSEQ]
                c.mm(o_, lz[:, h * 128:(h + 1) * 128], seqindf[:, :], True, True, [lz.b, seqindf.b], [c.pb[bl]])
            c.act(ebl[:, :, :], c.bk(bl)[:, 0:4 * NSEQ].rearrange("p (h s) -> p h s", h=4), AF.Exp, [c.pb[bl]], [ebl.b],
                  scale=-1.0 / 16)
        c.free(bl)
        if full:
            c.stt(qt[:, :], c.bk(bq), 128.0 ** -0.5, eb[:, :], ALU.mult, ALU.mult, [c.pb[bq], eb.b], [qt.b])
            c.free(bq)
        c.tt("dve", kt[:, :], c.bk(bkk), enb[:, :], ALU.mult, [c.pb[bkk], enb.b], [kt.b])
        c.free(bkk)
        yield
        if full:
            yield
            pt = c.bkbf(1).rearrange("p (k t) -> p k t", k=8)
            for h in range(4):
                c.tr(pt[:, h, :], qt[:, h * 128:(h + 1) * 128], ident[:, :], [qt.b, ident.b], [c.pb[1]])
            for h in range(4):
                c.tr(pt[:, 4 + h, :], kt[:, h * 128:(h + 1) * 128], ident[:, :], [kt.b, ident.b], [c.pb[1]])
            c.cp("act", qkT[:, :, :], pt, [c.pb[1]], [qkT.b])
            yield
            ba = c.bank()
            for h in range(4):
                c.mm(c.bk(ba)[:, h * 128:(h + 1) * 128], qkT[:, 4 + h, :], qkT[:, h, :], True, True, [qkT.b], [c.pb[ba]])
            c.tt("dve", ATm[:, :, :], c.bk(ba).rearrange("p (h t) -> p h t", h=4), bc(CU[:, None, :], [128, 4, 128]),
                 ALU.mult, [c.pb[ba], CU.b], [ATm.b])
            c.free(ba)
        yield "BACK"

        def v_proj(h):
            bv = c.bank()
            proj(c, uT, Wa_in, 1024 + h * 512, 512, bv)
            c.cp("act", vv[h][:, :], c.bk(bv), [c.pb[bv]], [vv[h].b])
            c.free(bv)

        v_proj(0)
        yield
        v_proj(1)
        yield
        for h in range(4):
            bo = None
            if not sample:
                S, Sbf = t.S, t.Sbf
                if full:
                    bo = c.bank()
                    c.mm(c.bk(bo), ATm[:, h, :], vv[h][:, :], True, False, [ATm.b, vv[h].b], [c.pb[bo]])
                    c.mm(c.bk(bo), qkT[:, h, :], Sbf[h][:, :], False, True, [qkT.b, Sbf[h].b], [c.pb[bo]])
                bd = c.bank()
                c.mm(c.bk(bd), kt[:, h * 128:(h + 1) * 128], vv[h][:, :], True, True, [kt.b, vv[h].b], [c.pb[bd]])
                if full:
                    bg = c.bank()
                    proj(c, uT, Wa_in, 3072 + h * 512, 512, bg)
                yield
                c.ts("dve", S[h][:, :], S[h][:, :], ebl[:, h:h + 1], ALU.mult, [S[h].b, ebl.b], [S[h].b])
                c.stt(S[h][:, :], c.bk(bd), ebl[:, h:h + 1], S[h][:, :], ALU.mult, ALU.add, [c.pb[bd], ebl.b, S[h].b], [S[h].b])
                c.free(bd)
                c.cp("pool", Sbf[h][:, :], S[h][:, :], [S[h].b], [Sbf[h].b])
            else:
                qms = t.qms[h % 2]
                c.tt("pool", qms[:, :, :], bc(qkT[:, h:h + 1, :], [128, NSEQ, 128]), t.CM[:, :, :], ALU.mult,
                     [qkT.b, t.CM.b], [qms.b])
                bo = c.bank()
                c.mm(c.bk(bo), ATm[:, h, :], vv[h][:, :], True, False, [ATm.b, vv[h].b], [c.pb[bo]])
                for s in range(NSEQ):
                    idx = h * NSEQ + s
                    Sb, Sbb, km = t.Ss[idx % NSB], t.Ssbf[idx % NSB], t.km[idx % NSB]
                    while t.next_load <= min(idx + NSB - 1, 4 * NSEQ - 1):
                        li = t.next_load
                        c.dma(t.Ss[li % NSB][:, :], state_s[li % NSEQ, li // NSEQ], [], [t.Ss[li % NSB].b], t.Ss_in[li % NSB])
                        t.next_load += 1
                    c.cp("act", Sbb[:, :], Sb[:, :], [Sb.b], [Sbb.b])
                    c.mm(c.bk(bo), qms[:, s, :], Sbb[:, :], False, s == NSEQ - 1, [qms.b, Sbb.b], [c.pb[bo]])
                    c.ts("dve", km[:, :], kt[:, h * 128:(h + 1) * 128], seqindf[:, s:s + 1], ALU.mult, [kt.b, seqindf.b], [km.b])
                    bd = c.bank()
                    c.mm(c.bk(bd), km[:, :], vv[h][:, :], True, True, [km.b, vv[h].b], [c.pb[bd]])
                    c.ts("dve", Sb[:, :], Sb[:, :], ebl[:, h, s:s + 1], ALU.mult, [Sb.b, ebl.b], [Sb.b])
                    c.stt(Sb[:, :], c.bk(bd), ebl[:, h, s:s + 1], Sb[:, :], ALU.mult, ALU.add, [c.pb[bd], ebl.b, Sb.b], [Sb.b])
                    c.free(bd)
                    finals.append(c.dma(st_s[s, h], Sb[:, :], [Sb.b], [], t.Ss_out[idx % NSB], eng="pool"))
                bg = c.bank()
                proj(c, uT, Wa_in, 3072 + h * 512, 512, bg)
            if full:
                c.act(t.junk2[:, 0:512], c.bk(bo), AF.Square, [c.pb[bo]], [t.junk2.b, t.sso.b], accum=t.sso[:, h:h + 1])
                c.act(t.ro[:, h:h + 1], t.sso[:, h:h + 1], AF.Ln, [t.sso.b], [t.ro.b], scale=1.0 / 512, bias=1e-6)
                c.act(t.ro[:, h:h + 1], t.ro[:, h:h + 1], AF.Exp, [t.ro.b], [t.ro.b], scale=-0.5)
                sg = t.sig[h % 2]
                so = t.sgo[h % 2]
                sigmoid_act(c, sg, bg)
                yield
                c.stt(so[:, :], c.bk(bo), t.ro[:, h:h + 1], sg[:, :], ALU.mult, ALU.mult, [c.pb[bo], t.ro.b, sg.b], [so.b])
                c.tt("dve", t.og[:, h * 512:(h + 1) * 512], c.bk(bg), so[:, :], ALU.mult, [c.pb[bg], so.b], [t.og.b])
                c.free(bg)
                c.free(bo)
            if h + 2 < 4:
                v_proj(h + 2)
            yield
        if full and not sample:
            hb = t.ho[i % 2]
            yield
            yield from out_proj(c, t.og, t.ogT, Wa_out, xb, hb, ident)
            c.dma(h1s[dst_rows:dst_rows + 128, :], hb[:, :], [hb.b], [h1blk[dst_rows // 128]], t.ho_slots[i % 2])
        elif sample:
            yield from out_proj(c, t.og, t.ogT, Wa_out, xb, hs, ident)

    gens = []
    srcsA = []
    bi = 0
    for j in range(NPRE):
        gens.append(l0_block(tA, x_pre, j * 128, "pre", bi, None))
        srcsA.append(x_pre[j * 128:(j + 1) * 128, :])
        bi += 1
    for j in range(NMAIN):
        gens.append(l0_block(tA, x_main, j * 128, "full", bi, j * 128))
        srcsA.append(x_main[j * 128:(j + 1) * 128, :])
        bi += 1

    def prefetchA(k):
        xb_ = tA.xt[k % 3]
        c.dma(xb_[:, :], srcsA[k], [], [xb_.b], tA.xt_slots[k % 3])

    def prermsA(k):
        rms_stats(c, tA.xt[k % 3], tA.junk, tA.ss1, tA.rr1, tA.xn)

    ratiosA = [1 if (k - 1) < NPRE else 2 for k in range(len(gens))]
    drive(gens, prefetchA, prermsA, rms_round=2, ratios=ratiosA, extra=late_weights(), extra_deadline=max(NPRE - 1, 0))
    stp_slot = P.slot("st_p")
    for h in range(4):
        finals.append(c.dma(st_p[h], tA.S[h][:, :], [tA.S[h].b], [], stp_slot))

    P.barrier()
    assert not c.hold, c.hold
    c.off = markA
    tAs = allocA("As", True)
    print("phase A(sample) arena words", c.off)
    c.dma(tAs.CM[:, :, :], CM_d.rearrange("p (s t) -> p s t", s=NSEQ), [], [tAs.CM.b], P.slot("c_CM"))
    c.memset("pool", tAs.glT[0:17, :], 1.0, [tAs.glT.b])
    drive([l0_block(tAs, x_s, 0, "sample", 0, None)])

    import os
    if os.environ.get("STOP_AFTER") == "A0":
        P.build(final_waits=finals)
        return nc
    if os.environ.get("STOP_AFTER") == "A":
        P.barrier()
        P.build(final_waits=finals)
        return nc
    P.barrier()
    assert not c.hold, c.hold
    P.new_epoch()
    c.off = persist_mark
    Wb_in = c.sb("Wb_in", [8, 4096], BF16)
    Wb_out = c.sb("Wb_out", [16, 1024], BF16)
    Wkv = c.sb("Wkv", [8, 512], BF16)
    gkv = c.sb("gkv", [8], F32)
    gb = c.sb("gb", [8], F32)
    GK = c.sb("GK", [64], F32)
    GKn = c.sb("GKn", [64], F32)
    GQ = c.sb("GQ", [64], F32)
    GQn = c.sb("GQn", [64], F32)
    ESINK = c.sb("ESINK", [32], F32)
    valid = c.sb("valid", [1], F32)
    markB = c.off

    def allocB(tag, sample, ncs):
        t = NS()
        nb = 1 if sample else 2
        nkv = 1 if sample else 3
        if sample:
            alloc_common(t, tag, nxt=0, nho=1, nsig=1)
        else:
            alloc_common(t, tag)
        t.uT = c.sb("uT" + tag, [8, 128], BF16)
        t.COS = c.sb("COS" + tag, [ncs, 32], F32)
        t.SIN = c.sb("SIN" + tag, [ncs, 32], F32)
        t.CK = c.sb("CK" + tag, [64], F32)
        t.SK = c.sb("SK" + tag, [64], F32)
        t.CQ = c.sb("CQ" + tag, [64], F32)
        t.SQ = c.sb("SQ" + tag, [64], F32)
        t.ksq = c.sb("ksq" + tag, [256], F32)
        t.ssk = c.sb("ssk" + tag, [4], F32)
        t.rk = c.sb("rk" + tag, [4], F32)
        t.kn = c.sb("kn" + tag, [4, 64], F32)
        t.km1 = c.sb("km1" + tag, [4, 64], F32)
        t.km2 = c.sb("km2" + tag, [4, 64], F32)
        t.kr = c.sb("kr" + tag, [4, 64], F32)
        t.vf = c.sb("vf" + tag, [256], F32)
        t.kdup = c.sb("kdup" + tag, [4, 2, 64], BF16)
        t.KT = [c.sb(f"KT{tag}{i}", [4, 128], BF16) for i in range(nkv)]
        t.Vaug = [c.sb(f"Vaug{tag}{i}", [4, 65], BF16) for i in range(nkv)]
        t.qsq = c.sb("qsq" + tag, [512], F32)
        t.ssq = c.sb("ssq" + tag, [32], F32)
        t.rq = c.sb("rq" + tag, [32], F32)
        t.qm1 = [c.sb(f"qm1{tag}{i}", [8, 64], F32) for i in range(nb)]
        t.qm2 = [c.sb(f"qm2{tag}{i}", [8, 64], F32) for i in range(nb)]
        t.qr = c.sb("qr" + tag, [2048], BF16)
        t.QT = [c.sb(f"QT{tag}{i}", [16, 128], BF16) for i in range(nb)]
        t.sgate = [c.sb(f"sgate{tag}{i}", [2048], BF16) for i in range(nb)]
        t.PT2 = [c.sb(f"PT2{tag}{h}", [nb, 4, 128], BF16) for h in range(2)]
        t.den = c.sb("den" + tag, [4], F32)
        t.rec = c.sb("rec" + tag, [4], F32)
        t.onrm = c.sb("onrm" + tag, [4, 64], F32)
        if sample:
            t.KTs = c.sb("KTs", [NSEQ, 4, 128], BF16)
            t.Vs = c.sb("Vs", [NSEQ, 4, 65], BF16)
            t.PTz = c.sb("PTz", [NSEQ * 516 + 64], BF16)
            t.kdups = [c.sb(f"kdups{i}", [4, 2, 64], BF16) for i in range(4)]
            t.kd_slots = [[P.slot(f"kd{i}a"), P.slot(f"kd{i}b")] for i in range(4)]
            t.vs_slots = [P.slot(f"vs_in{i}") for i in range(4)]
            t.vs_ser = [Buf(f"vs_ser{i}") for i in range(4)]
            t.VsB = [Buf(f"VsB{i}") for i in range(NSEQ)]
        return t

    tB = allocB("B", False, NMAIN)
    print("phase B arena words", c.off)

    for t, d in ((gkv, g_kv), (gb, g_b), (GK, gk_d), (GQ, gq_d), (ESINK, sinks_d), (valid, valid_d)):
        c.dma(t[:, :], d, [], [t.b], P.slot("c_" + t.b.name))
    c.dma(tB.COS[:, :, :], cos_d, [], [tB.COS.b], P.slot("c_cos"))
    c.dma(tB.SIN[:, :, :], sin_d, [], [tB.SIN.b], P.slot("c_sin"))
    c.ts("pool", GKn[:, :], GK[:, :], -1.0, ALU.mult, [GK.b], [GKn.b])
    c.ts("pool", GQn[:, :], GQ[:, :], -1.0, ALU.mult, [GQ.b], [GQn.b])
    c.act(ESINK[:, :], ESINK[:, :], AF.Exp, [ESINK.b], [ESINK.b])
    for i in range(3):
        c.memset("pool", tB.Vaug[i][:, :, :], 1.0, [tB.Vaug[i].b])
    if os.environ.get("STOP_AFTER") == "BS":
        P.barrier()
        P.build(final_waits=finals)
        return nc
    cnt = [0]
    stgB = tB.ho + tB.xt
    stgB_slots = [P.slot(f"stgB{i}") for i in range(len(stgB))]
    load_weight(c, wkv, Wkv, 8, 512, gkv, 8, stgB, stgB_slots, cnt)
    load_weight(c, b_w_in, Wb_in, 8, 4096, gb, 8, stgB, stgB_slots, cnt)

    def late_weights_B():
        cnt2 = [0]
        yield from load_weight_gen(c, b_w_out, Wb_out, 16, 0, 1024, None, 1, tB.ho, stgB_slots[0:2], cnt2)

    if os.environ.get("STOP_AFTER") == "BW":
        P.barrier()
        P.build(final_waits=finals)
        return nc

    def rope_tables(t, Ct, St, G, Gn, j):
        cj = t.COS[:, j, :]
        sj = t.SIN[:, j, :]
        c.tt("pool", Ct[:, 0:32], G[:, 0:32], cj, ALU.mult, [G.b, t.COS.b], [Ct.b])
        c.tt("pool", Ct[:, 32:64], G[:, 32:64], cj, ALU.mult, [G.b, t.COS.b], [Ct.b])
        c.tt("pool", St[:, 0:32], Gn[:, 32:64], sj, ALU.mult, [Gn.b, t.SIN.b], [St.b])
        c.tt("pool", St[:, 32:64], G[:, 0:32], sj, ALU.mult, [G.b, t.SIN.b], [St.b])

    ckvs_slot = P.slot("ckvs")

    def l1_block(t, j, mode):
        halo = mode == "halo"
        sample = mode == "sample"
        last = (mode == "full" and j == NMAIN - 1)
        uT = t.uT
        nkv = len(t.KT)
        KTc, Vc = t.KT[j % nkv], t.Vaug[j % nkv]
        KTp, Vp = t.KT[(j - 1) % nkv], t.Vaug[(j - 1) % nkv]
        QT = t.QT[j % len(t.QT)]
        sgate = t.sgate[j % len(t.sgate)]
        PT2 = t.PT2
        if sample:
            xb = hs
            rms_stats(c, xb, t.junk, t.ss1, t.rr1, t.xn)
        else:
            xb = t.xt[j % 3]
        rms_tr(c, t.xn, uT, ident, 0)
        tj = 0 if sample else j
        yield
        OLD_KT = os.environ.get("OLD_KT", "0") == "1"
        OLD_QT = os.environ.get("OLD_QT", "0") == "1"

        def kt_transposes():
            pt = c.bkbf(1).rearrange("p (k t) -> p k t", k=8)
            for gg in range(4):
                c.tr(pt[:, gg, :], t.kdup[:, gg, :, :].rearrange("p a d -> p (a d)"), ident[:, :], [t.kdup.b, ident.b], [c.pb[1]])
            c.cp("act", KTc[:, :, :], pt[:, 0:4, :], [c.pb[1]], [KTc.b])

        def qt_transposes():
            ptq = c.psum[:, 0:2, :].bitcast(BF16).rearrange("p b (k t) -> p (b k) t", k=8)
            for k in range(16):
                c.tr(ptq[:, k, :], qr[:, k * 128:(k + 1) * 128], ident[:, :], [qr.b, ident.b], [c.pb[0], c.pb[1]])
            c.cp("dve", QT[:, 0:8, :], ptq[:, 0:8, :], [c.pb[0]], [QT.b])
            c.cp("act", QT[:, 8:16, :], ptq[:, 8:16, :], [c.pb[1]], [QT.b])


        def kv_section():
            rope_tables(t, t.CK, t.SK, GK, GKn, tj)
            bkv = c.bank()
            proj(c, uT, Wkv, 0, 512, bkv)
            kps = c.bk(bkv)[:, 0:256].rearrange("p (g d) -> p g d", g=4)
            kn, km1, km2, kr, kdup = t.kn, t.km1, t.km2, t.kr, t.kdup
            c.act(t.ksq[:, :], c.bk(bkv)[:, 0:256], AF.Square, [c.pb[bkv]], [t.ksq.b])
            vps = c.bk(bkv)[:, 256:512].rearrange("p (g d) -> p g d", g=4)
            if j == nkv and not sample:
                c.memset("pool", Vc[:, :, 64:65], 1.0, [Vc.b])
            c.cp("act", Vc[:, :, 0:64], vps, [c.pb[bkv]], [Vc.b])
            if last or sample:
                c.cp("act", t.vf[:, :], c.bk(bkv)[:, 256:512], [c.pb[bkv]], [t.vf.b])
            c.P.emit("dve", lambda e: e.tensor_reduce(out=t.ssk[:, :], in_=t.ksq[:, :].rearrange("p (g d) -> p g d", g=4),
                                                      axis=AX.X, op=ALU.add), [t.ksq.b], [t.ssk.b])
            c.act(t.rk[:, :], t.ssk[:, :], AF.Ln, [t.ssk.b], [t.rk.b], scale=1.0 / 64, bias=1e-6)
            c.act(t.rk[:, :], t.rk[:, :], AF.Exp, [t.rk.b], [t.rk.b], scale=-0.5)
            c.tt("dve", kn[:, :, :], kps, bc(t.rk[:, :].unsqueeze(2), [128, 4, 64]), ALU.mult, [c.pb[bkv], t.rk.b], [kn.b])
            c.free(bkv)
            yield
            c.tt("pool", km1[:, :, :], kn[:, :, :], bc(t.CK[:, None, :], [128, 4, 64]), ALU.mult, [kn.b, t.CK.b], [km1.b])
            c.tt("pool", km2[:, :, 0:32], kn[:, :, 32:64], bc(t.SK[:, None, 0:32], [128, 4, 32]), ALU.mult, [kn.b, t.SK.b], [km2.b])
            c.tt("pool", km2[:, :, 32:64], kn[:, :, 0:32], bc(t.SK[:, None, 32:64], [128, 4, 32]), ALU.mult, [kn.b, t.SK.b], [km2.b])
            c.tt("pool", kr[:, :, :], km1[:, :, :], km2[:, :, :], ALU.add, [km1.b, km2.b], [kr.b])
            c.cp("pool", kdup[:, :, 0, :], kr[:, :, :], [kr.b], [kdup.b])
            c.cp("pool", kdup[:, :, 1, :], kr[:, :, :], [kr.b], [kdup.b])
            if halo:
                kt_transposes()
            if halo:
                c.ts("dve", Vc[:, :, :], Vc[:, :, :], valid[:, 0:1], ALU.mult, [Vc.b, valid.b], [Vc.b])
            if last:
                finals.append(c.dma(ck_p, kr[:, :, :].rearrange("p g d -> p (g d)"), [kr.b], [], P.slot("ck_p")))
                finals.append(c.dma(cv_p, t.vf[:, :], [t.vf.b], [], P.slot("cv_p")))
            if sample:
                krf = kr[:, :, :].rearrange("p g d -> p (g d)")
                for s in range(NSEQ):
                    finals.append(c.dma(ck_s[s, 128 - TS:128, :], krf[s * TS:(s + 1) * TS, :], [kr.b], [], ckvs_slot))
                    finals.append(c.dma(cv_s[s, 128 - TS:128, :], t.vf[s * TS:(s + 1) * TS, :], [t.vf.b], [], ckvs_slot))
            yield

        ssq, rq, qr = t.ssq, t.rq, t.qr
        CUT = int(os.environ.get("L1CUT", "0"))
        if CUT == 1 and not halo:
            return
        if halo:
            yield from kv_section()
        if not halo:
            rope_tables(t, t.CQ, t.SQ, GQ, GQn, tj)
            for g in range(4):
                bq = c.bank()
                proj(c, uT, Wb_in, g * 512, 512, bq)
                qps = c.bk(bq).rearrange("p (h d) -> p h d", h=8)
                c.act(t.qsq[:, :], c.bk(bq), AF.Square, [c.pb[bq]], [t.qsq.b])
                c.P.emit("dve", lambda e, g=g: e.tensor_reduce(out=ssq[:, g * 8:(g + 1) * 8],
                                                               in_=t.qsq[:, :].rearrange("p (h d) -> p h d", h=8),
                                                               axis=AX.X, op=ALU.add), [t.qsq.b], [ssq.b])
                c.act(rq[:, g * 8:(g + 1) * 8], ssq[:, g * 8:(g + 1) * 8], AF.Ln, [ssq.b], [rq.b], scale=1.0 / 64, bias=1e-6)
                c.act(rq[:, g * 8:(g + 1) * 8], rq[:, g * 8:(g + 1) * 8], AF.Exp, [rq.b], [rq.b], scale=-0.5)
                m1, m2 = t.qm1[g % len(t.qm1)], t.qm2[g % len(t.qm2)]
                c.tt("dve", m1[:, :, :], qps, bc(t.CQ[:, None, :], [128, 8, 64]), ALU.mult, [c.pb[bq], t.CQ.b], [m1.b])
                c.tt("dve", m2[:, :, 0:32], qps[:, :, 32:64], bc(t.SQ[:, None, 0:32], [128, 8, 32]), ALU.mult, [c.pb[bq], t.SQ.b], [m2.b])
                c.tt("dve", m2[:, :, 32:64], qps[:, :, 0:32], bc(t.SQ[:, None, 32:64], [128, 8, 32]), ALU.mult, [c.pb[bq], t.SQ.b], [m2.b])
                c.free(bq)
                c.tt("dve", m1[:, :, :], m1[:, :, :], m2[:, :, :], ALU.add, [m1.b, m2.b], [m1.b])
                c.tt("pool", qr[:, g * 512:(g + 1) * 512].rearrange("p (h d) -> p h d", h=8), m1[:, :, :],
                     bc(rq[:, g * 8:(g + 1) * 8].unsqueeze(2), [128, 8, 64]), ALU.mult, [m1.b, rq.b], [qr.b])
                yield
            yield from kv_section()
            for g in range(4):
                bg = c.bank()
                proj(c, uT, Wb_in, 2048 + g * 512, 512, bg)
                sg = t.sig[g % len(t.sig)]
                sigmoid_act(c, sg, bg)
                c.tt("dve", sgate[:, g * 512:(g + 1) * 512], c.bk(bg), sg[:, :], ALU.mult, [c.pb[bg], sg.b], [sgate.b])
                c.free(bg)
                yield
            kt_transposes()
            qt_transposes()
        if CUT == 4 and not halo:
            return
        yield "BACK"
        if halo:
            return
        og = t.og
        ogv = og[:, :].rearrange("p (g j f d) -> p g j f d", g=4, j=4, f=2)
        sgv = sgate[:, :].rearrange("p (g j f d) -> p g j f d", g=4, j=4, f=2)
        esv = ESINK[:, :].rearrange("p (g j f) -> p g j f", g=4, j=4)
        den, rec, onrm = t.den, t.rec, t.onrm
        if sample:
            PTz = t.PTz
            pz_diag = PTz[:, 0:NSEQ * 516].rearrange("p (s r) -> p s r", r=516)[:, :, 0:512].rearrange(
                "p s (j q) -> p s j q", q=128)[:, :, :, 0:TS]
            pz_full = PTz[:, 0:NSEQ * 512].rearrange("p (s j q) -> p s j q", s=NSEQ, j=4)
        def scores(g, half):
            lo, hi = half * 64, (half + 1) * 64
            if not sample:
                tiles = ((KTp, Vp, L), (KTc, Vc, U))
            else:
                tiles = ((KTc, Vc, U4),)
                bs = c.bank()
                for s in range(NSEQ):
                    c.mm(c.bk(bs)[:, s * 16:(s + 1) * 16], t.KTs[lo:hi, s, g, :], QT[lo:hi, 4 * g:4 * g + 4, s * TS:(s + 1) * TS],
                         True, True, [t.KTs.b, QT.b], [c.pb[bs]])
                c.act(pz_diag, c.bk(bs)[:, 0:NSEQ * 16].rearrange("p (s j t) -> p s j t", s=NSEQ, j=4), AF.Exp,
                      [c.pb[bs]], [PTz.b], scale=0.125)
                c.free(bs)
                c.tt("pool", pz_diag, pz_diag, bc(CMK[:, None, None, :], [128, NSEQ, 4, TS]), ALU.mult, [PTz.b, CMK.b], [PTz.b])
            p2 = PT2[half]
            for kti, (ktile, vt, mask) in enumerate(tiles):
                bs = c.bank()
                c.mm(c.bk(bs), ktile[lo:hi, g, :], QT[lo:hi, 4 * g:4 * g + 4, :], True, True, [ktile.b, QT.b], [c.pb[bs]])
                c.act(p2[:, kti, :, :], c.bk(bs).rearrange("p (j t) -> p j t", j=4), AF.Exp, [c.pb[bs]], [p2.b], scale=0.125)
                c.free(bs)
            if not sample:
                c.tt("dve", p2[:, :, :, :], p2[:, :, :, :], bc(LU[:, :, None, :], [128, 2, 4, 128]), ALU.mult, [p2.b, LU.b], [p2.b])
            else:
                c.tt("dve", p2[:, 0, :, :], p2[:, 0, :, :], bc(U4[:, None, :], [128, 4, 128]), ALU.mult, [p2.b, U4.b], [p2.b])

        def pv(g, half):
            bo = c.bank()
            ob = c.bk(bo)[:, 0:260].rearrange("p (j e) -> p j e", j=4)
            for jj in range(4):
                if not sample:
                    c.mm(ob[:, jj, :], PT2[half][:, 0, jj, :], Vp[:, g, :], True, False, [PT2[half].b, Vp.b], [c.pb[bo]])
                    c.mm(ob[:, jj, :], PT2[half][:, 1, jj, :], Vc[:, g, :], False, True, [PT2[half].b, Vc.b], [c.pb[bo]])
                else:
                    for s in range(NSEQ):
                        c.mm(ob[:, jj, :], pz_full[:, s, jj, :], t.Vs[:, s, g, :], s == 0, False,
                             [PTz.b, t.VsB[s]], [c.pb[bo]])
                    c.mm(ob[:, jj, :], PT2[half][:, 0, jj, :], Vc[:, g, :], False, True, [PT2[half].b, Vc.b], [c.pb[bo]])
            c.tt("dve", den[:, :], ob[:, :, 64], esv[:, g, :, half], ALU.add, [c.pb[bo], ESINK.b], [den.b])
            c.P.emit("dve", lambda e: e.reciprocal(out=rec[:, :], in_=den[:, :]), [den.b], [rec.b])
            c.tt("dve", onrm[:, :, :], ob[:, :, 0:64], bc(rec[:, :].unsqueeze(2), [128, 4, 64]), ALU.mult, [c.pb[bo], rec.b], [onrm.b])
            c.free(bo)
            c.tt("pool", ogv[:, g, :, half, :], onrm[:, :, :], sgv[:, g, :, half, :], ALU.mult, [onrm.b, sgate.b], [og.b])

        its = [(g, half) for g in range(4) for half in range(2)]
        if sample:
            for (g, half) in its:
                scores(g, half)
                yield
                pv(g, half)
                yield
        else:
            scores(*its[0])
            yield
            for i, (g, half) in enumerate(its):
                if i + 1 < len(its):
                    scores(*its[i + 1])
                    yield
                pv(g, half)
                yield
        if CUT == 5:
            return
        hb = t.ho[j % len(t.ho)]
        yield
        yield from out_proj(c, og, t.ogT, Wb_out, xb, hb, ident)
        if sample:
            finals.append(c.dma(y_s, hb[0:NSEQ * TS, :], [hb.b], [], t.ho_slots[0]))
        else:
            finals.append(c.dma(y_main[(j - 1) * 128:j * 128, :], hb[:, :], [hb.b], [], t.ho_slots[j % 2]))

    gens = [l1_block(tB, 0, "halo")]
    for j in range(1, NMAIN):
        gens.append(l1_block(tB, j, "full"))
    if os.environ.get("STOP_AFTER") == "BH":
        gens = gens[:1]

    def prefetchB(k):
        xb_ = tB.xt[k % 3]
        c.dma(xb_[:, :], h1s[k * 128:(k + 1) * 128, :], [h1blk[k]], [xb_.b], tB.xt_slots[k % 3])

    def prermsB(k):
        rms_stats(c, tB.xt[k % 3], tB.junk, tB.ss1, tB.rr1, tB.xn)

    drive(gens, prefetchB, prermsB, rms_round=8, extra=late_weights_B(), extra_deadline=2, extra_from=0)
    if os.environ.get("STOP_AFTER") == "BH":
        P.barrier()
        P.build(final_waits=[f for f in finals])
        return nc

    if os.environ.get("STOP_AFTER") == "B1":
        P.barrier()
        P.build(final_waits=finals)
        return nc
    P.barrier()
    assert not c.hold, c.hold
    c.off = markB
    tBs = allocB("Bs", True, 1)
    print("phase B(sample) arena words", c.off)
    c.dma(tBs.COS[:, :, :], coss_d, [], [tBs.COS.b], P.slot("c_coss"))
    c.dma(tBs.SIN[:, :, :], sins_d, [], [tBs.SIN.b], P.slot("c_sins"))
    c.memset("pool", tBs.Vaug[0][:, :, :], 1.0, [tBs.Vaug[0].b])
    c.memset("dve", tBs.Vs[:, :, :, :], 1.0, tBs.VsB)
    c.memset("dve", tBs.PTz[:, :], 0.0, [tBs.PTz.b])
    ck4 = cache_k_s.rearrange("s p (g d) -> s p g d", g=4)
    cv4 = cache_v_s.rearrange("s p (g d) -> s p g d", g=4)
    for s in range(NSEQ):
        kd = tBs.kdups[s % 4]
        c.dma(kd[:, :, 0, :], ck4[s], [], [kd.b], tBs.kd_slots[s % 4][0], eng="pool")
        c.dma(kd[:, :, 1, :], ck4[s], [], [kd.b], tBs.kd_slots[s % 4][1], eng="pool")
        c.dma(tBs.Vs[:, s, :, 0:64], cv4[s], [], [tBs.VsB[s], tBs.vs_ser[s % 4]], tBs.vs_slots[s % 4], eng="pool")
        tb = s % 2
        pt = c.bkbf(tb).rearrange("p (k t) -> p k t", k=8)
        for g in range(4):
            c.tr(pt[:, g, :], kd[:, g, :, :].rearrange("p a d -> p (a d)"), ident[:, :], [kd.b, ident.b], [c.pb[tb]])
        c.cp("act", tBs.KTs[:, s, :, :], pt[:, 0:4, :], [c.pb[tb]], [tBs.KTs.b])
    drive([l1_block(tBs, 0, "sample")])

    P.build(final_waits=finals)
    return nc


_CACHE = {}


def _consts():
    bf = ml_dtypes.bfloat16
    i = np.arange(128)
    Um = (i[:, None] <= i[None, :]).astype(np.float32)
    same = (i[:, None] // TS == i[None, :] // TS)
    U4 = (Um * same).astype(np.float32)
    seqind = (i[:, None] // TS == np.arange(NSEQ)[None, :]).astype(np.float32)
    CM = np.broadcast_to((np.arange(NSEQ)[:, None] == (i[None, :] // TS)).astype(np.float32)[None], (128, NSEQ, 128))
    CMK = (i[:, None] >= np.arange(TS)[None, :]).astype(np.float32)
    return {
        "ident": np.eye(128, dtype=np.float32).astype(bf),
        "U": Um.astype(bf),
        "L": Um.T.copy().astype(bf),
        "LU": np.concatenate([Um.T, Um], axis=1).astype(bf),
        "U32": Um.astype(np.float32),
        "U432": U4.astype(np.float32),
        "ones32": np.ones((128, 8), np.float32),
        "U4": U4.astype(bf),
        "ones": np.ones((128, 128), np.float32).astype(bf),
        "seqind": seqind.astype(bf),
        "seqindf": seqind.astype(np.float32),
        "CM": np.ascontiguousarray(CM.reshape(128, NSEQ * 128)).astype(bf),
        "CMK": CMK.astype(bf),
    }


def _pk(v, k):
    return np.ascontiguousarray(np.asarray(v, np.float32).reshape(k, 128).T)


def _rope_tab(pos):
    half = 32
    inv = (10000.0 ** (-np.arange(half, dtype=np.float64) / half)).astype(np.float32)
    ang = (pos.astype(np.float32)[:, None] * inv[None, :]).astype(np.float32).astype(np.float64)
    return np.cos(ang).astype(np.float32), np.sin(ang).astype(np.float32)


def run(inputs, NB):
    f = lambda a: np.asarray(a, np.float32)
    x_prompt = f(inputs["x_prompt"])
    NMAIN = NB + 1
    NPRE = NB - 1
    if NB not in _CACHE:
        _CACHE[NB] = build_program(NB)
    nc = _CACHE[NB]
    shared = {
        "a_w_in": f(inputs["a_w_in"])[0], "a_w_out": f(inputs["a_w_out"])[0],
        "wkv": np.ascontiguousarray(np.concatenate([f(inputs["w_k"]), f(inputs["w_v"])], axis=1)),
        "b_w_in": f(inputs["b_w_in"])[0], "b_w_out": f(inputs["b_w_out"])[0],
        "wg2a": np.ascontiguousarray(np.concatenate([f(inputs["a_w_gate2"])[0], f(inputs["a_b_gate"])[0][None, :]], axis=0)),
        "g_a": _pk(inputs["a_norm"][0], 8), "g_ao": _pk(inputs["a_out_norm"][0], 4),
        "g_kv": _pk(inputs["kv_norm"], 8), "g_b": _pk(inputs["b_norm"][0], 8),
        "gk": np.ascontiguousarray(np.broadcast_to(f(inputs["k_norm"])[None, :], (128, 64))),
        "gq": np.ascontiguousarray(np.broadcast_to(f(inputs["b_q_norm"])[0][None, :], (128, 64))),
        "sinks": np.ascontiguousarray(np.broadcast_to(f(inputs["b_sinks"])[0][None, :], (128, 32))),
        **_consts(),
    }
    PAST = 16384
    pos_s = PAST + (np.arange(128) % TS)
    cs, sn = _rope_tab(pos_s)
    shared["cos_s"] = np.ascontiguousarray(cs.reshape(128, 1, 32))
    shared["sin_s"] = np.ascontiguousarray(sn.reshape(128, 1, 32))
    x_sample = f(inputs["x_sample"])
    state = f(inputs["state_gla"])[0]
    ckc = f(inputs["cache_swa_k"])
    cvc = f(inputs["cache_swa_v"])
    in_maps = []
    for core in range(NCORES):
        b, hf = core // 2, core % 2
        start_blk = hf * NB
        xs = x_prompt[b]
        zeros = np.zeros((128, D), np.float32)
        if hf == 0:
            x_pre = np.zeros((max(NPRE, 1) * 128, D), np.float32)
            x_main = np.concatenate([zeros, xs[0:NB * 128]], axis=0)
        else:
            x_pre = xs[0:NPRE * 128] if NPRE > 0 else np.zeros((128, D), np.float32)
            x_main = xs[(NB - 1) * 128:2 * NB * 128]
        pos = (start_blk - 1) * 128 + np.arange(NMAIN * 128)
        cos, sin = _rope_tab(pos)
        cos = cos.reshape(NMAIN, 128, 32).transpose(1, 0, 2)
        sin = sin.reshape(NMAIN, 128, 32).transpose(1, 0, 2)
        sq = slice(core * NSEQ, (core + 1) * NSEQ)
        xsp = np.zeros((128, D), np.float32)
        xsp[0:NSEQ * TS] = x_sample[sq].reshape(NSEQ * TS, D)
        m = dict(shared)
        m.update({
            "x_pre": np.ascontiguousarray(x_pre), "x_main": np.ascontiguousarray(x_main),
            "cos": np.ascontiguousarray(cos), "sin": np.ascontiguousarray(sin),
            "valid": np.full((128, 1), float(hf), np.float32),
            "x_s": xsp,
            "state_s": np.ascontiguousarray(state[sq]),
            "cache_k_s": np.ascontiguousarray(ckc[sq].reshape(NSEQ, 128, 256)),
            "cache_v_s": np.ascontiguousarray(cvc[sq].reshape(NSEQ, 128, 256)),
        })
        in_maps.append(m)
    res = run_bass_kernel_spmd(nc, in_maps, core_ids=list(range(NCORES)))
    return res.results


def kernel(**inputs):
    NB = 32
    r = run(inputs, NB)
    B, SEQ = 4, 8192
    y_prompt = np.zeros((B, SEQ, D), np.float32)
    st_p = np.zeros((1, B, 4, 128, 512), np.float32)
    ck_p = np.zeros((B, 128, 4, 64), np.float32)
    cv_p = np.zeros((B, 128, 4, 64), np.float32)
    y_sample = np.zeros((128, TS, D), np.float32)
    st_s = np.zeros((1, 128, 4, 128, 512), np.float32)
    ck_s = np.zeros((128, 128, 4, 64), np.float32)
    cv_s = np.zeros((128, 128, 4, 64), np.float32)
    for core in range(NCORES):
        b, hf = core // 2, core % 2
        y_prompt[b, hf * 4096:(hf + 1) * 4096] = r[core]["y_main"]
        if hf == 1:
            st_p[0, b] = r[core]["st_p"]
            ck_p[b] = r[core]["ck_p"].reshape(128, 4, 64)
            cv_p[b] = r[core]["cv_p"].reshape(128, 4, 64)
        sq = slice(core * NSEQ, (core + 1) * NSEQ)
        y_sample[sq] = r[core]["y_s"].reshape(NSEQ, TS, D)
        st_s[0, sq] = r[core]["st_s"]
        ck_s[sq] = r[core]["ck_s"].reshape(NSEQ, 128, 4, 64)
        cv_s[sq] = r[core]["cv_s"].reshape(NSEQ, 128, 4, 64)
    return (y_prompt, y_sample, st_p, st_s, ck_p, cv_p, ck_s, cv_s)
```
